# Optimizing a Trainium2 kernel written in Bass

```python
import jax, jax.numpy as jnp
from jax import lax
import numpy as np

D_MODEL = 1024
BATCH = 8
SEQ = 2048
DEPTH = 1
DEC_BATCH = 128
DEC_SEQ = 4
PAST_LEN = 16384
PAGE_SIZE = 128

H_A = 8
DK_A = 128
DV_A = 64
D_QA = H_A * DK_A
D_VA = H_A * DV_A
HGRN_CHUNK = 64
H_B = 4
CH_B = 128
D_B = H_B * CH_B
CHUNK = 128
D_FF = 2816
CONV_W = 3
EPS = 1e-6

SIZES = (D_QA, D_QA, D_VA, D_VA, D_B, D_B, D_MODEL, D_MODEL)
D_IN = sum(SIZES)
SPLIT_IDX = tuple(int(s) for s in np.cumsum(SIZES)[:-1])

kernel_name = 'hgrn2_gmlp_convffn_hybrid_step'


def rms_norm(x, g):
    xf = x.astype(jnp.float32)
    y = xf * lax.rsqrt(jnp.mean(xf * xf, axis=-1, keepdims=True) + EPS)
    return (y * g.astype(jnp.float32)).astype(x.dtype)


def layer_norm(x, g, b):
    xf = x.astype(jnp.float32)
    mu = jnp.mean(xf, axis=-1, keepdims=True)
    xc = xf - mu
    y = xc * lax.rsqrt(jnp.mean(xc * xc, axis=-1, keepdims=True) + EPS)
    return (y * g.astype(jnp.float32) + b.astype(jnp.float32)).astype(x.dtype)


def hgrn2_chunk(S, q, k, i, logf):
    C = q.shape[-2]
    b = jnp.cumsum(logf, axis=-2)
    causal = jnp.tril(jnp.ones((C, C), dtype=bool))
    diff = b[..., :, None, :] - b[..., None, :, :]
    decay = jnp.exp(jnp.where(causal[:, :, None], diff, -jnp.inf))
    scores = jnp.einsum('bhtk,bhtsk,bhsk->bhts', q, decay, k)
    o = jnp.einsum('bhts,bhsv->bhtv', scores, i) + jnp.einsum('bhtk,bhkv->bhtv', q * jnp.exp(b), S)
    b_last = b[..., -1:, :]
    S_new = jnp.exp(b_last[..., 0, :])[..., None] * S + jnp.einsum('bhsk,bhsv->bhkv', k * jnp.exp(b_last - b), i)
    return S_new, o


def hgrn2_mixer(q, f_logit, i, S0, lb):
    B, L, _ = q.shape
    qf = jax.nn.silu(q.astype(jnp.float32))
    f = lb + (1.0 - lb) * jax.nn.sigmoid(f_logit.astype(jnp.float32))
    logf = jnp.log(f)
    k = 1.0 - f
    heads = lambda t, d: t.reshape(B, L, H_A, d).transpose(0, 2, 1, 3)
    qh, kh, lfh = heads(qf, DK_A), heads(k, DK_A), heads(logf, DK_A)
    ih = heads(i.astype(jnp.float32), DV_A)
    C = HGRN_CHUNK if L % HGRN_CHUNK == 0 else L
    nc = L // C
    to_chunks = lambda t: jnp.moveaxis(t.reshape(B, H_A, nc, C, t.shape[-1]), 2, 0)
    S_fin, o = lax.scan(lambda S, xs: hgrn2_chunk(S, *xs), S0.astype(jnp.float32),
                        (to_chunks(qh), to_chunks(kh), to_chunks(ih), to_chunks(lfh)))
    o = jnp.moveaxis(o, 0, 2).reshape(B, H_A, L, DV_A).transpose(0, 2, 1, 3)
    return o, S_fin


def chunk_spatial_gate(u, vn, w_s, b_s):
    B, L, _ = u.shape
    c = min(L, CHUNK)
    n = L // c
    vg = vn.reshape(B, n, c, H_B, CH_B)
    w = jnp.tril(w_s[:, :c, :c])
    s = jnp.einsum('gts,bnsgc->bntgc', w, vg) + b_s[:, :c].T[None, None, :, :, None]
    return u * s.reshape(B, L, D_B)


def decoder_layer(x, S0, conv_prev, lb, mix_pre_g, w_in, hgrn_norm_g, gmlp_ln_g, gmlp_ln_b, w_s, b_s,
                  w_pa, w_pb, w_o, mix_post_g, ffn_pre_g, w_up, conv_w, conv_b, w_down, ffn_post_g):
    B, L, _ = x.shape
    xn = rms_norm(x, mix_pre_g)
    z = xn @ w_in
    q, f_logit, i, og, u, v, ga, gb = jnp.split(z, SPLIT_IDX, axis=-1)
    oa, S_new = hgrn2_mixer(q, f_logit, i, S0, lb)
    oa = rms_norm(oa.astype(x.dtype), hgrn_norm_g) * jax.nn.silu(og.reshape(B, L, H_A, DV_A))
    oa = oa.reshape(B, L, D_VA)
    vn = layer_norm(jax.nn.gelu(v), gmlp_ln_g, gmlp_ln_b)
    ob = chunk_spatial_gate(jax.nn.gelu(u), vn, w_s, b_s)
    h = jax.nn.sigmoid(ga) * (oa @ w_pa) + jax.nn.sigmoid(gb) * (ob @ w_pb)
    x = x + rms_norm(h @ w_o, mix_post_g)
    xn = rms_norm(x, ffn_pre_g)
    up = xn @ w_up
    hp = jnp.concatenate([conv_prev.astype(up.dtype), up], axis=1)
    conv = conv_b + sum(conv_w[j] * hp[:, j:j + L] for j in range(CONV_W))
    gate, val = jnp.split(conv, 2, axis=-1)
    x = x + rms_norm((jax.nn.gelu(gate) * val) @ w_down, ffn_post_g)
    return x, S_new, hp[:, -(CONV_W - 1):], vn


def setup_inputs(seed: int = 0) -> dict:
    key = jax.random.key(seed)
    ks = jax.random.split(key, 24)
    nrm = lambda k, shape, s: jax.random.normal(k, shape, jnp.float32) * s
    gain = lambda k, shape: 1.0 + 0.05 * jax.random.normal(k, shape, jnp.float32)
    return {
        'x_prompt': nrm(ks[0], (BATCH, SEQ, D_MODEL), 1.0),
        'x_sample': nrm(ks[1], (DEC_BATCH, DEC_SEQ, D_MODEL), 1.0),
        'state_hgrn': nrm(ks[2], (DEPTH, DEC_BATCH, H_A, DK_A, DV_A), 0.5),
        'cache_ffn_conv': nrm(ks[3], (DEPTH, DEC_BATCH, CONV_W - 1, 2 * D_FF), 1.0),
        'lb_param': nrm(ks[4], (DEPTH + 1, D_QA), 0.1),
        'mix_pre_g': gain(ks[5], (DEPTH, D_MODEL)),
        'w_in': nrm(ks[6], (DEPTH, D_MODEL, D_IN), D_MODEL ** -0.5),
        'hgrn_norm_g': gain(ks[7], (DEPTH, DV_A)),
        'gmlp_ln_g': gain(ks[8], (DEPTH, D_B)),
        'gmlp_ln_b': nrm(ks[9], (DEPTH, D_B), 0.02),
        'w_s': nrm(ks[10], (DEPTH, H_B, CHUNK, CHUNK), CHUNK ** -0.5),
        'b_s': 1.0 + nrm(ks[11], (DEPTH, H_B, CHUNK), 0.1),
        'w_pa': nrm(ks[12], (DEPTH, D_VA, D_MODEL), D_VA ** -0.5),
        'w_pb': nrm(ks[13], (DEPTH, D_B, D_MODEL), D_B ** -0.5),
        'w_o': nrm(ks[14], (DEPTH, D_MODEL, D_MODEL), D_MODEL ** -0.5),
        'mix_post_g': gain(ks[15], (DEPTH, D_MODEL)),
        'ffn_pre_g': gain(ks[16], (DEPTH, D_MODEL)),
        'w_up': nrm(ks[17], (DEPTH, D_MODEL, 2 * D_FF), D_MODEL ** -0.5),
        'conv_w': nrm(ks[18], (DEPTH, CONV_W, 2 * D_FF), CONV_W ** -0.5),
        'conv_b': nrm(ks[19], (DEPTH, 2 * D_FF), 0.02),
        'w_down': nrm(ks[20], (DEPTH, D_FF, D_MODEL), D_FF ** -0.5),
        'ffn_post_g': gain(ks[21], (DEPTH, D_MODEL)),
    }


def reference(x_prompt, x_sample, state_hgrn, cache_ffn_conv, lb_param, mix_pre_g, w_in, hgrn_norm_g,
              gmlp_ln_g, gmlp_ln_b, w_s, b_s, w_pa, w_pb, w_o, mix_post_g, ffn_pre_g, w_up, conv_w,
              conv_b, w_down, ffn_post_g):
    lb_all = jnp.cumsum(jax.nn.softmax(lb_param.astype(jnp.float32), axis=0), axis=0)
    yp, ys = x_prompt, x_sample
    sp_l, ss_l, cp_l, cs_l, vs_l = [], [], [], [], []
    S_zero = jnp.zeros((x_prompt.shape[0], H_A, DK_A, DV_A), jnp.float32)
    conv_zero = jnp.zeros((x_prompt.shape[0], CONV_W - 1, 2 * D_FF), x_prompt.dtype)
    for l in range(DEPTH):
        w = (mix_pre_g[l], w_in[l], hgrn_norm_g[l], gmlp_ln_g[l], gmlp_ln_b[l], w_s[l], b_s[l],
             w_pa[l], w_pb[l], w_o[l], mix_post_g[l], ffn_pre_g[l], w_up[l], conv_w[l], conv_b[l],
             w_down[l], ffn_post_g[l])
        yp, sp, cp, _ = decoder_layer(yp, S_zero, conv_zero, lb_all[l], *w)
        ys, ss, cs, vs = decoder_layer(ys, state_hgrn[l], cache_ffn_conv[l], lb_all[l], *w)
        sp_l.append(sp); ss_l.append(ss); cp_l.append(cp); cs_l.append(cs); vs_l.append(vs)
    return (yp, ys, jnp.stack(sp_l), jnp.stack(ss_l), jnp.stack(cp_l), jnp.stack(cs_l), jnp.stack(vs_l))
```

```python
import numpy as np
from contextlib import ExitStack

import concourse.bass as bass
import concourse.mybir as mybir
from concourse.bass_utils import run_bass_kernel_spmd

F32 = mybir.dt.float32
BF16 = mybir.dt.bfloat16
AF = mybir.ActivationFunctionType
ALU = mybir.AluOpType
AX = mybir.AxisListType

NCORES = 8
D = 1024
SEQ = 2048
NPT = SEQ // 128
SB = 16
TS = 64
DIN = 6144
DFF = 2816
DUP = 2 * DFF
NJ = DFF // 128
EPS = 1e-6
WELEMS = 67584


class Op:
    __slots__ = ("eng", "fn", "edges", "deps", "chan", "sig", "need", "busy", "lat", "idx", "succ", "prio",
                 "start", "nun", "tag", "aset", "xedges", "urg")


class Prog:
    ENG = ("pe", "act", "dve", "pool", "sp")
    QUANT = 0.01

    def __init__(self, nc):
        self.nc = nc
        self.regions = [[]]
        self.lastw = {}
        self.readers = {}
        self.chan_last = {}
        self.chan_cnt = {}

    def op(self, eng, fn, r=(), w=(), chan=None, busy=0.3, lat=None, after=(), aset=0):
        o = Op()
        o.eng, o.fn, o.chan, o.need, o.sig = eng, fn, chan, False, None
        o.busy = busy
        o.lat = busy if lat is None else lat
        o.tag = getattr(self, "tag", "")
        o.aset = aset
        o.urg = getattr(self, "urg", 0.0)
        edges, seen = [], set()

        def add(d, is_war):
            if d is None or d is o or id(d) in seen:
                return
            seen.add(id(d))
            edges.append((d, is_war))
        for b in r:
            add(self.lastw.get(b), False)
        for b in w:
            add(self.lastw.get(b), False)
        if chan is not None:
            add(self.chan_last.get(chan), False)
            self.chan_last[chan] = o
        for b in w:
            for d in self.readers.get(b, ()):
                add(d, True)
        for b in after:
            add(self.lastw.get(b), False)
            for d in self.readers.get(b, ()):
                add(d, True)
        o.edges = edges
        for b in r:
            self.readers.setdefault(b, []).append(o)
        for b in w:
            self.lastw[b] = o
            self.readers[b] = []
        self.regions[-1].append(o)
        return o

    def barrier(self, keep=()):
        self.regions.append([])
        kept = {k: v for k, v in self.lastw.items() if v.chan is not None and (k in keep or (isinstance(k, tuple) and k[0] in keep))}
        self.nobar = getattr(self, "nobar", set()) | set(v.chan for v in kept.values())
        self.lastw, self.readers = dict(kept), {}

    def finish(self):
        pass

    @staticmethod
    def _schedule(ops):
        n = len(ops)
        for i, o in enumerate(ops):
            o.idx, o.succ, o.start = i, [], None
        inreg = set(id(o) for o in ops)
        for o in ops:
            o.xedges = [d for (d, wr) in o.edges if id(d) not in inreg]
            o.edges = [(d, wr) for (d, wr) in o.edges if id(d) in inreg]
            o.nun = len(o.edges)
            for d, _ in o.edges:
                d.succ.append(o)
        for o in reversed(ops):
            o.prio = o.lat + max([s_.prio for s_ in o.succ], default=0.0)
        for o in ops:
            if o.chan is not None and o.aset != -1:
                o.prio = 1e9 - o.idx
        free = {e: 0.0 for e in Prog.ENG}
        rel = {e: [] for e in Prog.ENG}
        avail = {}
        for o in ops:
            if o.nun == 0:
                rel[o.eng].append(o)
                avail[id(o)] = 0.0
        order = {e: [] for e in Prog.ENG}
        done = 0
        cur_set = 0
        TL = 1.3
        QUANT = Prog.QUANT
        while done < n:
            best = None
            for e in Prog.ENG:
                lst = rel[e]
                if not lst:
                    continue
                fe = free[e]
                cand, ck = None, None
                for o in lst:
                    t = avail[id(o)]
                    pen = TL if (e == "act" and o.aset > 0 and o.aset != cur_set) else 0.0
                    st_ = max(t, fe) + pen
                    k = (int((st_ - o.urg) / QUANT), -o.prio, st_, o.idx)
                    if ck is None or k < ck:
                        cand, ck = o, k
                if best is None or ck < best[1]:
                    best = (cand, ck)
            o, k = best
            e = o.eng
            rel[e].remove(o)
            if e == "act" and o.aset > 0 and o.aset != cur_set:
                cur_set = o.aset
            o.start = k[2]
            free[e] = o.start + o.busy
            fin = o.start + o.lat
            fin_same = o.start + o.busy
            order[e].append(o)
            done += 1
            for s_ in o.succ:
                a = avail.get(id(s_), 0.0)
                f_ = fin_same if (s_.eng == e and o.chan is None and s_.chan is None) else fin
                if f_ > a:
                    avail[id(s_)] = f_
                else:
                    avail[id(s_)] = a
                s_.nun -= 1
                if s_.nun == 0:
                    rel[s_.eng].append(s_)
        return order, max(free.values())

    def emit(self):
        nc = self.nc
        streams = {e: [] for e in self.ENG}
        self.sim_us = []
        prev_last = None
        for reg in self.regions:
            order, span = self._schedule(reg)
            self.sim_us.append(span)
            if prev_last is not None:
                for e in self.ENG:
                    b = Op()
                    b.eng, b.fn, b.chan, b.need, b.sig = e, None, None, False, None
                    b.xedges = None
                    b.urg = 0.0
                    b.edges = [(d, False) for d in prev_last if not (d.chan is None and d.eng == e)]
                    streams[e].append(b)
            for e in self.ENG:
                streams[e].extend(order[e])
            prev_last = [order[e][-1] for e in ("pe", "act", "dve", "pool") if order[e]]
            chl = {}
            for e in self.ENG:
                for o in order[e]:
                    if o.chan is not None:
                        chl[o.chan] = o
            prev_last += [v for c_, v in chl.items() if c_ not in getattr(self, "nobar", set())]
        fin = Op()
        fin.eng, fin.fn, fin.chan, fin.need, fin.sig = "sp", None, None, False, None
        chl = {}
        for e in self.ENG:
            for o in streams[e]:
                if o.chan is not None:
                    chl[o.chan] = o
        fin.edges = [(d, False) for d in chl.values()]
        fin.xedges = None
        fin.urg = 0.0
        streams["sp"].append(fin)
        for e in self.ENG:
            for i, o in enumerate(streams[e]):
                o.idx = i
        for e in self.ENG:
            for o in streams[e]:
                deps, latest = [], {}
                for d, is_war in o.edges:
                    if d.chan is None and o.chan is None and d.eng == o.eng:
                        if o.eng in ("pe", "sp"):
                            continue
                    if d.chan is None:
                        if d.eng not in latest or latest[d.eng].idx < d.idx:
                            latest[d.eng] = d
                    else:
                        deps.append(d)
                deps.extend(getattr(o, "xedges", None) or [])
                deps.extend(latest.values())
                for d in deps:
                    d.need = True
                o.deps = deps
        with ExitStack() as es:
            sems = {}
            for e in ("pe", "act", "dve", "pool"):
                sems[e] = es.enter_context(nc.semaphore("s_" + e))
            for c in self.chan_cnt_keys():
                sems[("ch", c)] = es.enter_context(nc.semaphore("c_" + str(c)))
            cnt = {e: 0 for e in self.ENG}
            chc = {}
            for e in self.ENG:
                for o in streams[e]:
                    if o.chan is not None:
                        chc[o.chan] = chc.get(o.chan, 0) + 16
                        o.sig = (("ch", o.chan), chc[o.chan], 16)
                    elif o.need:
                        assert o.fn is not None
                        cnt[e] += 1
                        o.sig = (e, cnt[e], 1)
            self.counts = dict(cnt)
            block = es.enter_context(nc.Block())

            def mk(e):
                def body(eng):
                    waited = {}
                    for o in streams[e]:
                        for d in o.deps:
                            k, v, _ = d.sig
                            if waited.get(k, 0) >= v:
                                continue
                            eng.wait_ge(sems[k], v)
                            waited[k] = v
                        if o.fn is None:
                            continue
                        ins = o.fn(eng)
                        if o.sig is not None:
                            ins.then_inc(sems[o.sig[0]], o.sig[2])
                return body

            block.tensor(mk("pe"))
            block.scalar(mk("act"))
            block.vector(mk("dve"))
            block.gpsimd(mk("pool"))
            block.sync(mk("sp"))
        self.ops = streams

    def chan_cnt_keys(self):
        return list(self.chan_last.keys())


def _bc(ap, axis, n):
    a = ap.unsqueeze(axis)
    shp = list(a.shape)
    shp[axis] = n
    return a.to_broadcast(shp)


def build():
    nc = bass.Bass("TRN2", target_bir_lowering=False)
    P = Prog(nc)
    es = ExitStack()

    def din(name, shape):
        return nc.dram_tensor(name, list(shape), F32, kind="ExternalInput").ap()

    def dout(name, shape):
        return nc.dram_tensor(name, list(shape), F32, kind="ExternalOutput").ap()

    xp = din("xp", (SEQ, D))
    xs = din("xs", (TS, D))
    st0 = din("st0", (SB, 8, 128, 64))
    cch = din("cch", (32, DUP))
    lbp = din("lbp", (2, D))
    gpre = din("gpre", (D,))
    w_in = din("w_in", (D, DIN))
    ghn = din("ghn", (64,))
    lng = din("lng", (512,))
    lnb = din("lnb", (512,))
    w_s = din("w_s", (4, 128, 128))
    b_s = din("b_s", (4, 128))
    w_pa = din("w_pa", (512, D))
    w_pb = din("w_pb", (512, D))
    w_o = din("w_o", (D, D))
    gpost = din("gpost", (D,))
    gffn = din("gffn", (D,))
    w_up = din("w_up", (D, DUP))
    conv_w = din("conv_w", (3, DUP))
    conv_b = din("conv_b", (DUP,))
    w_dn = din("w_dn", (DFF, D))
    gpost2 = din("gpost2", (D,))
    ident_d = din("ident", (128, 128))
    maskP_d = din("maskP", (128, 128))
    maskS_d = din("maskS", (64, 64))
    selS_d = din("selS", (64, 16))
    selE_d = din("selE", (4, 64))

    yp = dout("yp", (SEQ, D))
    ys = dout("ys", (TS, D))
    spo = dout("spo", (8, 128, 64))
    sso = dout("sso", (SB, 8, 128, 64))
    cpo = dout("cpo", (2, DUP))
    cso = dout("cso", (32, DUP))
    vso = dout("vso", (TS, 512))
    x1s = nc.dram_tensor("x1s", [SEQ + TS, D], F32, kind="Internal").ap()
    sK_d = nc.dram_tensor("sK_d", [TS, D], BF16, kind="Internal").ap()
    sI_d = nc.dram_tensor("sI_d", [TS, 512], BF16, kind="Internal").ap()
    sG_d = nc.dram_tensor("sG_d", [128, 128], F32, kind="Internal").ap()

    def sb(name, shape, dt=F32):
        return es.enter_context(nc.sbuf_tensor("sb_" + name, list(shape), dt))

    Wh = sb("Wbig", (128, WELEMS), BF16)

    def Wv(off, kstride, kc, c0, n):
        return Wh[:, off + kc * kstride + c0: off + kc * kstride + c0 + n]

    OFF_PA, OFF_PB, OFF_O = 49152, 53248, 57344
    OFF_DN = 8 * DUP

    def Win(kc, c0, n): return Wv(0, DIN, kc, c0, n)
    def Wpa(kc, c0, n): return Wv(OFF_PA, D, kc, c0, n)
    def Wpb(kc, c0, n): return Wv(OFF_PB, D, kc, c0, n)
    def Wo(kc, c0, n): return Wv(OFF_O, D, kc, c0, n)
    def Wup(kc, c0, n): return Wv(0, DUP, kc, c0, n)
    def Wdn(kc, c0, n): return Wv(OFF_DN, D, kc, c0, n)

    identb = sb("identb", (128, 128), BF16)
    identf = sb("identf", (128, 128))
    maskP = sb("maskP", (128, 128))
    maskS = sb("maskS", (64, 64))
    selS = sb("selS", (64, 16))
    wTp = sb("wTp", (128, 4, 128), BF16)
    wTs = sb("wTs", (64, 4, 64), BF16)
    bsP = sb("bsP", (1, 4, 128))
    bsS = sb("bsS", (1, 4, 64))
    onesf = sb("onesf", (1, 128))

    ones2 = sb("ones2", (128, 128))
    epsc = sb("epsc", (128, 1))
    prmT = sb("prmT", (128, 120))
    prmTB = sb("prmTB", (128, 88))
    prm2 = sb("prm2", (128, 3, 8))
    selE = sb("selE", (4, 64))
    bs4 = sb("bs4", (1, 4, 4))
    w4 = sb("w4", (4, 4, 4))
    m1 = sb("m1", (4, 64))
    P0T, P1T, GPRET, GFFNT, LBT, OMLT, NOMLT = range(7)
    ghn_bc = sb("ghn_bc", (128, 64))
    lng_bc = sb("lng_bc", (128, 512))
    lnb_bc = sb("lnb_bc", (128, 512))
    GP_bc = sb("GP_bc", (128, D))
    small = sb("small", (128, 96))

    def prmv(i):
        if i < 4:
            return prmT[:, 8 * i:8 * i + 8]
        return prm2[:, i - 4, :]

    def convb_v(bi):
        return prmT[:, 32 + bi:33 + bi]

    def convw_v(tap, bi):
        if tap == 2:
            return prmT[:, 76 + bi:77 + bi]
        return prmTB[:, tap * 44 + bi:tap * 44 + bi + 1]

    layA = [("xt0", 4096), ("xsb", 2048), ("xnT0", 2048), ("xnT1", 2048), ("SG", 4096), ("F", 4096), ("B", 4096),
            ("GA", 4096), ("GB", 4096), ("QT", 2048), ("KT", 2048), ("Ktok", 2048), ("ibf", 1024),
            ("sog", 2048), ("gv", 2048), ("vnb", 1024), ("guT", 2048), ("scT", 2048), ("R1", 2048),
            ("osb", 2048), ("oab", 1024), ("oaT", 1024), ("obT", 1024), ("hTb", 2048), ("S", 2048), ("Spb", 1024)]
    layB = [("x1t0", 4096), ("x1t1", 4096), ("x1t2", 4096), ("xsb", 2048), ("xnT0", 4096), ("xnT1", 4096),
            ("upx0", 2112), ("upx1", 2112), ("cg0", 1024), ("cg1", 1024), ("cv0", 1024), ("cv1", 1024),
            ("hT", 11264), ("ybuf", 4096), ("tmu0", 2048), ("tmu1", 2048), ("carryP", 384), ("carryS", 5632),
            ("sibf", 1024)]
    szA = sum(s for _, s in layA)
    szB = sum(s for _, s in layB)
    ARENA = max(szA, szB)
    arena = sb("arena", (128, ARENA // 4))

    def mkviews(lay):
        d, off = {}, 0
        for n, s in lay:
            d[n] = (off // 4, s // 4)
            off += s
        return d
    vA, vB = mkviews(layA), mkviews(layB)

    def V(views, name, dt=F32, shape3=None, parts=128):
        o, n = views[name]
        a = arena[0:parts, o:o + n]
        if dt == BF16:
            a = a.bitcast(BF16)
        if shape3 is not None:
            a = a.rearrange("p (a b) -> p a b", b=shape3)
        return a

    def ps(name, n):
        return es.enter_context(nc.psum_tensor("ps_" + name, [128, n], F32))
    FA, FB, T2 = ps("FA", 1024), ps("FB", 1024), ps("T2", 1024)
    XA, XB = ps("XA", 512), ps("XB", 512)
    FA3 = FA[:, :].rearrange("p (a b) -> p a b", b=128)
    FB3 = FB[:, :].rearrange("p (a b) -> p a b", b=128)
    XB3 = XB[:, :].rearrange("p (a b) -> p a b", b=128)
    XAb = XA[:, :].bitcast(BF16)
    XAb3 = XAb.rearrange("p (a b) -> p a b", b=128)
    TA, TB = T2[:, 0:512], T2[:, 512:1024]

    def sm(c, n=1, parts=128):
        return small[0:parts, c:c + n]

    def fsz(ap):
        n = 1
        for d in ap.shape[1:]:
            n *= int(d)
        return n

    def dma(eng, out, in_, r, w, chan, after=()):
        nbytes = fsz(out) * int(out.shape[0]) * 4
        busy = 0.15 if eng == "sp" else max(1.0, nbytes / 300e3)
        return P.op(eng, lambda e: e.dma_start(out=out, in_=in_), r, w, chan, busy=busy,
                    lat=2.2 + nbytes / 150e3, after=after)

    def actf(out, in_, func, r, w, bias=None, scale=None, accum=None):
        kw = {}
        if bias is not None:
            kw["bias"] = bias
        if scale is not None:
            kw["scale"] = scale
        if accum is not None:
            kw["accum_out"] = accum
        aset = {AF.Exp: 6, AF.Ln: 6, AF.Sigmoid: 2, AF.Silu: 18, AF.Gelu_apprx_tanh: 11}.get(func, 0)
        return P.op("act", lambda e: e.activation(out=out, in_=in_, func=func, **kw), r, w,
                    busy=0.24 + fsz(out) * 0.00078 + (0.1 if accum is not None else 0.0), aset=aset)

    def tt(eng, out, in0, in1, op, r, w):
        b = 0.12 + fsz(out) * 0.0011 if eng == "dve" else 0.2 + fsz(out) * 0.0026
        return P.op(eng, lambda e: e.tensor_tensor(out=out, in0=in0, in1=in1, op=op), r, w, busy=b)

    def ts(eng, out, in0, s1, s2, op0, op1, r, w):
        b = 0.12 + fsz(out) * 0.0008 if eng == "dve" else 0.2 + fsz(out) * 0.0026
        if s2 is None:
            return P.op(eng, lambda e: e.tensor_scalar(out=out, in0=in0, scalar1=s1, scalar2=None, op0=op0), r, w,
                        busy=b)
        return P.op(eng, lambda e: e.tensor_scalar(out=out, in0=in0, scalar1=s1, scalar2=s2, op0=op0, op1=op1), r, w,
                    busy=b)

    def stt(out, in0, scalar, in1, op0, op1, r, w):
        return P.op("dve", lambda e: e.scalar_tensor_tensor(out=out, in0=in0, scalar=scalar, in1=in1,
                                                            op0=op0, op1=op1), r, w,
                    busy=0.16 + fsz(out) * 0.0015)

    def dvop(fn, r, w, n, k=0.0011):
        return P.op("dve", fn, r, w, busy=0.12 + n * k)

    def mms(lst, r, w, after=()):
        def fn(e):
            ins = None
            for (o_, l_, r_, st, sp_) in lst:
                ins = e.matmul(o_, l_, r_, start=st, stop=sp_)
            return ins
        b = sum(0.035 + max(fsz(o_), 96) * 0.00045 for (o_, l_, r_, st, sp_) in lst)
        return P.op("pe", fn, r, w, busy=b, lat=b + 0.25, after=after)

    def transposes(lst, r, w, after=()):
        def fn(e):
            ins = None
            for (o_, i_, id_) in lst:
                ins = e.transpose(o_, i_, id_)
            return ins
        b = 0.1 * len(lst)
        return P.op("pe", fn, r, w, busy=b, lat=b + 0.25, after=after)

    wch = [0]

    wspans = []

    def wdma(out, in_, key, span=None, after=(), own_chan=False):
        c = ("wo%d" % wch[0]) if own_chan else ("w%d" % (wch[0] % 8))
        wch[0] += 1
        if span is not None:
            wspans.append((key, span[0], span[1]))
        o_ = dma("pool", out, in_, [], [key], c, after=after)
        if own_chan:
            o_.aset = -1
        return o_

    pch = [0]

    plate = [0.0]

    def pdma(out, in_, key):
        c = "p%d" % (pch[0] % 14)
        pch[0] += 1
        return dma("sp", out, in_, [], [key] if not isinstance(key, list) else key, c)

    wdma(identb[:, :], ident_d[:, :], "identb")
    dma("sp", arena[:, vA["xt0"][0]:vA["xt0"][0] + 1024], xp[0:128, :], [], ["xt0"], "ld0")
    pdma(identf[:, :], ident_d[:, :], "identf")
    PAr = arena[:, vA["B"][0]:vA["B"][0] + 128]
    PBr = arena[:, vA["SG"][0]:vA["SG"][0] + 128]
    KB8 = [("B", h_) for h_ in range(8)]
    KS8 = [("SG", h_) for h_ in range(8)]
    for i_, (r0, vec) in enumerate(((0, lbp[0]), (8, lbp[1]), (16, gpre), (24, gffn))):
        pdma(PAr[r0:r0 + 8, :], vec.rearrange("(h k) -> h k", k=128), ("B", i_))
    pdma(PAr[32:76, :], conv_b.rearrange("(b p) -> b p", p=128), ("B", 4))
    pdma(PAr[76:120, :], conv_w[2].rearrange("(b p) -> b p", p=128), ("B", 5))
    pdma(PBr[0:44, :], conv_w[0].rearrange("(b p) -> b p", p=128), ("SG", 0))
    pdma(PBr[44:88, :], conv_w[1].rearrange("(b p) -> b p", p=128), ("SG", 1))

    pdma(maskP[:, :], maskP_d[:, :], "maskP")
    pdma(maskS[:, :], maskS_d[:, :], "maskS")
    pdma(selS[:, :], selS_d[:, :], "selS")
    pdma(selE[:, :], selE_d[:, :], "selE")
    pdma(ghn_bc[:, :], ghn.partition_broadcast(128), "ghn_bc")
    pdma(lng_bc[:, :], lng.partition_broadcast(128), "lng_bc")
    pdma(lnb_bc[:, :], lnb.partition_broadcast(128), "lnb_bc")
    pdma(GP_bc[:, :], gpost.partition_broadcast(128), "GP_bc")
    pdma(bsP[0:1, :, :], b_s.unsqueeze(0), "bsP")
    pdma(bs4[0:1, :, :], b_s[:, 0:4].unsqueeze(0), "bs4")
    pdma(w4[:, :, :], w_s[:, 0:4, 0:4].rearrange("g b a -> b g a"), "w4")
    transposes([(XB[:, 0:120], PAr[0:120, :], identf[0:120, 0:120])], KB8 + ["identf"], ["XB"])
    actf(prmT[:, :], XB[:, 0:120], AF.Copy, ["XB"], ["prmT"])
    transposes([(XB[:, 0:88], PBr[0:88, :], identf[0:88, 0:88])], KS8 + ["identf"], ["XB"])
    actf(prmTB[:, :], XB[:, 0:88], AF.Copy, ["XB"], ["prmTB"])
    P.op("dve", lambda e: e.memset(onesf[:, :], 1.0), [], ["onesf"])
    P.op("dve", lambda e: e.memset(ones2[:, :], 1.0), [], ["ones2"])
    P.op("dve", lambda e: e.memset(epsc[:, :], EPS), [], ["epsc"])
    tt("dve", prmv(LBT), prmv(P0T), prmv(P1T), ALU.subtract, ["prmT"], ["lbT"])
    dvop(lambda e: e.tensor_copy(out=bsS[0:1, :, :].rearrange("p g (j s) -> p (g j) s", s=16),
                                 in_=_bc(bs4[0:1, :, :].rearrange("p g j -> p (g j)"), 2, 16)), ["bs4"], ["bsS"], 256)
    actf(prmv(LBT), prmv(LBT), AF.Sigmoid, ["lbT"], ["lbT"])
    ts("dve", prmv(OMLT), prmv(LBT), -1.0, 1.0, ALU.mult, ALU.add, ["lbT"], ["omlT"])
    ts("dve", prmv(NOMLT), prmv(OMLT), -1.0, None, ALU.mult, None, ["omlT"], ["nomlT"])

    wTs_keys = [("wTs", g) for g in range(4)]

    def load_w(dst_fn, src, K, ncols, cw, name, order=None, off=0, kstride=0, overlay=False, own_chan=False,
               late=None):
        ncg = ncols // cw
        todo = [(cg, kc) for cg in (order if order is not None else range(ncg)) for kc in range(K)]
        if overlay and late is not None:
            def is_late(cg, kc):
                s0 = off + kc * kstride + cg * cw
                return any(a_ < s0 + cw and s0 < b_ and late(k_) for (k_, a_, b_) in wspans)
            todo = [x for x in todo if not is_late(*x)] + [x for x in todo if is_late(*x)]
        for cg, kc in todo:
            if True:
                s0 = off + kc * kstride + cg * cw
                aft = [k_ for (k_, a_, b_) in wspans if a_ < s0 + cw and s0 < b_] if overlay else ()
                wdma(dst_fn(kc, cg * cw, cw), src[kc * 128:(kc + 1) * 128, cg * cw:(cg + 1) * cw], (name, cg, kc),
                     span=None if overlay else (s0, s0 + cw), after=aft, own_chan=own_chan)

    load_w(Win, w_in, 8, DIN, 1024, "w_in", order=[1, 0, 2, 3, 4, 5], off=0, kstride=DIN)
    load_w(Wpa, w_pa, 4, D, 1024, "w_pa", off=OFF_PA, kstride=D)
    load_w(Wpb, w_pb, 4, D, 1024, "w_pb", off=OFF_PB, kstride=D)
    load_w(Wo, w_o, 8, D, 1024, "w_o", off=OFF_O, kstride=D)
    wspans.append(("xt1", 65536, 67584))

    def wk(name, K, c0, n, cw=1024):
        return [(name, cg, kc) for cg in range(c0 // cw, (c0 + n - 1) // cw + 1) for kc in range(K)]

    xt = [V(vA, "xt0"), Wh[:, 65536:67584].bitcast(F32)]
    xsb = V(vA, "xsb", BF16)
    xnT = [V(vA, "xnT0", BF16, 128), V(vA, "xnT1", BF16, 128)]
    SG3, F3, B3, GA3, GB3 = (V(vA, n, F32, 128) for n in ("SG", "F", "B", "GA", "GB"))
    GAt, GBt = V(vA, "GA"), V(vA, "GB")
    QT3, KT3 = V(vA, "QT", BF16, 128), V(vA, "KT", BF16, 128)
    Ktok = V(vA, "Ktok", BF16)
    ibf = V(vA, "ibf", BF16)
    sog = V(vA, "sog")
    gv = V(vA, "gv")
    vnb = V(vA, "vnb", BF16)
    guT3 = V(vA, "guT", F32, 128)
    scT3 = V(vA, "scT", BF16, 128)
    osb = V(vA, "osb")
    oab = V(vA, "oab", BF16)
    oaT3 = V(vA, "oaT", BF16, 128)
    obT3 = V(vA, "obT", BF16, 128)
    hTb3 = V(vA, "hTb", BF16, 128)
    R1 = V(vA, "R1")
    S_ = V(vA, "S")
    S3 = V(vA, "S", F32, 64)
    Spb = V(vA, "Spb", BF16)
    junkA = arena[:, vA["oab"][0]:vA["oab"][0] + 512].bitcast(BF16)
    XAb = XA[:, :].bitcast(BF16)
    XAb3 = XAb.rearrange("p (a b) -> p a b", b=128)
    XBb = XB[:, :].bitcast(BF16)
    TB3 = TB.rearrange("p (a b) -> p a b", b=128)

    def K8(n):
        return [(n, h) for h in range(8)]

    P.op("dve", lambda e: e.memset(S_, 0.0), [], ["S"])
    P.op("dve", lambda e: e.memset(scT3[64:128, :, 0:64], 0.0), [], ["scT"])

    def rms_rstd(ss_ap, ln_ap, out_ap, dim, rkeys, wkey):
        actf(ln_ap, ss_ap, AF.Ln, rkeys + ["epsc"], [wkey + "_ln"], bias=epsc[0:ss_ap.shape[0], :], scale=1.0 / dim)
        actf(out_ap, ln_ap, AF.Exp, [wkey + "_ln"], [wkey], scale=-0.5)

    def xload(ti, kind, which):
        T = 128 if kind == "p" else TS
        src = xp[ti * 128:(ti + 1) * 128, :] if kind == "p" else xs[:, :]
        dma("sp", xt[which][0:T, :], src, [], ["xt%d" % which], "ld%d" % which)

    def make_tile(ti, kind):
        T = 128 if kind == "p" else TS
        par = ti % 2 if kind == "p" else 0
        xtb, xtk = xt[0], "xt0"
        xrb, xrk = xt[1], "xt1"
        xn, xnk = xnT[par], "xnT%d" % par
        row0 = ti * 128 if kind == "p" else SEQ
        C = {}

        def fm(ps3, c0, nb, pkey):
            for b in range(nb):
                mms([(ps3[:, b, 0:T], Win(kc, c0 + b * 128, 128), xn[:, kc, 0:T], kc == 0, kc == 7)
                     for kc in range(8)], [xnk] + wk("w_in", 8, c0 + b * 128, 128), [pkey])

        def tm(psv, c0, pkey):
            mms([(psv[0:T, :], xn[:, kc, 0:T], Win(kc, c0, 512), kc == 0, kc == 7) for kc in range(8)],
                [xnk] + wk("w_in", 8, c0, 512), [pkey])

        def c_F1():
            actf(xsb[0:T, :], xtb[0:T, :], AF.Square, [xtk], ["xsb", "ss"], accum=sm(0, 1, T))
            rms_rstd(sm(0, 1, T), sm(1, 1, T), sm(2, 1, T), D, ["ss"], "rstd")
            actf(xsb[0:T, :], xtb[0:T, :], AF.Identity, [xtk, "rstd"], ["xsb"], scale=sm(2, 1, T))
            transposes([(XAb3[:, kc, 0:T], xsb[0:T, kc * 128:(kc + 1) * 128], identb[0:T, 0:T]) for kc in range(8)],
                       ["xsb", "identb"], ["XA"])
            tt("dve", xn[:, :, 0:T], XAb3[:, :, 0:T], _bc(prmv(GPRET), 2, T), ALU.mult, ["XA", "prmT"], [xnk])

        def c_F2():
            fm(FA3, 1024, 8, "FA")
            fm(FB3, 0, 8, "FB")
            actf(SG3[:, :, 0:T], FA3[:, :, 0:T], AF.Sigmoid, ["FA"], K8("SG"))
            for h in range(8):
                actf(F3[:, h, 0:T], SG3[:, h, 0:T], AF.Ln, [("SG", h), "omlT", "lbT"], [("F", h)],
                     scale=prmv(OMLT)[:, h:h + 1], bias=prmv(LBT)[:, h:h + 1])
            for h in range(8):
                actf(SG3[:, h, 0:T], SG3[:, h, 0:T], AF.Identity, [("SG", h), "omlT", "nomlT"], [("SG", h)],
                     scale=prmv(NOMLT)[:, h:h + 1], bias=prmv(OMLT)[:, h:h + 1])

        def c_F4():
            if kind == "p":
                for h in range(8):
                    dvop(lambda e, h=h: e.tensor_tensor_scan(out=B3[:, h, 0:T], data0=ones2[:, 0:T],
                                                                    data1=F3[:, h, 0:T], initial=0.0,
                                                                    op0=ALU.mult, op1=ALU.add),
                         [("F", h), "ones2"], [("B", h)], T, 0.0022)
                dvop(lambda e: e.tensor_copy(out=small[:, 8:16], in_=B3[:, :, 63]), K8("B"), ["b63"], 8)
                ts("dve", small[:, 16:24], B3[:, :, 63], -1.0, None, ALU.mult, None, K8("B"), ["nb63"])
                actf(small[:, 24:32], small[:, 8:16], AF.Exp, ["b63"], ["eb63"])
                for h in range(8):
                    actf(F3[:, h, 0:T], B3[:, h, 0:T], AF.Exp, [("B", h), "nb63"], [("F", h)],
                         bias=small[:, 16 + h:17 + h], scale=1.0)
                    actf(B3[:, h, 0:T], B3[:, h, 0:T], AF.Exp, [("B", h), "b63"], [("B", h)],
                         bias=small[:, 8 + h:9 + h], scale=-1.0)
            else:
                dvop(lambda e: e.tensor_copy(out=B3[:, :, 0:16], in_=F3[:, :, 0:16]), K8("F"), K8("B"), 128)
                for j in range(1, 4):
                    tt("dve", B3[:, :, 16 * j:16 * j + 16], B3[:, :, 16 * j - 16:16 * j],
                       F3[:, :, 16 * j:16 * j + 16], ALU.add, K8("B") + K8("F"), K8("B"))
                actf(F3[:, :, 0:T], B3[:, :, 0:T], AF.Exp, K8("B"), K8("F"))
                actf(B3[:, :, 0:T], B3[:, :, 0:T], AF.Exp, K8("B"), K8("B"), scale=-1.0)
            tt("dve", KT3[:, :, 0:T], SG3[:, :, 0:T], B3[:, :, 0:T], ALU.mult, K8("SG") + K8("B"), ["KT"])

        def c_F4b():
            actf(B3[:, :, 0:T], FB3[:, :, 0:T], AF.Silu, ["FB"], K8("B"))
            tt("dve", QT3[:, :, 0:T], B3[:, :, 0:T], F3[:, :, 0:T], ALU.mult, K8("B") + K8("F"), ["QT"])
            if kind == "p":
                dvop(lambda e: e.tensor_copy(out=small[:, 32:40], in_=F3[:, :, T - 1]), K8("F"), ["glast"], 8)
            else:
                glS = V(vA, "S", F32, 16)
                dvop(lambda e: e.tensor_copy(out=glS[:, 0:8, :], in_=F3[:, :, 48:64]), K8("F"), ["S"], 128)

        def c_F3():
            tm(XB, 2048, "XB")
            actf(ibf[0:T, :], XB[0:T, :], AF.Copy, ["XB"], ["ibf"])
            tm(TA, 2560, "TA")
            actf(sog[0:T, :], TA[0:T, :], AF.Silu, ["TA"], ["sog"])
            s3 = sog[0:T, :].rearrange("p (h d) -> p h d", d=64)
            tt("dve", s3, s3, _bc(ghn_bc[0:T, :], 1, 8), ALU.mult, ["sog", "ghn_bc"], ["sog"])

        def c_F5():
            tm(XA, 3584, "XA")
            actf(gv[0:T, :], XA[0:T, :], AF.Gelu_apprx_tanh, ["XA"], ["gv"])
            dvop(lambda e: e.bn_stats(out=small[0:T, 40:46], in_=gv[0:T, :]), ["gv"], ["bst"], 512)
            dvop(lambda e: e.bn_aggr(out=small[0:T, 46:48], in_=small[0:T, 40:46]), ["bst"], ["mv"], 8)
            rms_rstd(small[0:T, 47:48], sm(48, 1, T), sm(49, 1, T), 1.0, ["mv"], "rsv")
            stt(sm(63, 1, T), small[0:T, 46:47], -1.0, sm(49, 1, T), ALU.mult, ALU.mult, ["mv", "rsv"], ["nmr"])
            actf(gv[0:T, :], gv[0:T, :], AF.Identity, ["gv", "rsv", "nmr"], ["gv"], scale=sm(49, 1, T),
                 bias=sm(63, 1, T))
            tt("dve", gv[0:T, :], gv[0:T, :], lng_bc[0:T, :], ALU.mult, ["gv", "lng_bc"], ["gv"])
            if kind == "p":
                tt("dve", vnb[0:T, :], gv[0:T, :], lnb_bc[0:T, :], ALU.add, ["gv", "lnb_bc"], ["vnb"])
            else:
                tt("dve", gv[0:T, :], gv[0:T, :], lnb_bc[0:T, :], ALU.add, ["gv", "lnb_bc"], ["gv"])
                dma("sp", vso[:, :], gv[0:T, :], ["gv"], ["vso"], "vs")
                actf(vnb[0:T, :], gv[0:T, :], AF.Copy, ["gv"], ["vnb"])
            fm(XB3, 3072, 4, "XB")
            actf(guT3[:, 0:4, 0:T], XB3[:, 0:4, 0:T], AF.Gelu_apprx_tanh, ["XB"], ["guT"])

        def gates():
            for cb in range(2):
                tm(FA[:, cb * 512:(cb + 1) * 512], 4096 + cb * 512, "FA")
            actf(GAt[0:T, :], FA[0:T, :], AF.Sigmoid, ["FA"], ["GA"])
            for cb in range(2):
                tm(FB[:, cb * 512:(cb + 1) * 512], 5120 + cb * 512, "FB")
            actf(GBt[0:T, :], FB[0:T, :], AF.Sigmoid, ["FB"], ["GB"])

        def c_B1():
            if kind == "s":
                gates()
            transposes([(XBb[0:T, h * 128:(h + 1) * 128], KT3[:, h, 0:T], identb[:, :]) for h in range(8)],
                       ["KT", "identb"], ["XB"])
            actf(Ktok[0:T, :], XBb[0:T, :], AF.Copy, ["XB"], ["Ktok"])

        def c_B2():
            if kind == "p":
                mms([x for h in range(8) for x in
                     ((FA3[0:128, h, 64:128], KT3[:, h, 0:128], QT3[:, h, 64:128], True, True),
                      (FA3[0:64, h, 0:64], KT3[:, h, 0:64], QT3[:, h, 0:64], True, True))],
                    ["KT", "QT"], ["FA"])
                P.urg = URG
                for hb in range(2):
                    hs = slice(4 * hb, 4 * hb + 4)
                    tt("dve", scT3[:, hs, 64:128], FA3[:, hs, 64:128], _bc(maskP[:, 64:128], 1, 4), ALU.mult,
                       ["FA", "maskP"], ["scT"])
                    tt("dve", scT3[0:64, hs, 0:64], FA3[0:64, hs, 0:64], _bc(maskP[0:64, 0:64], 1, 4), ALU.mult,
                       ["FA", "maskP"], ["scT"])
                tt("dve", S3, S3, _bc(small[:, 24:32], 2, 64), ALU.mult, ["S", "eb63"], ["S"])
                actf(Spb, S_, AF.Copy, ["S"], ["Spb"])
                P.urg = 0.0
            else:
                mms([(FA3[0:T, h, 0:T], KT3[:, h, 0:T], QT3[:, h, 0:T], True, True) for h in range(8)],
                    ["KT", "QT"], ["FA"])
                for hb in range(2):
                    hs = slice(4 * hb, 4 * hb + 4)
                    tt("dve", scT3[0:T, hs, 0:T], FA3[0:T, hs, 0:T], _bc(maskS[:, :], 1, 4), ALU.mult,
                       ["FA", "maskS"], ["scT"])

        def c_B3():
            if kind == "p":
                mms([x for h in range(8) for x in
                     ((TA[0:T, h * 64:(h + 1) * 64], scT3[0:T, h, 0:T], ibf[0:T, h * 64:(h + 1) * 64], True, False),
                      (TA[0:T, h * 64:(h + 1) * 64], QT3[:, h, 0:T], Spb[:, h * 64:(h + 1) * 64], False, True))],
                    ["scT", "ibf", "QT", "Spb"], ["TA"])
                actf(osb[0:T, :], TA[0:T, :], AF.Copy, ["TA"], ["osb"])
                mms([(TB[:, h * 64:(h + 1) * 64], Ktok[0:T, h * 128:(h + 1) * 128], ibf[0:T, h * 64:(h + 1) * 64],
                      True, True) for h in range(8)], ["Ktok", "ibf"], ["TB"])
                tt("dve", S_, TB[:, :], S_, ALU.add, ["TB", "S"], ["S"])
                tt("dve", S3, S3, _bc(small[:, 32:40], 2, 64), ALU.mult, ["S", "glast"], ["S"])
                if ti == NPT - 1:
                    dma("sp", spo.rearrange("h k d -> k h d"), S3, ["S"], ["spo"], "spo")
            else:
                mms([(TA[0:T, h * 64:(h + 1) * 64], scT3[0:T, h, 0:T], ibf[0:T, h * 64:(h + 1) * 64], True, True)
                     for h in range(8)], ["scT", "ibf"], ["TA"])
                glS = V(vA, "S", F32, 16)
                def half(name, i_):
                    o_ = vA[name][0] + 512 * i_
                    return arena[:, o_:o_ + 512].bitcast(BF16)
                S0b = [half("SG", 0), half("SG", 1), half("B", 0), half("B", 1), half("xt0", 0), half("xt0", 1),
                       V(vA, "KT", BF16), V(vA, "gv", BF16)]
                S0bk = [[("SG", h_) for h_ in range(4)], [("SG", h_) for h_ in range(4, 8)],
                        [("B", h_) for h_ in range(4)], [("B", h_) for h_ in range(4, 8)], ["xt0"], ["xt0"],
                        ["KT"], ["gv"]]
                NSB = len(S0b)

                def sload(h_):
                    dma("pool", S0b[h_ % NSB].rearrange("p (s d) -> p s d", d=64),
                        st0[:, h_, :, :].rearrange("s k d -> k s d"), [], S0bk[h_ % NSB], "sl%d" % (h_ % NSB))
                seltmp = arena[0:64, vA["scT"][0]:vA["scT"][0] + 1024]
                selk = ["scT", "R1"]
                dma("sp", sK_d[:, :], Ktok[0:T, :], ["Ktok"], ["sK_d"], "sv0")
                dma("sp", sI_d[:, :], ibf[0:T, :], ["ibf"], ["sI_d"], "sv1")
                dma("sp", sG_d[:, :], V(vA, "S")[:, 0:128], ["S"], ["sG_d"], "sv2")
                for h in range(NSB):
                    sload(h)
                for h in range(8):
                    b = h % NSB
                    PB_, pbk_ = (FA, "FA") if h % 2 == 0 else (FB, "FB")
                    mms([(PB_[0:T, 0:512], QT3[:, h, 0:T], S0b[b][:, 0:512], True, True),
                         (PB_[0:T, 512:1024], QT3[:, h, 0:T], S0b[b][:, 512:1024], True, True)],
                        ["QT"] + S0bk[b], [pbk_])
                    if h + NSB < 8:
                        sload(h + NSB)
                    for half in range(2):
                        cs_ = slice(512 * half, 512 * half + 512)
                        tt("dve", seltmp[:, cs_].rearrange("p (s d) -> p s d", d=64),
                           PB_[0:T, cs_].rearrange("p (s d) -> p s d", d=64),
                           _bc(selS[:, 8 * half:8 * half + 8], 2, 64), ALU.mult, [pbk_, "selS"], selk)
                    dvop(lambda e, h=h: e.tensor_reduce(
                        out=osb[0:T, h * 64:(h + 1) * 64], in_=seltmp.rearrange("p (s d) -> p d s", d=64),
                        axis=AX.X, op=ALU.add), selk, [("osbS", h)], 1024)
                tt("dve", osb[0:T, :], TA[0:T, :], osb[0:T, :], ALU.add, ["TA"] + [("osbS", h) for h in range(8)],
                   ["osb"])

        def c_B4():
            osq = R1[0:T, 0:512]
            tt("dve", osq, osb[0:T, :], osb[0:T, :], ALU.mult, ["osb"], ["R1"])
            dvop(lambda e: e.tensor_reduce(out=small[0:T, 50:58], in_=osq.rearrange("p (h d) -> p h d", d=64),
                                           axis=AX.X, op=ALU.add), ["R1"], ["ssq"], 512)
            rms_rstd(small[0:T, 50:58], small[0:T, 64:72], small[0:T, 72:80], 64.0, ["ssq"], "r8")
            o3 = osb[0:T, :].rearrange("p (h d) -> p h d", d=64)
            tt("dve", o3, o3, _bc(small[0:T, 72:80], 2, 64), ALU.mult, ["osb", "r8"], ["osb"])
            tt("dve", oab[0:T, :], osb[0:T, :], sog[0:T, :], ALU.mult, ["osb", "sog"], ["oab"])
            transposes([(XAb3[:, c, 0:T], oab[0:T, c * 128:(c + 1) * 128], identb[0:T, 0:T]) for c in range(4)],
                       ["oab", "identb"], ["XA"])
            actf(oaT3[:, 0:4, 0:T], XAb3[:, 0:4, 0:T], AF.Copy, ["XA"], ["oaT"])

        def c_B5():
            wT = wTp if kind == "p" else wTs
            bsr = bsP if kind == "p" else bsS
            mms([x for g in range(4) for x in
                 ((TB3[:, g, 0:T], vnb[0:T, g * 128:(g + 1) * 128], wT[0:T, g, 0:T], True, False),
                  (TB3[:, g, 0:T], onesf[0:1, :], bsr[0:1, g, 0:T], False, True))],
                ["vnb", "wTp", "onesf", "bsP", "bsS"] + wTs_keys, ["TB"])
            tt("dve", obT3[:, 0:4, 0:T], TB3[:, 0:4, 0:T], guT3[:, 0:4, 0:T], ALU.mult, ["TB", "guT"], ["obT"])
            if kind == "p":
                gates()
            for cb in range(2):
                mms([(FA[0:T, cb * 512:(cb + 1) * 512], oaT3[:, kc, 0:T], Wpa(kc, cb * 512, 512), kc == 0, kc == 3)
                     for kc in range(4)], ["oaT"] + wk("w_pa", 4, 0, 1024), ["FA"])
            for cb in range(2):
                mms([(FB[0:T, cb * 512:(cb + 1) * 512], obT3[:, kc, 0:T], Wpb(kc, cb * 512, 512), kc == 0, kc == 3)
                     for kc in range(4)], ["obT"] + wk("w_pb", 4, 0, 1024), ["FB"])

        def c_B6():
            tt("dve", GAt[0:T, :], FA[0:T, :], GAt[0:T, :], ALU.mult, ["FA", "GA"], ["GA"])
            tt("dve", GBt[0:T, :], FB[0:T, :], GBt[0:T, :], ALU.mult, ["FB", "GB"], ["GB"])
            htok = R1.bitcast(BF16)
            tt("dve", htok[0:T, :], GAt[0:T, :], GBt[0:T, :], ALU.add, ["GA", "GB"], ["R1"])
            transposes([(XAb3[:, c, 0:T], htok[0:T, c * 128:(c + 1) * 128], identb[0:T, 0:T]) for c in range(8)],
                       ["R1", "identb"], ["XA"])
            actf(hTb3[:, :, 0:T], XAb3[:, :, 0:T], AF.Copy, ["XA"], ["hTb"])
            for cb in range(2):
                mms([(FA[0:T, cb * 512:(cb + 1) * 512], hTb3[:, kc, 0:T], Wo(kc, cb * 512, 512), kc == 0, kc == 7)
                     for kc in range(8)], ["hTb"] + wk("w_o", 8, 0, 1024), ["FA"])
            for cb in range(2):
                actf(junkA[0:T, cb * 512:(cb + 1) * 512], FA[0:T, cb * 512:(cb + 1) * 512], AF.Square, ["FA"],
                     ["oab", "oaT", ("ssm", cb)], accum=sm(58 + cb, 1, T))
            tt("dve", sm(60, 1, T), sm(58, 1, T), sm(59, 1, T), ALU.add, [("ssm", 0), ("ssm", 1)], ["ssmt"])
            rms_rstd(sm(60, 1, T), sm(61, 1, T), sm(62, 1, T), D, ["ssmt"], "rm")
            for cb in range(2):
                cs_ = slice(cb * 512, (cb + 1) * 512)
                stt(R1[0:T, :], FA[0:T, cs_], sm(62, 1, T), GP_bc[0:T, cs_], ALU.mult, ALU.mult,
                    ["FA", "rm", "GP_bc"], ["R1"])
                tt("dve", xrb[0:T, cs_], R1[0:T, :], xrb[0:T, cs_], ALU.add, ["R1", xrk], [xrk])
            dma("sp", x1s[row0:row0 + T, :], xrb[0:T, :], [xrk], [("x1s", ti if kind == "p" else NPT)], "x1st")

        C.update(F1=c_F1, F2=c_F2, F4=c_F4, F4b=c_F4b, F3=c_F3, F5=c_F5, B1=c_B1, B2=c_B2, B3=c_B3, B4=c_B4, B5=c_B5, B6=c_B6)
        return C

    def build_gate_weights():
        wraw = V(vA, "GA", F32, 128)
        for g in range(4):
            pdma(wraw[:, g, :], w_s[g], ["GA"])
        transposes([(XB3[:, g, :], wraw[:, g, :], identf[:, :]) for g in range(4)],
                   ["GA"] + ["identf"], ["XB"], after=["QT"])
        tt("dve", wTp[:, :, :], XB3[:, 0:4, :], _bc(maskP[:, :], 1, 4), ALU.mult, ["XB", "maskP"], ["wTp"])
        for g in range(4):
            mms([(XB[0:4, 0:64], w4[:, g, :], selE[:, :], True, True)], ["w4", "selE"], ["XB"], after=["QT"])
            actf(m1[:, :], XB[0:4, 0:64], AF.Copy, ["XB"], ["m1"])
            mms([(XB[0:64, 64:128], selE[:, :], m1[:, :], True, True)], ["m1", "selE"], ["XB"])
            tt("dve", wTs[:, g, :], XB[0:64, 64:128], maskS[:, :], ALU.mult, ["XB", "maskS"], [("wTs", g)])

    URG = 0.0
    ORDER = ["F1", "B1", "F2", "B2", "F4", "B3", "F4b", "B4", "F3", "B5", "F5", "B6"]
    tiles = [make_tile(ti, "p") for ti in range(NPT)]
    tileS = make_tile(0, "s")
    for rnd in range(NPT + 1):
        fr = tiles[rnd] if rnd < NPT else None
        bk = tiles[rnd - 1] if rnd >= 1 else None
        if rnd == 1:
            P.tag = "gatew"
            build_gate_weights()
        if bk is not None:
            P.tag = "xr"
            xload(rnd - 1, "p", 1)
        for c in ORDER:
            P.tag = c
            if c[0] == "F" and fr is not None:
                fr[c]()
                if c == "F1":
                    P.tag = "xl"
                    if rnd + 1 < NPT:
                        xload(rnd + 1, "p", 0)
                    elif rnd + 1 == NPT:
                        xload(0, "s", 0)
            if c[0] == "B" and bk is not None:
                bk[c]()
    P.tag = "xr"
    xload(0, "s", 1)
    for c in ORDER:
        if c[0] == "F":
            P.tag = "s" + c
            tileS[c]()
    for c in ORDER:
        if c[0] == "B":
            P.tag = "s" + c
            tileS[c]()

    P.tag = "preS0"
    xtmp = arena[:, vA["xnT0"][0]:vA["xnT0"][0] + 1024]
    xn2pre = V(vA, "F", BF16, 256)
    for sub in range(2):
        dma("sp", xtmp, x1s[sub * 128:(sub + 1) * 128, :], [("x1s", sub)], ["xnT0", "xnT1"], "pre")
        actf(xsb[:, :], xtmp, AF.Square, ["xnT0", "xnT1"], ["xsb", "ss"], accum=sm(0, 1, 128))
        rms_rstd(sm(0, 1, 128), sm(1, 1, 128), sm(2, 1, 128), D, ["ss"], "rstd")
        ts("dve", xsb[:, :], xtmp, sm(2, 1, 128), None, ALU.mult, None, ["xnT0", "xnT1", "rstd"], ["xsb"])
        transposes([(XAb3[:, kc, :], xsb[:, kc * 128:(kc + 1) * 128], identb[:, :]) for kc in range(8)],
                   ["xsb", "identb"], ["XA"])
        tt("dve", xn2pre[:, :, sub * 128:(sub + 1) * 128], XAb3[:, :, :], _bc(prmv(GFFNT), 2, 128), ALU.mult,
           ["XA", "prmT"], K8("F"))
    P.tag = "wup"
    load_w(Wup, w_up, 8, DUP, 1408, "w_up", order=[0, 2, 1, 3], off=0, kstride=DUP, overlay=True,
           late=lambda k_: k_[0] == "w_in" and k_[1] >= 4)
    P.barrier(keep=("w_up",))
    load_w(Wdn, w_dn, NJ, D, 1024, "w_dn", off=OFF_DN, kstride=D, overlay=True, own_chan=True)
    pdma(GP_bc[:, :], gpost2.partition_broadcast(128), "GP_bc")

    TBP = 256
    NSUP = SEQ // TBP
    x1t = [V(vB, "x1t%d" % i) for i in range(3)]
    xrot = [0]

    def getx():
        k = xrot[0] % 3
        xrot[0] += 1
        return x1t[k], "x1t%d" % k, "ldB%d" % k

    xsbB = V(vB, "xsb", BF16)
    xn2 = [V(vB, "xnT0", BF16, TBP), V(vB, "xnT1", BF16, TBP)]
    upx = [V(vB, "upx%d" % i, F32, 264) for i in range(2)]
    cg = [V(vB, "cg0"), V(vB, "cg1")]
    cv = [V(vB, "cv0"), V(vB, "cv1")]
    hT3 = V(vB, "hT", BF16, TBP)
    ybuf = V(vB, "ybuf")
    tmu = [V(vB, "tmu0", parts=32), V(vB, "tmu1", parts=32)]
    carryP = V(vB, "carryP")[:, 0:88].rearrange("p (r g j) -> p g j r", g=2, r=2)
    carryS = V(vB, "carryS").rearrange("p (g j r) -> p g j r", g=2, r=32)
    P.op("dve", lambda e: e.memset(V(vB, "carryP"), 0.0), [], ["carryP"])
    sS0h = V(vB, "carryS")[:, 0:1024].rearrange("p (s d) -> p s d", d=64)
    sgl = V(vB, "carryS")[:, 1024:1152].rearrange("p (h s) -> p h s", s=16)
    simask = V(vB, "tmu0", BF16, parts=64)
    sKtok = V(vB, "tmu1", BF16, parts=64)
    sibf = V(vB, "sibf", BF16, parts=64)
    dma("sp", sKtok[:, :], sK_d[:, :], [], ["tmu1"], "sv0")
    dma("sp", sibf[:, :], sI_d[:, :], [], ["sibf"], "sv1")
    dma("sp", V(vB, "carryS")[:, 1024:1152], sG_d[:, :], [], [("carryS", "g")], "sv2")

    def state_update(h):
        dma("sp", sS0h, st0[:, h, :, :].rearrange("s k d -> k s d"), [], ["carryS"], "st0")
        tt("dve", simask.rearrange("p (s d) -> p s d", d=64), _bc(sibf[:, h * 64:(h + 1) * 64], 1, 16),
           _bc(selS[:, :], 2, 64), ALU.mult, ["sibf", "selS"], ["tmu0"])
        mms([(TA[:, :], sKtok[:, h * 128:(h + 1) * 128], simask[:, 0:512], True, True)], ["tmu1", "tmu0"], [("T2", 0)])
        mms([(TB[:, :], sKtok[:, h * 128:(h + 1) * 128], simask[:, 512:1024], True, True)], ["tmu1", "tmu0"],
            [("T2", 1)])
        S0f = sS0h.rearrange("p s d -> p (s d)")
        tt("dve", S0f[:, 0:512], TA[:, :], S0f[:, 0:512], ALU.add, [("T2", 0), "carryS"], ["carryS"])
        tt("dve", S0f[:, 512:1024], TB[:, :], S0f[:, 512:1024], ALU.add, [("T2", 1), "carryS"], ["carryS"])
        tt("dve", sS0h, sS0h, _bc(sgl[:, h, :], 2, 64), ALU.mult, ["carryS", ("carryS", "g")], ["carryS"])
        dma("sp", sso[:, h, :, :].rearrange("s k d -> k s d"), sS0h, ["carryS"], [("sso", h)], "so0")

    def cache_prologue():
        cbuf = V(vB, "tmu1", parts=32)
        for c in range(11):
            dma("sp", cbuf[:, 0:512], cch[:, c * 512:(c + 1) * 512], [], ["tmu1"], "cch")
            transposes([(XB[:, b * 32:(b + 1) * 32], cbuf[:, b * 128:(b + 1) * 128], identf[0:32, 0:32])
                        for b in range(4)], ["tmu1", "identf"], ["XB"])
            dst = V(vB, "carryS")[:, c * 128:(c + 1) * 128]
            actf(dst, XB[:, 0:128], AF.Copy, ["XB"], ["carryS"])

    def geo(kind):
        if kind == "p":
            return TBP, 2, 1, 2, 128
        return TS, 32, 16, 1, TS

    def S0(n, kind):
        T, PV, sh, nsub, Tt = geo(kind)
        xn = xn2[n % 2]
        xnk = "xnT%d" % (n % 2)
        for sub in range(nsub):
            row0 = n * TBP + sub * 128 if kind == "p" else SEQ
            xb_, xk, xc = getx()
            dma("sp", xb_[0:Tt, :], x1s[row0:row0 + Tt, :], [], [xk], xc)
            actf(xsbB[0:Tt, :], xb_[0:Tt, :], AF.Square, [xk], ["xsb", "ss"], accum=sm(0, 1, Tt))
            rms_rstd(sm(0, 1, Tt), sm(1, 1, Tt), sm(2, 1, Tt), D, ["ss"], "rstd")
            ts("dve", xsbB[0:Tt, :], xb_[0:Tt, :], sm(2, 1, Tt), None, ALU.mult, None, [xk, "rstd"], ["xsb"])
            transposes([(XAb3[:, kc, 0:Tt], xsbB[0:Tt, kc * 128:(kc + 1) * 128], identb[0:Tt, 0:Tt])
                        for kc in range(8)], ["xsb", "identb"], ["XA"])
            tt("dve", xn[:, :, sub * 128:sub * 128 + Tt], XAb3[:, :, 0:Tt], _bc(prmv(GFFNT), 2, Tt), ALU.mult,
               ["XA", "prmT"], [(xnk, sub)])

    pbanks = [(FA[:, 0:512], ("FA", 0)), (FA[:, 512:1024], ("FA", 1)),
              (FB[:, 0:512], ("FB", 0)), (FB[:, 512:1024], ("FB", 1))]

    def pbank(j):
        bk, bkey = pbanks[j % 4]
        return bk.rearrange("p (g t) -> p g t", g=2), bkey

    def stA(n, kind, j):
        T, PV, sh, nsub, Tt = geo(kind)
        bk3, bkey = pbank(j)
        xn = xn2[n % 2]
        xnk = "xnT%d" % (n % 2)
        for g2 in range(2):
            c0 = g2 * DFF + j * 128
            mms([(bk3[:, g2, 0:T], Wup(kc, c0, 128), xn[:, kc, 0:T], kc == 0, kc == 7) for kc in range(8)],
                [(xnk, 0), (xnk, 1)] + wk("w_up", 8, c0, 128, 1408), [bkey])

    def stB(n, kind, j):
        T, PV, sh, nsub, Tt = geo(kind)
        bk3, bkey = pbank(j)
        u = j % 2
        ux, uk = upx[u], "upx%d" % u
        carry = carryP if kind == "p" else carryS
        ckey = "carryP" if kind == "p" else "carryS"
        actf(ux[:, :, PV:PV + T], bk3[:, :, 0:T], AF.Copy, [bkey], [uk])
        P.op("pool", lambda e: e.tensor_copy(out=ux[:, :, 0:PV], in_=carry[:, :, j, :]), [ckey], [(uk, "c")])
        if kind == "p":
            P.op("pool", lambda e: e.tensor_copy(out=carry[:, :, j, :], in_=ux[:, :, T:T + PV]), [uk], [ckey])
        for g2, acc, ak in ((0, cg[u], "cg%d" % u), (1, cv[u], "cv%d" % u)):
            bi = g2 * NJ + j
            actf(acc[:, 0:T], bk3[:, g2, 0:T], AF.Identity, [bkey, "prmT"], [ak], bias=convb_v(bi), scale=convw_v(2, bi))

    def stC(n, kind, j):
        T, PV, sh, nsub, Tt = geo(kind)
        u = j % 2
        ux, uk = upx[u], "upx%d" % u
        for g2, acc, ak in ((0, cg[u], "cg%d" % u), (1, cv[u], "cv%d" % u)):
            bi = g2 * NJ + j
            a = acc[:, 0:T]
            stt(a, ux[:, g2, 0:T], convw_v(0, bi), a, ALU.mult, ALU.add, [uk, (uk, "c"), "prmTB", ak], [ak])
            stt(a, ux[:, g2, sh:sh + T], convw_v(1, bi), a, ALU.mult, ALU.add, [uk, (uk, "c"), "prmTB", ak], [ak])

    def stD(n, kind, j):
        T, PV, sh, nsub, Tt = geo(kind)
        u = j % 2
        actf(cg[u][:, 0:T], cg[u][:, 0:T], AF.Gelu_apprx_tanh, ["cg%d" % u], ["cg%d" % u])
        tt("pool", hT3[:, j, 0:T], cg[u][:, 0:T], cv[u][:, 0:T], ALU.mult, ["cg%d" % u, "cv%d" % u], [("hT", j)])

    XAf = XA[:, :]

    def ytail(n, kind):
        T, PV, sh, nsub, Tt = geo(kind)
        phs = [[(TA, ("T2", 0)), (TB, ("T2", 1))], [(XAf, "XA"), (XB[:, :], "XB")]]
        for kc in range(NJ):
            for sub in range(nsub):
                for cb in range(2):
                    mms([(phs[sub][cb][0][0:Tt, :], hT3[:, kc, sub * 128:sub * 128 + Tt], Wdn(kc, cb * 512, 512),
                          kc == 0, kc == NJ - 1)], [("hT", kc), ("w_dn", 0, kc)], [phs[sub][cb][1]])
        for sub in range(nsub):
            row0 = n * TBP + sub * 128 if kind == "p" else SEQ
            dst = yp[row0:row0 + Tt, :] if kind == "p" else ys[:, :]
            ph = phs[sub]
            xb_, xk, xc = getx()
            dma("sp", xb_[0:Tt, :], x1s[row0:row0 + Tt, :], [], [xk], xc)
            for cb in range(2):
                actf(xsbB[0:Tt, cb * 512:(cb + 1) * 512], ph[cb][0][0:Tt, :], AF.Square, [ph[cb][1]],
                     ["xsb", ("ssm", cb)], accum=sm(58 + cb, 1, Tt))
            tt("dve", sm(60, 1, Tt), sm(58, 1, Tt), sm(59, 1, Tt), ALU.add, [("ssm", 0), ("ssm", 1)], ["ssmt"])
            rms_rstd(sm(60, 1, Tt), sm(61, 1, Tt), sm(62, 1, Tt), D, ["ssmt"], "rm")
            for cb in range(2):
                cs_ = slice(cb * 512, (cb + 1) * 512)
                stt(ybuf[0:Tt, cs_], ph[cb][0][0:Tt, :], sm(62, 1, Tt), GP_bc[0:Tt, cs_], ALU.mult, ALU.mult,
                    [ph[cb][1], "rm", "GP_bc"], ["ybuf"])
            tt("pool", ybuf[0:Tt, :], ybuf[0:Tt, :], xb_[0:Tt, :], ALU.add, ["ybuf", xk], ["ybuf"])
            dma("sp", dst, ybuf[0:Tt, :], ["ybuf"], [("y", n, sub)], "yst")

    def pair_loop(n, kind, nxt):
        for s_ in range(NJ + 3):
            if 0 <= s_ - 3 < NJ:
                P.tag = "stD"
                stD(n, kind, s_ - 3)
            if 0 <= s_ - 2 < NJ:
                P.tag = "stC"
                stC(n, kind, s_ - 2)
            if 0 <= s_ - 1 < NJ:
                P.tag = "stB"
                stB(n, kind, s_ - 1)
            if s_ < NJ:
                P.tag = "stA"
                stA(n, kind, s_)
            if s_ == 6 and nxt is not None:
                P.tag = "S0"
                S0(*nxt)

    for n in range(NSUP):
        nxt = (n + 1, "p") if n + 1 < NSUP else (NSUP, "s")
        pair_loop(n, "p", nxt)
        P.tag = "ytail"
        ytail(n, "p")
        if n < 4:
            P.tag = "state"
            state_update(2 * n)
            state_update(2 * n + 1)
        if n == 3:
            P.tag = "cache"
            cache_prologue()
    transposes([(XB[0:88, 0:128], V(vB, "carryP")[:, 0:88], identf[:, :])], ["carryP", "identf"], ["XB"])
    cpT = V(vB, "tmu0")
    actf(cpT[0:88, 0:128], XB[0:88, 0:128], AF.Copy, ["XB"], ["tmu0"])
    for r_ in range(2):
        dma("sp", cpo[r_].rearrange("(b p) -> b p", p=128), cpT[r_ * 44:(r_ + 1) * 44, 0:128], ["tmu0"],
            [("cpo", r_)], "cpo")
    pair_loop(NSUP, "s", None)
    xnS = xn2[NSUP % 2]
    xnSk = "xnT%d" % (NSUP % 2)
    for cb in range(11):
        tb = cb % 2
        pbk = TA if tb == 0 else TB
        mms([(pbk[0:32, :], xnS[:, kc, 32:64], Wup(kc, cb * 512, 512), kc == 0, kc == 7) for kc in range(8)],
            [(xnSk, 0)] + wk("w_up", 8, cb * 512, 512, 1408), [("T2", tb)])
        actf(tmu[tb][:, :], pbk[0:32, :], AF.Copy, [("T2", tb)], ["tmu%d" % tb])
        dma("sp", cso[:, cb * 512:(cb + 1) * 512], tmu[tb][:, :], ["tmu%d" % tb], [("cso", cb)], "cso%d" % tb)
    ytail(NSUP, "s")
    P.finish()

    with nc.allow_non_contiguous_dma(reason="small parameter / state layouts"):
        P.emit()
    es.close()
    return nc, P


def _host_consts():
    ident = np.eye(128, dtype=np.float32)
    s = np.arange(128)
    maskP = (s[:, None] <= s[None, :]).astype(np.float32)
    a = np.arange(64)
    maskS = ((a[:, None] % 16 == a[None, :] % 16) & (a[:, None] // 16 <= a[None, :] // 16)).astype(np.float32)
    selS = (a[:, None] % 16 == np.arange(16)[None, :]).astype(np.float32)
    selE = (np.arange(4)[:, None] == a[None, :] // 16).astype(np.float32)
    return ident, maskP, maskS, selS, selE


_CACHE = {}


def kernel(x_prompt, x_sample, state_hgrn, cache_ffn_conv, lb_param, mix_pre_g, w_in, hgrn_norm_g,
           gmlp_ln_g, gmlp_ln_b, w_s, b_s, w_pa, w_pb, w_o, mix_post_g, ffn_pre_g, w_up, conv_w,
           conv_b, w_down, ffn_post_g):
    f = lambda a: np.ascontiguousarray(np.asarray(a, dtype=np.float32))
    if "nc" not in _CACHE:
        _CACHE["nc"] = build()[0]
    nc = _CACHE["nc"]
    ident, maskP, maskS, selS, selE = _host_consts()
    shared = {
        "lbp": f(lb_param), "gpre": f(mix_pre_g)[0], "w_in": f(w_in)[0], "ghn": f(hgrn_norm_g)[0],
        "lng": f(gmlp_ln_g)[0], "lnb": f(gmlp_ln_b)[0], "w_s": f(w_s)[0], "b_s": f(b_s)[0],
        "w_pa": f(w_pa)[0], "w_pb": f(w_pb)[0], "w_o": f(w_o)[0], "gpost": f(mix_post_g)[0],
        "gffn": f(ffn_pre_g)[0], "w_up": f(w_up)[0], "conv_w": f(conv_w)[0], "conv_b": f(conv_b)[0],
        "w_dn": f(w_down)[0], "gpost2": f(ffn_post_g)[0],
        "ident": ident, "maskP": maskP, "maskS": maskS, "selS": selS, "selE": selE,
    }
    x_prompt, x_sample = f(x_prompt), f(x_sample)
    state_hgrn, cache_ffn_conv = f(state_hgrn), f(cache_ffn_conv)
    in_maps = []
    for c in range(NCORES):
        m = dict(shared)
        m["xp"] = x_prompt[c]
        m["xs"] = np.ascontiguousarray(x_sample[c * SB:(c + 1) * SB].transpose(1, 0, 2).reshape(TS, D))
        m["st0"] = state_hgrn[0, c * SB:(c + 1) * SB]
        m["cch"] = np.ascontiguousarray(cache_ffn_conv[0, c * SB:(c + 1) * SB].transpose(1, 0, 2).reshape(32, DUP))
        in_maps.append(m)
    res = run_bass_kernel_spmd(nc, in_maps, core_ids=list(range(NCORES)))
    R = res.results
    yp = np.stack([R[c]["yp"] for c in range(NCORES)], 0)
    ys = np.concatenate([R[c]["ys"].reshape(4, SB, D).transpose(1, 0, 2) for c in range(NCORES)], 0)
    sp = np.stack([R[c]["spo"] for c in range(NCORES)], 0)[None]
    ss = np.concatenate([R[c]["sso"] for c in range(NCORES)], 0)[None]
    cp = np.stack([R[c]["cpo"] for c in range(NCORES)], 0)[None]
    cs = np.concatenate([R[c]["cso"].reshape(2, SB, DUP).transpose(1, 0, 2) for c in range(NCORES)], 0)[None]
    vs = np.concatenate([R[c]["vso"].reshape(4, SB, 512).transpose(1, 0, 2) for c in range(NCORES)], 0)[None]
    return (np.ascontiguousarray(yp, dtype=np.float32), np.ascontiguousarray(ys, dtype=np.float32),
            np.ascontiguousarray(sp, dtype=np.float32), np.ascontiguousarray(ss, dtype=np.float32),
            np.ascontiguousarray(cp, dtype=np.float32), np.ascontiguousarray(cs, dtype=np.float32),
            np.ascontiguousarray(vs, dtype=np.float32))
```

```python
import numpy as np
from contextlib import ExitStack

import concourse.bass as bass
import concourse.mybir as mybir
from concourse.bass_utils import run_bass_kernel_spmd

F32 = mybir.dt.float32
BF16 = mybir.dt.bfloat16
AF = mybir.ActivationFunctionType
ALU = mybir.AluOpType
AX = mybir.AxisListType

NCORES = 8
D = 1024
SEQ = 2048
NPT = SEQ // 128
SB = 16
TS = 64
DIN = 6144
DFF = 2816
DUP = 2 * DFF
NJ = DFF // 128
EPS = 1e-6
WELEMS = 67584


class Op:
    __slots__ = ("eng", "fn", "edges", "deps", "chan", "sig", "need", "busy", "lat", "idx", "succ", "prio",
                 "start", "nun", "tag", "aset", "xedges", "urg")


class Prog:
    ENG = ("pe", "act", "dve", "pool", "sp")
    QUANT = 0.01

    def __init__(self, nc):
        self.nc = nc
        self.regions = [[]]
        self.lastw = {}
        self.readers = {}
        self.chan_last = {}
        self.chan_cnt = {}

    def op(self, eng, fn, r=(), w=(), chan=None, busy=0.3, lat=None, after=(), aset=0):
        o = Op()
        o.eng, o.fn, o.chan, o.need, o.sig = eng, fn, chan, False, None
        o.busy = busy
        o.lat = busy if lat is None else lat
        o.tag = getattr(self, "tag", "")
        o.aset = aset
        o.urg = getattr(self, "urg", 0.0)
        edges, seen = [], set()

        def add(d, is_war):
            if d is None or d is o or id(d) in seen:
                return
            seen.add(id(d))
            edges.append((d, is_war))
        for b in r:
            add(self.lastw.get(b), False)
        for b in w:
            add(self.lastw.get(b), False)
        if chan is not None:
            add(self.chan_last.get(chan), False)
            self.chan_last[chan] = o
        for b in w:
            for d in self.readers.get(b, ()):
                add(d, True)
        for b in after:
            add(self.lastw.get(b), False)
            for d in self.readers.get(b, ()):
                add(d, True)
        o.edges = edges
        for b in r:
            self.readers.setdefault(b, []).append(o)
        for b in w:
            self.lastw[b] = o
            self.readers[b] = []
        self.regions[-1].append(o)
        return o

    def barrier(self, keep=()):
        self.regions.append([])
        kept = {k: v for k, v in self.lastw.items() if v.chan is not None and (k in keep or (isinstance(k, tuple) and k[0] in keep))}
        self.nobar = getattr(self, "nobar", set()) | set(v.chan for v in kept.values())
        self.lastw, self.readers = dict(kept), {}

    def finish(self):
        pass

    @staticmethod
    def _schedule(ops):
        n = len(ops)
        for i, o in enumerate(ops):
            o.idx, o.succ, o.start = i, [], None
        inreg = set(id(o) for o in ops)
        for o in ops:
            o.xedges = [d for (d, wr) in o.edges if id(d) not in inreg]
            o.edges = [(d, wr) for (d, wr) in o.edges if id(d) in inreg]
            o.nun = len(o.edges)
            for d, _ in o.edges:
                d.succ.append(o)
        for o in reversed(ops):
            o.prio = o.lat + max([s_.prio for s_ in o.succ], default=0.0)
        for o in ops:
            if o.chan is not None and o.aset != -1:
                o.prio = 1e9 - o.idx
        free = {e: 0.0 for e in Prog.ENG}
        rel = {e: [] for e in Prog.ENG}
        avail = {}
        for o in ops:
            if o.nun == 0:
                rel[o.eng].append(o)
                avail[id(o)] = 0.0
        order = {e: [] for e in Prog.ENG}
        done = 0
        cur_set = 0
        TL = 1.3
        QUANT = Prog.QUANT
        while done < n:
            best = None
            for e in Prog.ENG:
                lst = rel[e]
                if not lst:
                    continue
                fe = free[e]
                cand, ck = None, None
                for o in lst:
                    t = avail[id(o)]
                    pen = TL if (e == "act" and o.aset > 0 and o.aset != cur_set) else 0.0
                    st_ = max(t, fe) + pen
                    k = (int((st_ - o.urg) / QUANT), -o.prio, st_, o.idx)
                    if ck is None or k < ck:
                        cand, ck = o, k
                if best is None or ck < best[1]:
                    best = (cand, ck)
            o, k = best
            e = o.eng
            rel[e].remove(o)
            if e == "act" and o.aset > 0 and o.aset != cur_set:
                cur_set = o.aset
            o.start = k[2]
            free[e] = o.start + o.busy
            fin = o.start + o.lat
            fin_same = o.start + o.busy
            order[e].append(o)
            done += 1
            for s_ in o.succ:
                a = avail.get(id(s_), 0.0)
                f_ = fin_same if (s_.eng == e and o.chan is None and s_.chan is None) else fin
                if f_ > a:
                    avail[id(s_)] = f_
                else:
                    avail[id(s_)] = a
                s_.nun -= 1
                if s_.nun == 0:
                    rel[s_.eng].append(s_)
        return order, max(free.values())

    def emit(self):
        nc = self.nc
        streams = {e: [] for e in self.ENG}
        self.sim_us = []
        prev_last = None
        for reg in self.regions:
            order, span = self._schedule(reg)
            self.sim_us.append(span)
            if prev_last is not None:
                for e in self.ENG:
                    b = Op()
                    b.eng, b.fn, b.chan, b.need, b.sig = e, None, None, False, None
                    b.xedges = None
                    b.urg = 0.0
                    b.edges = [(d, False) for d in prev_last if not (d.chan is None and d.eng == e)]
                    streams[e].append(b)
            for e in self.ENG:
                streams[e].extend(order[e])
            prev_last = [order[e][-1] for e in ("pe", "act", "dve", "pool") if order[e]]
            chl = {}
            for e in self.ENG:
                for o in order[e]:
                    if o.chan is not None:
                        chl[o.chan] = o
            prev_last += [v for c_, v in chl.items() if c_ not in getattr(self, "nobar", set())]
        fin = Op()
        fin.eng, fin.fn, fin.chan, fin.need, fin.sig = "sp", None, None, False, None
        chl = {}
        for e in self.ENG:
            for o in streams[e]:
                if o.chan is not None:
                    chl[o.chan] = o
        fin.edges = [(d, False) for d in chl.values()]
        fin.xedges = None
        fin.urg = 0.0
        streams["sp"].append(fin)
        for e in self.ENG:
            for i, o in enumerate(streams[e]):
                o.idx = i
        for e in self.ENG:
            for o in streams[e]:
                deps, latest = [], {}
                for d, is_war in o.edges:
                    if d.chan is None and o.chan is None and d.eng == o.eng:
                        if o.eng in ("pe", "sp"):
                            continue
                    if d.chan is None:
                        if d.eng not in latest or latest[d.eng].idx < d.idx:
                            latest[d.eng] = d
                    else:
                        deps.append(d)
                deps.extend(getattr(o, "xedges", None) or [])
                deps.extend(latest.values())
                for d in deps:
                    d.need = True
                o.deps = deps
        with ExitStack() as es:
            sems = {}
            for e in ("pe", "act", "dve", "pool"):
                sems[e] = es.enter_context(nc.semaphore("s_" + e))
            for c in self.chan_cnt_keys():
                sems[("ch", c)] = es.enter_context(nc.semaphore("c_" + str(c)))
            cnt = {e: 0 for e in self.ENG}
            chc = {}
            for e in self.ENG:
                for o in streams[e]:
                    if o.chan is not None:
                        chc[o.chan] = chc.get(o.chan, 0) + 16
                        o.sig = (("ch", o.chan), chc[o.chan], 16)
                    elif o.need:
                        assert o.fn is not None
                        cnt[e] += 1
                        o.sig = (e, cnt[e], 1)
            self.counts = dict(cnt)
            block = es.enter_context(nc.Block())

            def mk(e):
                def body(eng):
                    waited = {}
                    for o in streams[e]:
                        for d in o.deps:
                            k, v, _ = d.sig
                            if waited.get(k, 0) >= v:
                                continue
                            eng.wait_ge(sems[k], v)
                            waited[k] = v
                        if o.fn is None:
                            continue
                        ins = o.fn(eng)
                        if o.sig is not None:
                            ins.then_inc(sems[o.sig[0]], o.sig[2])
                return body

            block.tensor(mk("pe"))
            block.scalar(mk("act"))
            block.vector(mk("dve"))
            block.gpsimd(mk("pool"))
            block.sync(mk("sp"))
        self.ops = streams

    def chan_cnt_keys(self):
        return list(self.chan_last.keys())


def _bc(ap, axis, n):
    a = ap.unsqueeze(axis)
    shp = list(a.shape)
    shp[axis] = n
    return a.to_broadcast(shp)


def build():
    nc = bass.Bass("TRN2", target_bir_lowering=False)
    P = Prog(nc)
    es = ExitStack()

    def din(name, shape):
        return nc.dram_tensor(name, list(shape), F32, kind="ExternalInput").ap()

    def dout(name, shape):
        return nc.dram_tensor(name, list(shape), F32, kind="ExternalOutput").ap()

    xp = din("xp", (SEQ, D))
    xs = din("xs", (TS, D))
    st0 = din("st0", (SB, 8, 128, 64))
    cch = din("cch", (32, DUP))
    lbp = din("lbp", (2, D))
    gpre = din("gpre", (D,))
    w_in = din("w_in", (D, DIN))
    ghn = din("ghn", (64,))
    lng = din("lng", (512,))
    lnb = din("lnb", (512,))
    w_s = din("w_s", (4, 128, 128))
    b_s = din("b_s", (4, 128))
    w_pa = din("w_pa", (512, D))
    w_pb = din("w_pb", (512, D))
    w_o = din("w_o", (D, D))
    gpost = din("gpost", (D,))
    gffn = din("gffn", (D,))
    w_up = din("w_up", (D, DUP))
    conv_w = din("conv_w", (3, DUP))
    conv_b = din("conv_b", (DUP,))
    w_dn = din("w_dn", (DFF, D))
    gpost2 = din("gpost2", (D,))
    ident_d = din("ident", (128, 128))
    maskP_d = din("maskP", (128, 128))
    maskS_d = din("maskS", (64, 64))
    selS_d = din("selS", (64, 16))
    selE_d = din("selE", (4, 64))

    yp = dout("yp", (SEQ, D))
    ys = dout("ys", (TS, D))
    spo = dout("spo", (8, 128, 64))
    sso = dout("sso", (SB, 8, 128, 64))
    cpo = dout("cpo", (2, DUP))
    cso = dout("cso", (32, DUP))
    vso = dout("vso", (TS, 512))
    x1s = nc.dram_tensor("x1s", [SEQ + TS, D], F32, kind="Internal").ap()
    wupb = nc.dram_tensor("wupb", [D, DUP], BF16, kind="Internal").ap()
    wdnb = nc.dram_tensor("wdnb", [DFF, D], BF16, kind="Internal").ap()
    sK_d = nc.dram_tensor("sK_d", [TS, D], BF16, kind="Internal").ap()
    sI_d = nc.dram_tensor("sI_d", [TS, 512], BF16, kind="Internal").ap()
    sG_d = nc.dram_tensor("sG_d", [128, 128], F32, kind="Internal").ap()

    def sb(name, shape, dt=F32):
        return es.enter_context(nc.sbuf_tensor("sb_" + name, list(shape), dt))

    Wh = sb("Wbig", (128, WELEMS), BF16)

    def Wv(off, kstride, kc, c0, n):
        return Wh[:, off + kc * kstride + c0: off + kc * kstride + c0 + n]

    OFF_PA, OFF_PB, OFF_O = 49152, 53248, 57344
    OFF_DN = 8 * DUP

    def Win(kc, c0, n): return Wv(0, DIN, kc, c0, n)
    def Wpa(kc, c0, n): return Wv(OFF_PA, D, kc, c0, n)
    def Wpb(kc, c0, n): return Wv(OFF_PB, D, kc, c0, n)
    def Wo(kc, c0, n): return Wv(OFF_O, D, kc, c0, n)
    def Wup(kc, c0, n): return Wv(0, DUP, kc, c0, n)
    def Wdn(kc, c0, n): return Wv(OFF_DN, D, kc, c0, n)

    identb = sb("identb", (128, 128), BF16)
    identf = sb("identf", (128, 128))
    maskP = sb("maskP", (128, 128))
    maskS = sb("maskS", (64, 64))
    selS = sb("selS", (64, 16))
    wTp = sb("wTp", (128, 4, 128), BF16)
    wTs = sb("wTs", (64, 4, 64), BF16)
    bsP = sb("bsP", (1, 4, 128))
    bsS = sb("bsS", (1, 4, 64))
    onesf = sb("onesf", (1, 128))

    ones2 = sb("ones2", (128, 128))
    epsc = sb("epsc", (128, 1))
    prmT = sb("prmT", (128, 120))
    prmTB = sb("prmTB", (128, 88))
    prm2 = sb("prm2", (128, 3, 8))
    selE = sb("selE", (4, 64))
    bs4 = sb("bs4", (1, 4, 4))
    w4 = sb("w4", (4, 4, 4))
    m1 = sb("m1", (4, 64))
    P0T, P1T, GPRET, GFFNT, LBT, OMLT, NOMLT = range(7)
    ghn_bc = sb("ghn_bc", (128, 64))
    lng_bc = sb("lng_bc", (128, 512))
    lnb_bc = sb("lnb_bc", (128, 512))
    GP_bc = sb("GP_bc", (128, D))
    small = sb("small", (128, 96))

    def prmv(i):
        if i < 4:
            return prmT[:, 8 * i:8 * i + 8]
        return prm2[:, i - 4, :]

    def convb_v(bi):
        return prmT[:, 32 + bi:33 + bi]

    def convw_v(tap, bi):
        if tap == 2:
            return prmT[:, 76 + bi:77 + bi]
        return prmTB[:, tap * 44 + bi:tap * 44 + bi + 1]

    layA = [("xt0", 4096), ("xsb", 2048), ("xnT0", 2048), ("xnT1", 2048), ("SG", 4096), ("F", 4096), ("B", 4096),
            ("GA", 4096), ("GB", 4096), ("QT", 2048), ("KT", 2048), ("Ktok", 2048), ("ibf", 1024),
            ("sog", 2048), ("gv", 2048), ("vnb", 1024), ("guT", 2048), ("scT", 2048), ("R1", 2048),
            ("osb", 2048), ("oab", 1024), ("oaT", 1024), ("obT", 1024), ("hTb", 2048), ("S", 2048), ("Spb", 1024)]
    layB = [("x1t0", 4096), ("x1t1", 4096), ("x1t2", 4096), ("xsb", 2048), ("xnT0", 4096), ("xnT1", 4096),
            ("upx0", 2112), ("upx1", 2112), ("cg0", 1024), ("cg1", 1024), ("cv0", 1024), ("cv1", 1024),
            ("hT", 11264), ("ybuf", 4096), ("tmu0", 2048), ("tmu1", 2048), ("carryP", 384), ("carryS", 5632),
            ("sibf", 1024)]
    szA = sum(s for _, s in layA)
    szB = sum(s for _, s in layB)
    ARENA = max(szA, szB)
    arena = sb("arena", (128, ARENA // 4))

    def mkviews(lay):
        d, off = {}, 0
        for n, s in lay:
            d[n] = (off // 4, s // 4)
            off += s
        return d
    vA, vB = mkviews(layA), mkviews(layB)

    def V(views, name, dt=F32, shape3=None, parts=128):
        o, n = views[name]
        a = arena[0:parts, o:o + n]
        if dt == BF16:
            a = a.bitcast(BF16)
        if shape3 is not None:
            a = a.rearrange("p (a b) -> p a b", b=shape3)
        return a

    def ps(name, n):
        return es.enter_context(nc.psum_tensor("ps_" + name, [128, n], F32))
    FA, FB, T2 = ps("FA", 1024), ps("FB", 1024), ps("T2", 1024)
    XA, XB = ps("XA", 512), ps("XB", 512)
    FA3 = FA[:, :].rearrange("p (a b) -> p a b", b=128)
    FB3 = FB[:, :].rearrange("p (a b) -> p a b", b=128)
    XB3 = XB[:, :].rearrange("p (a b) -> p a b", b=128)
    XAb = XA[:, :].bitcast(BF16)
    XAb3 = XAb.rearrange("p (a b) -> p a b", b=128)
    TA, TB = T2[:, 0:512], T2[:, 512:1024]

    def sm(c, n=1, parts=128):
        return small[0:parts, c:c + n]

    def fsz(ap):
        n = 1
        for d in ap.shape[1:]:
            n *= int(d)
        return n

    def dma(eng, out, in_, r, w, chan, after=(), bpe=4):
        nbytes = fsz(out) * int(out.shape[0]) * bpe
        busy = 0.15 if eng == "sp" else max(1.0, nbytes / 300e3)
        return P.op(eng, lambda e: e.dma_start(out=out, in_=in_), r, w, chan, busy=busy,
                    lat=2.2 + nbytes / 150e3, after=after)

    def actf(out, in_, func, r, w, bias=None, scale=None, accum=None):
        kw = {}
        if bias is not None:
            kw["bias"] = bias
        if scale is not None:
            kw["scale"] = scale
        if accum is not None:
            kw["accum_out"] = accum
        aset = {AF.Exp: 6, AF.Ln: 6, AF.Sigmoid: 2, AF.Silu: 18, AF.Gelu_apprx_tanh: 11}.get(func, 0)
        return P.op("act", lambda e: e.activation(out=out, in_=in_, func=func, **kw), r, w,
                    busy=0.24 + fsz(out) * 0.00078 + (0.1 if accum is not None else 0.0), aset=aset)

    def tt(eng, out, in0, in1, op, r, w):
        b = 0.12 + fsz(out) * 0.0011 if eng == "dve" else 0.2 + fsz(out) * 0.0026
        return P.op(eng, lambda e: e.tensor_tensor(out=out, in0=in0, in1=in1, op=op), r, w, busy=b)

    def ts(eng, out, in0, s1, s2, op0, op1, r, w):
        b = 0.12 + fsz(out) * 0.0008 if eng == "dve" else 0.2 + fsz(out) * 0.0026
        if s2 is None:
            return P.op(eng, lambda e: e.tensor_scalar(out=out, in0=in0, scalar1=s1, scalar2=None, op0=op0), r, w,
                        busy=b)
        return P.op(eng, lambda e: e.tensor_scalar(out=out, in0=in0, scalar1=s1, scalar2=s2, op0=op0, op1=op1), r, w,
                    busy=b)

    def stt(out, in0, scalar, in1, op0, op1, r, w):
        return P.op("dve", lambda e: e.scalar_tensor_tensor(out=out, in0=in0, scalar=scalar, in1=in1,
                                                            op0=op0, op1=op1), r, w,
                    busy=0.16 + fsz(out) * 0.0015)

    def dvop(fn, r, w, n, k=0.0011):
        return P.op("dve", fn, r, w, busy=0.12 + n * k)

    def mms(lst, r, w, after=()):
        def fn(e):
            ins = None
            for (o_, l_, r_, st, sp_) in lst:
                ins = e.matmul(o_, l_, r_, start=st, stop=sp_)
            return ins
        b = sum(0.035 + max(fsz(o_), 96) * 0.00045 for (o_, l_, r_, st, sp_) in lst)
        return P.op("pe", fn, r, w, busy=b, lat=b + 0.25, after=after)

    def transposes(lst, r, w, after=()):
        def fn(e):
            ins = None
            for (o_, i_, id_) in lst:
                ins = e.transpose(o_, i_, id_)
            return ins
        b = 0.1 * len(lst)
        return P.op("pe", fn, r, w, busy=b, lat=b + 0.25, after=after)

    wch = [0]

    wspans = []

    def wdma(out, in_, key, span=None, after=(), own_chan=False, r=(), bpe=4):
        c = ("wo%d" % wch[0]) if own_chan else ("w%d" % (wch[0] % 8))
        wch[0] += 1
        if span is not None:
            wspans.append((key, span[0], span[1]))
        o_ = dma("pool", out, in_, list(r), [key], c, after=after, bpe=bpe)
        if own_chan:
            o_.aset = -1
        return o_

    pch = [0]

    plate = [0.0]

    def pdma(out, in_, key):
        c = "p%d" % (pch[0] % 14)
        pch[0] += 1
        return dma("sp", out, in_, [], [key] if not isinstance(key, list) else key, c)

    wdma(identb[:, :], ident_d[:, :], "identb")
    dma("sp", arena[:, vA["xt0"][0]:vA["xt0"][0] + 1024], xp[0:128, :], [], ["xt0"], "ld0")
    pdma(identf[:, :], ident_d[:, :], "identf")
    PAr = arena[:, vA["B"][0]:vA["B"][0] + 128]
    PBr = arena[:, vA["SG"][0]:vA["SG"][0] + 128]
    KB8 = [("B", h_) for h_ in range(8)]
    KS8 = [("SG", h_) for h_ in range(8)]
    for i_, (r0, vec) in enumerate(((0, lbp[0]), (8, lbp[1]), (16, gpre), (24, gffn))):
        pdma(PAr[r0:r0 + 8, :], vec.rearrange("(h k) -> h k", k=128), ("B", i_))
    pdma(PAr[32:76, :], conv_b.rearrange("(b p) -> b p", p=128), ("B", 4))
    pdma(PAr[76:120, :], conv_w[2].rearrange("(b p) -> b p", p=128), ("B", 5))
    pdma(PBr[0:44, :], conv_w[0].rearrange("(b p) -> b p", p=128), ("SG", 0))
    pdma(PBr[44:88, :], conv_w[1].rearrange("(b p) -> b p", p=128), ("SG", 1))

    pdma(maskP[:, :], maskP_d[:, :], "maskP")
    pdma(maskS[:, :], maskS_d[:, :], "maskS")
    pdma(selS[:, :], selS_d[:, :], "selS")
    pdma(selE[:, :], selE_d[:, :], "selE")
    pdma(ghn_bc[:, :], ghn.partition_broadcast(128), "ghn_bc")
    pdma(lng_bc[:, :], lng.partition_broadcast(128), "lng_bc")
    pdma(lnb_bc[:, :], lnb.partition_broadcast(128), "lnb_bc")
    pdma(GP_bc[:, :], gpost.partition_broadcast(128), "GP_bc")
    pdma(bsP[0:1, :, :], b_s.unsqueeze(0), "bsP")
    pdma(bs4[0:1, :, :], b_s[:, 0:4].unsqueeze(0), "bs4")
    pdma(w4[:, :, :], w_s[:, 0:4, 0:4].rearrange("g b a -> b g a"), "w4")
    transposes([(XB[:, 0:120], PAr[0:120, :], identf[0:120, 0:120])], KB8 + ["identf"], ["XB"])
    actf(prmT[:, :], XB[:, 0:120], AF.Copy, ["XB"], ["prmT"])
    transposes([(XB[:, 0:88], PBr[0:88, :], identf[0:88, 0:88])], KS8 + ["identf"], ["XB"])
    actf(prmTB[:, :], XB[:, 0:88], AF.Copy, ["XB"], ["prmTB"])
    P.op("dve", lambda e: e.memset(onesf[:, :], 1.0), [], ["onesf"])
    P.op("dve", lambda e: e.memset(ones2[:, :], 1.0), [], ["ones2"])
    P.op("dve", lambda e: e.memset(epsc[:, :], EPS), [], ["epsc"])
    tt("dve", prmv(LBT), prmv(P0T), prmv(P1T), ALU.subtract, ["prmT"], ["lbT"])
    dvop(lambda e: e.tensor_copy(out=bsS[0:1, :, :].rearrange("p g (j s) -> p (g j) s", s=16),
                                 in_=_bc(bs4[0:1, :, :].rearrange("p g j -> p (g j)"), 2, 16)), ["bs4"], ["bsS"], 256)
    actf(prmv(LBT), prmv(LBT), AF.Sigmoid, ["lbT"], ["lbT"])
    ts("dve", prmv(OMLT), prmv(LBT), -1.0, 1.0, ALU.mult, ALU.add, ["lbT"], ["omlT"])
    ts("dve", prmv(NOMLT), prmv(OMLT), -1.0, None, ALU.mult, None, ["omlT"], ["nomlT"])

    wTs_keys = [("wTs", g) for g in range(4)]

    def load_w(dst_fn, src, K, ncols, cw, name, order=None, off=0, kstride=0, overlay=False, own_chan=False,
               late=None, rkey=None, bpe=4):
        ncg = ncols // cw
        todo = [(cg, kc) for cg in (order if order is not None else range(ncg)) for kc in range(K)]
        if overlay and late is not None:
            def is_late(cg, kc):
                s0 = off + kc * kstride + cg * cw
                return any(a_ < s0 + cw and s0 < b_ and late(k_) for (k_, a_, b_) in wspans)
            todo = [x for x in todo if not is_late(*x)] + [x for x in todo if is_late(*x)]
        for cg, kc in todo:
            if True:
                s0 = off + kc * kstride + cg * cw
                aft = [k_ for (k_, a_, b_) in wspans if a_ < s0 + cw and s0 < b_] if overlay else ()
                wdma(dst_fn(kc, cg * cw, cw), src[kc * 128:(kc + 1) * 128, cg * cw:(cg + 1) * cw], (name, cg, kc),
                     span=None if overlay else (s0, s0 + cw), after=aft, own_chan=own_chan,
                     r=[rkey(kc)] if rkey is not None else (), bpe=bpe)

    load_w(Win, w_in, 8, DIN, 1024, "w_in", order=[1, 0, 2, 3, 4, 5], off=0, kstride=DIN)
    load_w(Wpa, w_pa, 4, D, 1024, "w_pa", off=OFF_PA, kstride=D)
    load_w(Wpb, w_pb, 4, D, 1024, "w_pb", off=OFF_PB, kstride=D)
    load_w(Wo, w_o, 8, D, 1024, "w_o", off=OFF_O, kstride=D)
    wspans.append(("xt1", 65536, 67584))
    for kc in range(8):
        wdma(wupb[kc * 128:(kc + 1) * 128, :], w_up[kc * 128:(kc + 1) * 128, :], ("wupb", kc))
    for i_ in range(11):
        wdma(wdnb[i_ * 256:(i_ + 1) * 256, :], w_dn[i_ * 256:(i_ + 1) * 256, :], ("wdnb", i_))

    def wk(name, K, c0, n, cw=1024):
        return [(name, cg, kc) for cg in range(c0 // cw, (c0 + n - 1) // cw + 1) for kc in range(K)]

    xt = [V(vA, "xt0"), Wh[:, 65536:67584].bitcast(F32)]
    xsb = V(vA, "xsb", BF16)
    xnT = [V(vA, "xnT0", BF16, 128), V(vA, "xnT1", BF16, 128)]
    SG3, F3, B3, GA3, GB3 = (V(vA, n, F32, 128) for n in ("SG", "F", "B", "GA", "GB"))
    GAt, GBt = V(vA, "GA"), V(vA, "GB")
    QT3, KT3 = V(vA, "QT", BF16, 128), V(vA, "KT", BF16, 128)
    Ktok = V(vA, "Ktok", BF16)
    ibf = V(vA, "ibf", BF16)
    sog = V(vA, "sog")
    gv = V(vA, "gv")
    vnb = V(vA, "vnb", BF16)
    guT3 = V(vA, "guT", F32, 128)
    scT3 = V(vA, "scT", BF16, 128)
    osb = V(vA, "osb")
    oab = V(vA, "oab", BF16)
    oaT3 = V(vA, "oaT", BF16, 128)
    obT3 = V(vA, "obT", BF16, 128)
    hTb3 = V(vA, "hTb", BF16, 128)
    R1 = V(vA, "R1")
    S_ = V(vA, "S")
    S3 = V(vA, "S", F32, 64)
    Spb = V(vA, "Spb", BF16)
    junkA = arena[:, vA["oab"][0]:vA["oab"][0] + 512].bitcast(BF16)
    XAb = XA[:, :].bitcast(BF16)
    XAb3 = XAb.rearrange("p (a b) -> p a b", b=128)
    XBb = XB[:, :].bitcast(BF16)
    TB3 = TB.rearrange("p (a b) -> p a b", b=128)

    def K8(n):
        return [(n, h) for h in range(8)]

    P.op("dve", lambda e: e.memset(S_, 0.0), [], ["S"])
    P.op("dve", lambda e: e.memset(scT3[64:128, :, 0:64], 0.0), [], ["scT"])

    def rms_rstd(ss_ap, ln_ap, out_ap, dim, rkeys, wkey):
        actf(ln_ap, ss_ap, AF.Ln, rkeys + ["epsc"], [wkey + "_ln"], bias=epsc[0:ss_ap.shape[0], :], scale=1.0 / dim)
        actf(out_ap, ln_ap, AF.Exp, [wkey + "_ln"], [wkey], scale=-0.5)

    def xload(ti, kind, which):
        T = 128 if kind == "p" else TS
        src = xp[ti * 128:(ti + 1) * 128, :] if kind == "p" else xs[:, :]
        dma("sp", xt[which][0:T, :], src, [], ["xt%d" % which], "ld%d" % which)

    def make_tile(ti, kind):
        T = 128 if kind == "p" else TS
        par = ti % 2 if kind == "p" else 0
        xtb, xtk = xt[0], "xt0"
        xrb, xrk = xt[1], "xt1"
        xn, xnk = xnT[par], "xnT%d" % par
        row0 = ti * 128 if kind == "p" else SEQ
        C = {}

        def fm(ps3, c0, nb, pkey):
            for b in range(nb):
                mms([(ps3[:, b, 0:T], Win(kc, c0 + b * 128, 128), xn[:, kc, 0:T], kc == 0, kc == 7)
                     for kc in range(8)], [xnk] + wk("w_in", 8, c0 + b * 128, 128), [pkey])

        def tm(psv, c0, pkey):
            mms([(psv[0:T, :], xn[:, kc, 0:T], Win(kc, c0, 512), kc == 0, kc == 7) for kc in range(8)],
                [xnk] + wk("w_in", 8, c0, 512), [pkey])

        def c_F1():
            actf(xsb[0:T, :], xtb[0:T, :], AF.Square, [xtk], ["xsb", "ss"], accum=sm(0, 1, T))
            rms_rstd(sm(0, 1, T), sm(1, 1, T), sm(2, 1, T), D, ["ss"], "rstd")
            actf(xsb[0:T, :], xtb[0:T, :], AF.Identity, [xtk, "rstd"], ["xsb"], scale=sm(2, 1, T))
            transposes([(XAb3[:, kc, 0:T], xsb[0:T, kc * 128:(kc + 1) * 128], identb[0:T, 0:T]) for kc in range(8)],
                       ["xsb", "identb"], ["XA"])
            tt("dve", xn[:, :, 0:T], XAb3[:, :, 0:T], _bc(prmv(GPRET), 2, T), ALU.mult, ["XA", "prmT"], [xnk])

        def c_F2():
            fm(FA3, 1024, 8, "FA")
            fm(FB3, 0, 8, "FB")
            actf(SG3[:, :, 0:T], FA3[:, :, 0:T], AF.Sigmoid, ["FA"], K8("SG"))
            for h in range(8):
                actf(F3[:, h, 0:T], SG3[:, h, 0:T], AF.Ln, [("SG", h), "omlT", "lbT"], [("F", h)],
                     scale=prmv(OMLT)[:, h:h + 1], bias=prmv(LBT)[:, h:h + 1])
            for h in range(8):
                actf(SG3[:, h, 0:T], SG3[:, h, 0:T], AF.Identity, [("SG", h), "omlT", "nomlT"], [("SG", h)],
                     scale=prmv(NOMLT)[:, h:h + 1], bias=prmv(OMLT)[:, h:h + 1])

        def c_F4():
            if kind == "p":
                for h in range(8):
                    dvop(lambda e, h=h: e.tensor_tensor_scan(out=B3[:, h, 0:T], data0=ones2[:, 0:T],
                                                                    data1=F3[:, h, 0:T], initial=0.0,
                                                                    op0=ALU.mult, op1=ALU.add),
                         [("F", h), "ones2"], [("B", h)], T, 0.0022)
                dvop(lambda e: e.tensor_copy(out=small[:, 8:16], in_=B3[:, :, 63]), K8("B"), ["b63"], 8)
                ts("dve", small[:, 16:24], B3[:, :, 63], -1.0, None, ALU.mult, None, K8("B"), ["nb63"])
                actf(small[:, 24:32], small[:, 8:16], AF.Exp, ["b63"], ["eb63"])
                for h in range(8):
                    actf(F3[:, h, 0:T], B3[:, h, 0:T], AF.Exp, [("B", h), "nb63"], [("F", h)],
                         bias=small[:, 16 + h:17 + h], scale=1.0)
                    actf(B3[:, h, 0:T], B3[:, h, 0:T], AF.Exp, [("B", h), "b63"], [("B", h)],
                         bias=small[:, 8 + h:9 + h], scale=-1.0)
            else:
                dvop(lambda e: e.tensor_copy(out=B3[:, :, 0:16], in_=F3[:, :, 0:16]), K8("F"), K8("B"), 128)
                for j in range(1, 4):
                    tt("dve", B3[:, :, 16 * j:16 * j + 16], B3[:, :, 16 * j - 16:16 * j],
                       F3[:, :, 16 * j:16 * j + 16], ALU.add, K8("B") + K8("F"), K8("B"))
                actf(F3[:, :, 0:T], B3[:, :, 0:T], AF.Exp, K8("B"), K8("F"))
                actf(B3[:, :, 0:T], B3[:, :, 0:T], AF.Exp, K8("B"), K8("B"), scale=-1.0)
            tt("dve", KT3[:, :, 0:T], SG3[:, :, 0:T], B3[:, :, 0:T], ALU.mult, K8("SG") + K8("B"), ["KT"])

        def c_F4b():
            actf(B3[:, :, 0:T], FB3[:, :, 0:T], AF.Silu, ["FB"], K8("B"))
            tt("dve", QT3[:, :, 0:T], B3[:, :, 0:T], F3[:, :, 0:T], ALU.mult, K8("B") + K8("F"), ["QT"])
            if kind == "p":
                dvop(lambda e: e.tensor_copy(out=small[:, 32:40], in_=F3[:, :, T - 1]), K8("F"), ["glast"], 8)
            else:
                glS = V(vA, "S", F32, 16)
                dvop(lambda e: e.tensor_copy(out=glS[:, 0:8, :], in_=F3[:, :, 48:64]), K8("F"), ["S"], 128)

        def c_F3():
            tm(XB, 2048, "XB")
            actf(ibf[0:T, :], XB[0:T, :], AF.Copy, ["XB"], ["ibf"])
            tm(TA, 2560, "TA")
            actf(sog[0:T, :], TA[0:T, :], AF.Silu, ["TA"], ["sog"])
            s3 = sog[0:T, :].rearrange("p (h d) -> p h d", d=64)
            tt("dve", s3, s3, _bc(ghn_bc[0:T, :], 1, 8), ALU.mult, ["sog", "ghn_bc"], ["sog"])

        def c_F5():
            tm(XA, 3584, "XA")
            actf(gv[0:T, :], XA[0:T, :], AF.Gelu_apprx_tanh, ["XA"], ["gv"])
            dvop(lambda e: e.bn_stats(out=small[0:T, 40:46], in_=gv[0:T, :]), ["gv"], ["bst"], 512)
            dvop(lambda e: e.bn_aggr(out=small[0:T, 46:48], in_=small[0:T, 40:46]), ["bst"], ["mv"], 8)
            rms_rstd(small[0:T, 47:48], sm(48, 1, T), sm(49, 1, T), 1.0, ["mv"], "rsv")
            stt(sm(63, 1, T), small[0:T, 46:47], -1.0, sm(49, 1, T), ALU.mult, ALU.mult, ["mv", "rsv"], ["nmr"])
            actf(gv[0:T, :], gv[0:T, :], AF.Identity, ["gv", "rsv", "nmr"], ["gv"], scale=sm(49, 1, T),
                 bias=sm(63, 1, T))
            tt("dve", gv[0:T, :], gv[0:T, :], lng_bc[0:T, :], ALU.mult, ["gv", "lng_bc"], ["gv"])
            if kind == "p":
                tt("dve", vnb[0:T, :], gv[0:T, :], lnb_bc[0:T, :], ALU.add, ["gv", "lnb_bc"], ["vnb"])
            else:
                tt("dve", gv[0:T, :], gv[0:T, :], lnb_bc[0:T, :], ALU.add, ["gv", "lnb_bc"], ["gv"])
                dma("sp", vso[:, :], gv[0:T, :], ["gv"], ["vso"], "vs")
                actf(vnb[0:T, :], gv[0:T, :], AF.Copy, ["gv"], ["vnb"])
            fm(XB3, 3072, 4, "XB")
            actf(guT3[:, 0:4, 0:T], XB3[:, 0:4, 0:T], AF.Gelu_apprx_tanh, ["XB"], ["guT"])

        def gates():
            for cb in range(2):
                tm(FA[:, cb * 512:(cb + 1) * 512], 4096 + cb * 512, "FA")
            actf(GAt[0:T, :], FA[0:T, :], AF.Sigmoid, ["FA"], ["GA"])
            for cb in range(2):
                tm(FB[:, cb * 512:(cb + 1) * 512], 5120 + cb * 512, "FB")
            actf(GBt[0:T, :], FB[0:T, :], AF.Sigmoid, ["FB"], ["GB"])

        def c_B1():
            if kind == "s":
                gates()
            transposes([(XBb[0:T, h * 128:(h + 1) * 128], KT3[:, h, 0:T], identb[:, :]) for h in range(8)],
                       ["KT", "identb"], ["XB"])
            actf(Ktok[0:T, :], XBb[0:T, :], AF.Copy, ["XB"], ["Ktok"])

        def c_B2():
            if kind == "p":
                mms([x for h in range(8) for x in
                     ((FA3[0:128, h, 64:128], KT3[:, h, 0:128], QT3[:, h, 64:128], True, True),
                      (FA3[0:64, h, 0:64], KT3[:, h, 0:64], QT3[:, h, 0:64], True, True))],
                    ["KT", "QT"], ["FA"])
                P.urg = URG
                for hb in range(2):
                    hs = slice(4 * hb, 4 * hb + 4)
                    tt("dve", scT3[:, hs, 64:128], FA3[:, hs, 64:128], _bc(maskP[:, 64:128], 1, 4), ALU.mult,
                       ["FA", "maskP"], ["scT"])
                    tt("dve", scT3[0:64, hs, 0:64], FA3[0:64, hs, 0:64], _bc(maskP[0:64, 0:64], 1, 4), ALU.mult,
                       ["FA", "maskP"], ["scT"])
                tt("dve", S3, S3, _bc(small[:, 24:32], 2, 64), ALU.mult, ["S", "eb63"], ["S"])
                actf(Spb, S_, AF.Copy, ["S"], ["Spb"])
                P.urg = 0.0
            else:
                mms([(FA3[0:T, h, 0:T], KT3[:, h, 0:T], QT3[:, h, 0:T], True, True) for h in range(8)],
                    ["KT", "QT"], ["FA"])
                for hb in range(2):
                    hs = slice(4 * hb, 4 * hb + 4)
                    tt("dve", scT3[0:T, hs, 0:T], FA3[0:T, hs, 0:T], _bc(maskS[:, :], 1, 4), ALU.mult,
                       ["FA", "maskS"], ["scT"])

        def c_B3():
            if kind == "p":
                mms([x for h in range(8) for x in
                     ((TA[0:T, h * 64:(h + 1) * 64], scT3[0:T, h, 0:T], ibf[0:T, h * 64:(h + 1) * 64], True, False),
                      (TA[0:T, h * 64:(h + 1) * 64], QT3[:, h, 0:T], Spb[:, h * 64:(h + 1) * 64], False, True))],
                    ["scT", "ibf", "QT", "Spb"], ["TA"])
                actf(osb[0:T, :], TA[0:T, :], AF.Copy, ["TA"], ["osb"])
                mms([(TB[:, h * 64:(h + 1) * 64], Ktok[0:T, h * 128:(h + 1) * 128], ibf[0:T, h * 64:(h + 1) * 64],
                      True, True) for h in range(8)], ["Ktok", "ibf"], ["TB"])
                tt("dve", S_, TB[:, :], S_, ALU.add, ["TB", "S"], ["S"])
                tt("dve", S3, S3, _bc(small[:, 32:40], 2, 64), ALU.mult, ["S", "glast"], ["S"])
                if ti == NPT - 1:
                    dma("sp", spo.rearrange("h k d -> k h d"), S3, ["S"], ["spo"], "spo")
            else:
                mms([(TA[0:T, h * 64:(h + 1) * 64], scT3[0:T, h, 0:T], ibf[0:T, h * 64:(h + 1) * 64], True, True)
                     for h in range(8)], ["scT", "ibf"], ["TA"])
                glS = V(vA, "S", F32, 16)
                def half(name, i_):
                    o_ = vA[name][0] + 512 * i_
                    return arena[:, o_:o_ + 512].bitcast(BF16)
                S0b = [half("SG", 0), half("SG", 1), half("B", 0), half("B", 1), half("xt0", 0), half("xt0", 1),
                       V(vA, "KT", BF16), V(vA, "gv", BF16)]
                S0bk = [[("SG", h_) for h_ in range(4)], [("SG", h_) for h_ in range(4, 8)],
                        [("B", h_) for h_ in range(4)], [("B", h_) for h_ in range(4, 8)], ["xt0"], ["xt0"],
                        ["KT"], ["gv"]]
                NSB = len(S0b)

                def sload(h_):
                    dma("pool", S0b[h_ % NSB].rearrange("p (s d) -> p s d", d=64),
                        st0[:, h_, :, :].rearrange("s k d -> k s d"), [], S0bk[h_ % NSB], "sl%d" % (h_ % NSB))
                seltmp = arena[0:64, vA["scT"][0]:vA["scT"][0] + 1024]
                selk = ["scT", "R1"]
                dma("sp", sK_d[:, :], Ktok[0:T, :], ["Ktok"], ["sK_d"], "sv0")
                dma("sp", sI_d[:, :], ibf[0:T, :], ["ibf"], ["sI_d"], "sv1")
                dma("sp", sG_d[:, :], V(vA, "S")[:, 0:128], ["S"], ["sG_d"], "sv2")
                for h in range(NSB):
                    sload(h)
                for h in range(8):
                    b = h % NSB
                    PB_, pbk_ = (FA, "FA") if h % 2 == 0 else (FB, "FB")
                    mms([(PB_[0:T, 0:512], QT3[:, h, 0:T], S0b[b][:, 0:512], True, True),
                         (PB_[0:T, 512:1024], QT3[:, h, 0:T], S0b[b][:, 512:1024], True, True)],
                        ["QT"] + S0bk[b], [pbk_])
                    if h + NSB < 8:
                        sload(h + NSB)
                    for half in range(2):
                        cs_ = slice(512 * half, 512 * half + 512)
                        tt("dve", seltmp[:, cs_].rearrange("p (s d) -> p s d", d=64),
                           PB_[0:T, cs_].rearrange("p (s d) -> p s d", d=64),
                           _bc(selS[:, 8 * half:8 * half + 8], 2, 64), ALU.mult, [pbk_, "selS"], selk)
                    dvop(lambda e, h=h: e.tensor_reduce(
                        out=osb[0:T, h * 64:(h + 1) * 64], in_=seltmp.rearrange("p (s d) -> p d s", d=64),
                        axis=AX.X, op=ALU.add), selk, [("osbS", h)], 1024)
                tt("dve", osb[0:T, :], TA[0:T, :], osb[0:T, :], ALU.add, ["TA"] + [("osbS", h) for h in range(8)],
                   ["osb"])

        def c_B4():
            osq = R1[0:T, 0:512]
            tt("dve", osq, osb[0:T, :], osb[0:T, :], ALU.mult, ["osb"], ["R1"])
            dvop(lambda e: e.tensor_reduce(out=small[0:T, 50:58], in_=osq.rearrange("p (h d) -> p h d", d=64),
                                           axis=AX.X, op=ALU.add), ["R1"], ["ssq"], 512)
            rms_rstd(small[0:T, 50:58], small[0:T, 64:72], small[0:T, 72:80], 64.0, ["ssq"], "r8")
            o3 = osb[0:T, :].rearrange("p (h d) -> p h d", d=64)
            tt("dve", o3, o3, _bc(small[0:T, 72:80], 2, 64), ALU.mult, ["osb", "r8"], ["osb"])
            tt("dve", oab[0:T, :], osb[0:T, :], sog[0:T, :], ALU.mult, ["osb", "sog"], ["oab"])
            transposes([(XAb3[:, c, 0:T], oab[0:T, c * 128:(c + 1) * 128], identb[0:T, 0:T]) for c in range(4)],
                       ["oab", "identb"], ["XA"])
            actf(oaT3[:, 0:4, 0:T], XAb3[:, 0:4, 0:T], AF.Copy, ["XA"], ["oaT"])

        def c_B5():
            wT = wTp if kind == "p" else wTs
            bsr = bsP if kind == "p" else bsS
            mms([x for g in range(4) for x in
                 ((TB3[:, g, 0:T], vnb[0:T, g * 128:(g + 1) * 128], wT[0:T, g, 0:T], True, False),
                  (TB3[:, g, 0:T], onesf[0:1, :], bsr[0:1, g, 0:T], False, True))],
                ["vnb", "wTp", "onesf", "bsP", "bsS"] + wTs_keys, ["TB"])
            tt("dve", obT3[:, 0:4, 0:T], TB3[:, 0:4, 0:T], guT3[:, 0:4, 0:T], ALU.mult, ["TB", "guT"], ["obT"])
            if kind == "p":
                gates()
            for cb in range(2):
                mms([(FA[0:T, cb * 512:(cb + 1) * 512], oaT3[:, kc, 0:T], Wpa(kc, cb * 512, 512), kc == 0, kc == 3)
                     for kc in range(4)], ["oaT"] + wk("w_pa", 4, 0, 1024), ["FA"])
            for cb in range(2):
                mms([(FB[0:T, cb * 512:(cb + 1) * 512], obT3[:, kc, 0:T], Wpb(kc, cb * 512, 512), kc == 0, kc == 3)
                     for kc in range(4)], ["obT"] + wk("w_pb", 4, 0, 1024), ["FB"])

        def c_B6():
            tt("dve", GAt[0:T, :], FA[0:T, :], GAt[0:T, :], ALU.mult, ["FA", "GA"], ["GA"])
            tt("dve", GBt[0:T, :], FB[0:T, :], GBt[0:T, :], ALU.mult, ["FB", "GB"], ["GB"])
            htok = R1.bitcast(BF16)
            tt("dve", htok[0:T, :], GAt[0:T, :], GBt[0:T, :], ALU.add, ["GA", "GB"], ["R1"])
            transposes([(XAb3[:, c, 0:T], htok[0:T, c * 128:(c + 1) * 128], identb[0:T, 0:T]) for c in range(8)],
                       ["R1", "identb"], ["XA"])
            actf(hTb3[:, :, 0:T], XAb3[:, :, 0:T], AF.Copy, ["XA"], ["hTb"])
            for cb in range(2):
                mms([(FA[0:T, cb * 512:(cb + 1) * 512], hTb3[:, kc, 0:T], Wo(kc, cb * 512, 512), kc == 0, kc == 7)
                     for kc in range(8)], ["hTb"] + wk("w_o", 8, 0, 1024), ["FA"])
            for cb in range(2):
                actf(junkA[0:T, cb * 512:(cb + 1) * 512], FA[0:T, cb * 512:(cb + 1) * 512], AF.Square, ["FA"],
                     ["oab", "oaT", ("ssm", cb)], accum=sm(58 + cb, 1, T))
            tt("dve", sm(60, 1, T), sm(58, 1, T), sm(59, 1, T), ALU.add, [("ssm", 0), ("ssm", 1)], ["ssmt"])
            rms_rstd(sm(60, 1, T), sm(61, 1, T), sm(62, 1, T), D, ["ssmt"], "rm")
            for cb in range(2):
                cs_ = slice(cb * 512, (cb + 1) * 512)
                stt(R1[0:T, :], FA[0:T, cs_], sm(62, 1, T), GP_bc[0:T, cs_], ALU.mult, ALU.mult,
                    ["FA", "rm", "GP_bc"], ["R1"])
                tt("dve", xrb[0:T, cs_], R1[0:T, :], xrb[0:T, cs_], ALU.add, ["R1", xrk], [xrk])
            dma("sp", x1s[row0:row0 + T, :], xrb[0:T, :], [xrk], [("x1s", ti if kind == "p" else NPT)], "x1st")

        C.update(F1=c_F1, F2=c_F2, F4=c_F4, F4b=c_F4b, F3=c_F3, F5=c_F5, B1=c_B1, B2=c_B2, B3=c_B3, B4=c_B4, B5=c_B5, B6=c_B6)
        return C

    def build_gate_weights():
        wraw = V(vA, "GA", F32, 128)
        for g in range(4):
            pdma(wraw[:, g, :], w_s[g], ["GA"])
        transposes([(XB3[:, g, :], wraw[:, g, :], identf[:, :]) for g in range(4)],
                   ["GA"] + ["identf"], ["XB"], after=["QT"])
        tt("dve", wTp[:, :, :], XB3[:, 0:4, :], _bc(maskP[:, :], 1, 4), ALU.mult, ["XB", "maskP"], ["wTp"])
        for g in range(4):
            mms([(XB[0:4, 0:64], w4[:, g, :], selE[:, :], True, True)], ["w4", "selE"], ["XB"], after=["QT"])
            actf(m1[:, :], XB[0:4, 0:64], AF.Copy, ["XB"], ["m1"])
            mms([(XB[0:64, 64:128], selE[:, :], m1[:, :], True, True)], ["m1", "selE"], ["XB"])
            tt("dve", wTs[:, g, :], XB[0:64, 64:128], maskS[:, :], ALU.mult, ["XB", "maskS"], [("wTs", g)])

    URG = 0.0
    ORDER = ["F1", "B1", "F2", "B2", "F4", "B3", "F4b", "B4", "F3", "B5", "F5", "B6"]
    tiles = [make_tile(ti, "p") for ti in range(NPT)]
    tileS = make_tile(0, "s")
    for rnd in range(NPT + 1):
        fr = tiles[rnd] if rnd < NPT else None
        bk = tiles[rnd - 1] if rnd >= 1 else None
        if rnd == 1:
            P.tag = "gatew"
            build_gate_weights()
        if bk is not None:
            P.tag = "xr"
            xload(rnd - 1, "p", 1)
        for c in ORDER:
            P.tag = c
            if c[0] == "F" and fr is not None:
                fr[c]()
                if c == "F1":
                    P.tag = "xl"
                    if rnd + 1 < NPT:
                        xload(rnd + 1, "p", 0)
                    elif rnd + 1 == NPT:
                        xload(0, "s", 0)
            if c[0] == "B" and bk is not None:
                bk[c]()
    P.tag = "xr"
    xload(0, "s", 1)
    for c in ORDER:
        if c[0] == "F":
            P.tag = "s" + c
            tileS[c]()
    for c in ORDER:
        if c[0] == "B":
            P.tag = "s" + c
            tileS[c]()

    P.tag = "preS0"
    xtmp = arena[:, vA["xnT0"][0]:vA["xnT0"][0] + 1024]
    xn2pre = V(vA, "F", BF16, 256)
    for sub in range(2):
        dma("sp", xtmp, x1s[sub * 128:(sub + 1) * 128, :], [("x1s", sub)], ["xnT0", "xnT1"], "pre")
        actf(xsb[:, :], xtmp, AF.Square, ["xnT0", "xnT1"], ["xsb", "ss"], accum=sm(0, 1, 128))
        rms_rstd(sm(0, 1, 128), sm(1, 1, 128), sm(2, 1, 128), D, ["ss"], "rstd")
        ts("dve", xsb[:, :], xtmp, sm(2, 1, 128), None, ALU.mult, None, ["xnT0", "xnT1", "rstd"], ["xsb"])
        transposes([(XAb3[:, kc, :], xsb[:, kc * 128:(kc + 1) * 128], identb[:, :]) for kc in range(8)],
                   ["xsb", "identb"], ["XA"])
        tt("dve", xn2pre[:, :, sub * 128:(sub + 1) * 128], XAb3[:, :, :], _bc(prmv(GFFNT), 2, 128), ALU.mult,
           ["XA", "prmT"], K8("F"))
    P.tag = "wup"
    load_w(Wup, wupb, 8, DUP, 1408, "w_up", order=[0, 2, 1, 3], off=0, kstride=DUP, overlay=True,
           late=lambda k_: k_[0] == "w_in" and k_[1] >= 4, rkey=lambda kc: ("wupb", kc), bpe=2)
    P.barrier(keep=("w_up", "wdnb"))
    load_w(Wdn, wdnb, NJ, D, 1024, "w_dn", off=OFF_DN, kstride=D, overlay=True, own_chan=True, bpe=2,
           rkey=lambda kc: ("wdnb", kc // 2))
    pdma(GP_bc[:, :], gpost2.partition_broadcast(128), "GP_bc")

    TBP = 256
    NSUP = SEQ // TBP
    x1t = [V(vB, "x1t%d" % i) for i in range(3)]
    xrot = [0]

    def getx():
        k = xrot[0] % 3
        xrot[0] += 1
        return x1t[k], "x1t%d" % k, "ldB%d" % k

    xsbB = V(vB, "xsb", BF16)
    xn2 = [V(vB, "xnT0", BF16, TBP), V(vB, "xnT1", BF16, TBP)]
    upx = [V(vB, "upx%d" % i, F32, 264) for i in range(2)]
    cg = [V(vB, "cg0"), V(vB, "cg1")]
    cv = [V(vB, "cv0"), V(vB, "cv1")]
    hT3 = V(vB, "hT", BF16, TBP)
    ybuf = V(vB, "ybuf")
    tmu = [V(vB, "tmu0", parts=32), V(vB, "tmu1", parts=32)]
    carryP = V(vB, "carryP")[:, 0:88].rearrange("p (r g j) -> p g j r", g=2, r=2)
    carryS = V(vB, "carryS").rearrange("p (g j r) -> p g j r", g=2, r=32)
    P.op("dve", lambda e: e.memset(V(vB, "carryP"), 0.0), [], ["carryP"])
    sS0h = V(vB, "carryS")[:, 0:1024].rearrange("p (s d) -> p s d", d=64)
    sgl = V(vB, "carryS")[:, 1024:1152].rearrange("p (h s) -> p h s", s=16)
    simask = V(vB, "tmu0", BF16, parts=64)
    sKtok = V(vB, "tmu1", BF16, parts=64)
    sibf = V(vB, "sibf", BF16, parts=64)
    dma("sp", sKtok[:, :], sK_d[:, :], [], ["tmu1"], "sv0")
    dma("sp", sibf[:, :], sI_d[:, :], [], ["sibf"], "sv1")
    dma("sp", V(vB, "carryS")[:, 1024:1152], sG_d[:, :], [], [("carryS", "g")], "sv2")

    def state_update(h):
        dma("sp", sS0h, st0[:, h, :, :].rearrange("s k d -> k s d"), [], ["carryS"], "st0")
        tt("dve", simask.rearrange("p (s d) -> p s d", d=64), _bc(sibf[:, h * 64:(h + 1) * 64], 1, 16),
           _bc(selS[:, :], 2, 64), ALU.mult, ["sibf", "selS"], ["tmu0"])
        mms([(TA[:, :], sKtok[:, h * 128:(h + 1) * 128], simask[:, 0:512], True, True)], ["tmu1", "tmu0"], [("T2", 0)])
        mms([(TB[:, :], sKtok[:, h * 128:(h + 1) * 128], simask[:, 512:1024], True, True)], ["tmu1", "tmu0"],
            [("T2", 1)])
        S0f = sS0h.rearrange("p s d -> p (s d)")
        tt("dve", S0f[:, 0:512], TA[:, :], S0f[:, 0:512], ALU.add, [("T2", 0), "carryS"], ["carryS"])
        tt("dve", S0f[:, 512:1024], TB[:, :], S0f[:, 512:1024], ALU.add, [("T2", 1), "carryS"], ["carryS"])
        tt("dve", sS0h, sS0h, _bc(sgl[:, h, :], 2, 64), ALU.mult, ["carryS", ("carryS", "g")], ["carryS"])
        dma("sp", sso[:, h, :, :].rearrange("s k d -> k s d"), sS0h, ["carryS"], [("sso", h)], "so0")

    def cache_prologue():
        cbuf = V(vB, "tmu1", parts=32)
        for c in range(11):
            dma("sp", cbuf[:, 0:512], cch[:, c * 512:(c + 1) * 512], [], ["tmu1"], "cch")
            transposes([(XB[:, b * 32:(b + 1) * 32], cbuf[:, b * 128:(b + 1) * 128], identf[0:32, 0:32])
                        for b in range(4)], ["tmu1", "identf"], ["XB"])
            dst = V(vB, "carryS")[:, c * 128:(c + 1) * 128]
            actf(dst, XB[:, 0:128], AF.Copy, ["XB"], ["carryS"])

    def geo(kind):
        if kind == "p":
            return TBP, 2, 1, 2, 128
        return TS, 32, 16, 1, TS

    def S0(n, kind):
        T, PV, sh, nsub, Tt = geo(kind)
        xn = xn2[n % 2]
        xnk = "xnT%d" % (n % 2)
        for sub in range(nsub):
            row0 = n * TBP + sub * 128 if kind == "p" else SEQ
            xb_, xk, xc = getx()
            dma("sp", xb_[0:Tt, :], x1s[row0:row0 + Tt, :], [], [xk], xc)
            actf(xsbB[0:Tt, :], xb_[0:Tt, :], AF.Square, [xk], ["xsb", "ss"], accum=sm(0, 1, Tt))
            rms_rstd(sm(0, 1, Tt), sm(1, 1, Tt), sm(2, 1, Tt), D, ["ss"], "rstd")
            ts("dve", xsbB[0:Tt, :], xb_[0:Tt, :], sm(2, 1, Tt), None, ALU.mult, None, [xk, "rstd"], ["xsb"])
            transposes([(XAb3[:, kc, 0:Tt], xsbB[0:Tt, kc * 128:(kc + 1) * 128], identb[0:Tt, 0:Tt])
                        for kc in range(8)], ["xsb", "identb"], ["XA"])
            tt("dve", xn[:, :, sub * 128:sub * 128 + Tt], XAb3[:, :, 0:Tt], _bc(prmv(GFFNT), 2, Tt), ALU.mult,
               ["XA", "prmT"], [(xnk, sub)])

    pbanks = [(FA[:, 0:512], ("FA", 0)), (FA[:, 512:1024], ("FA", 1)),
              (FB[:, 0:512], ("FB", 0)), (FB[:, 512:1024], ("FB", 1))]

    def pbank(j):
        bk, bkey = pbanks[j % 4]
        return bk.rearrange("p (g t) -> p g t", g=2), bkey

    def stA(n, kind, j):
        T, PV, sh, nsub, Tt = geo(kind)
        bk3, bkey = pbank(j)
        xn = xn2[n % 2]
        xnk = "xnT%d" % (n % 2)
        for g2 in range(2):
            c0 = g2 * DFF + j * 128
            mms([(bk3[:, g2, 0:T], Wup(kc, c0, 128), xn[:, kc, 0:T], kc == 0, kc == 7) for kc in range(8)],
                [(xnk, 0), (xnk, 1)] + wk("w_up", 8, c0, 128, 1408), [bkey])

    def stB(n, kind, j):
        T, PV, sh, nsub, Tt = geo(kind)
        bk3, bkey = pbank(j)
        u = j % 2
        ux, uk = upx[u], "upx%d" % u
        carry = carryP if kind == "p" else carryS
        ckey = "carryP" if kind == "p" else "carryS"
        actf(ux[:, :, PV:PV + T], bk3[:, :, 0:T], AF.Copy, [bkey], [uk])
        P.op("pool", lambda e: e.tensor_copy(out=ux[:, :, 0:PV], in_=carry[:, :, j, :]), [ckey], [(uk, "c")])
        if kind == "p":
            P.op("pool", lambda e: e.tensor_copy(out=carry[:, :, j, :], in_=ux[:, :, T:T + PV]), [uk], [ckey])
        for g2, acc, ak in ((0, cg[u], "cg%d" % u), (1, cv[u], "cv%d" % u)):
            bi = g2 * NJ + j
            actf(acc[:, 0:T], bk3[:, g2, 0:T], AF.Identity, [bkey, "prmT"], [ak], bias=convb_v(bi), scale=convw_v(2, bi))

    def stC(n, kind, j):
        T, PV, sh, nsub, Tt = geo(kind)
        u = j % 2
        ux, uk = upx[u], "upx%d" % u
        for g2, acc, ak in ((0, cg[u], "cg%d" % u), (1, cv[u], "cv%d" % u)):
            bi = g2 * NJ + j
            a = acc[:, 0:T]
            stt(a, ux[:, g2, 0:T], convw_v(0, bi), a, ALU.mult, ALU.add, [uk, (uk, "c"), "prmTB", ak], [ak])
            stt(a, ux[:, g2, sh:sh + T], convw_v(1, bi), a, ALU.mult, ALU.add, [uk, (uk, "c"), "prmTB", ak], [ak])

    def stD(n, kind, j):
        T, PV, sh, nsub, Tt = geo(kind)
        u = j % 2
        actf(cg[u][:, 0:T], cg[u][:, 0:T], AF.Gelu_apprx_tanh, ["cg%d" % u], ["cg%d" % u])
        tt("pool", hT3[:, j, 0:T], cg[u][:, 0:T], cv[u][:, 0:T], ALU.mult, ["cg%d" % u, "cv%d" % u], [("hT", j)])

    XAf = XA[:, :]

    def ytail(n, kind):
        T, PV, sh, nsub, Tt = geo(kind)
        phs = [[(TA, ("T2", 0)), (TB, ("T2", 1))], [(XAf, "XA"), (XB[:, :], "XB")]]
        for kc in range(NJ):
            for sub in range(nsub):
                for cb in range(2):
                    mms([(phs[sub][cb][0][0:Tt, :], hT3[:, kc, sub * 128:sub * 128 + Tt], Wdn(kc, cb * 512, 512),
                          kc == 0, kc == NJ - 1)], [("hT", kc), ("w_dn", 0, kc)], [phs[sub][cb][1]])
        for sub in range(nsub):
            row0 = n * TBP + sub * 128 if kind == "p" else SEQ
            dst = yp[row0:row0 + Tt, :] if kind == "p" else ys[:, :]
            ph = phs[sub]
            xb_, xk, xc = getx()
            dma("sp", xb_[0:Tt, :], x1s[row0:row0 + Tt, :], [], [xk], xc)
            for cb in range(2):
                actf(xsbB[0:Tt, cb * 512:(cb + 1) * 512], ph[cb][0][0:Tt, :], AF.Square, [ph[cb][1]],
                     ["xsb", ("ssm", cb)], accum=sm(58 + cb, 1, Tt))
            tt("dve", sm(60, 1, Tt), sm(58, 1, Tt), sm(59, 1, Tt), ALU.add, [("ssm", 0), ("ssm", 1)], ["ssmt"])
            rms_rstd(sm(60, 1, Tt), sm(61, 1, Tt), sm(62, 1, Tt), D, ["ssmt"], "rm")
            for cb in range(2):
                cs_ = slice(cb * 512, (cb + 1) * 512)
                stt(ybuf[0:Tt, cs_], ph[cb][0][0:Tt, :], sm(62, 1, Tt), GP_bc[0:Tt, cs_], ALU.mult, ALU.mult,
                    [ph[cb][1], "rm", "GP_bc"], ["ybuf"])
            tt("pool", ybuf[0:Tt, :], ybuf[0:Tt, :], xb_[0:Tt, :], ALU.add, ["ybuf", xk], ["ybuf"])
            dma("sp", dst, ybuf[0:Tt, :], ["ybuf"], [("y", n, sub)], "yst")

    def pair_loop(n, kind, nxt):
        for s_ in range(NJ + 3):
            if 0 <= s_ - 3 < NJ:
                P.tag = "stD"
                stD(n, kind, s_ - 3)
            if 0 <= s_ - 2 < NJ:
                P.tag = "stC"
                stC(n, kind, s_ - 2)
            if 0 <= s_ - 1 < NJ:
                P.tag = "stB"
                stB(n, kind, s_ - 1)
            if s_ < NJ:
                P.tag = "stA"
                stA(n, kind, s_)
            if s_ == 6 and nxt is not None:
                P.tag = "S0"
                S0(*nxt)

    for n in range(NSUP):
        nxt = (n + 1, "p") if n + 1 < NSUP else (NSUP, "s")
        pair_loop(n, "p", nxt)
        P.tag = "ytail"
        ytail(n, "p")
        if n < 4:
            P.tag = "state"
            state_update(2 * n)
            state_update(2 * n + 1)
        if n == 3:
            P.tag = "cache"
            cache_prologue()
    transposes([(XB[0:88, 0:128], V(vB, "carryP")[:, 0:88], identf[:, :])], ["carryP", "identf"], ["XB"])
    cpT = V(vB, "tmu0")
    actf(cpT[0:88, 0:128], XB[0:88, 0:128], AF.Copy, ["XB"], ["tmu0"])
    for r_ in range(2):
        dma("sp", cpo[r_].rearrange("(b p) -> b p", p=128), cpT[r_ * 44:(r_ + 1) * 44, 0:128], ["tmu0"],
            [("cpo", r_)], "cpo")
    pair_loop(NSUP, "s", None)
    xnS = xn2[NSUP % 2]
    xnSk = "xnT%d" % (NSUP % 2)
    for cb in range(11):
        tb = cb % 2
        pbk = TA if tb == 0 else TB
        mms([(pbk[0:32, :], xnS[:, kc, 32:64], Wup(kc, cb * 512, 512), kc == 0, kc == 7) for kc in range(8)],
            [(xnSk, 0)] + wk("w_up", 8, cb * 512, 512, 1408), [("T2", tb)])
        actf(tmu[tb][:, :], pbk[0:32, :], AF.Copy, [("T2", tb)], ["tmu%d" % tb])
        dma("sp", cso[:, cb * 512:(cb + 1) * 512], tmu[tb][:, :], ["tmu%d" % tb], [("cso", cb)], "cso%d" % tb)
    ytail(NSUP, "s")
    P.finish()

    with nc.allow_non_contiguous_dma(reason="small parameter / state layouts"):
        P.emit()
    es.close()
    return nc, P


def _host_consts():
    ident = np.eye(128, dtype=np.float32)
    s = np.arange(128)
    maskP = (s[:, None] <= s[None, :]).astype(np.float32)
    a = np.arange(64)
    maskS = ((a[:, None] % 16 == a[None, :] % 16) & (a[:, None] // 16 <= a[None, :] // 16)).astype(np.float32)
    selS = (a[:, None] % 16 == np.arange(16)[None, :]).astype(np.float32)
    selE = (np.arange(4)[:, None] == a[None, :] // 16).astype(np.float32)
    return ident, maskP, maskS, selS, selE


_CACHE = {}


def kernel(x_prompt, x_sample, state_hgrn, cache_ffn_conv, lb_param, mix_pre_g, w_in, hgrn_norm_g,
           gmlp_ln_g, gmlp_ln_b, w_s, b_s, w_pa, w_pb, w_o, mix_post_g, ffn_pre_g, w_up, conv_w,
           conv_b, w_down, ffn_post_g):
    f = lambda a: np.ascontiguousarray(np.asarray(a, dtype=np.float32))
    if "nc" not in _CACHE:
        _CACHE["nc"] = build()[0]
    nc = _CACHE["nc"]
    ident, maskP, maskS, selS, selE = _host_consts()
    shared = {
        "lbp": f(lb_param), "gpre": f(mix_pre_g)[0], "w_in": f(w_in)[0], "ghn": f(hgrn_norm_g)[0],
        "lng": f(gmlp_ln_g)[0], "lnb": f(gmlp_ln_b)[0], "w_s": f(w_s)[0], "b_s": f(b_s)[0],
        "w_pa": f(w_pa)[0], "w_pb": f(w_pb)[0], "w_o": f(w_o)[0], "gpost": f(mix_post_g)[0],
        "gffn": f(ffn_pre_g)[0], "w_up": f(w_up)[0], "conv_w": f(conv_w)[0], "conv_b": f(conv_b)[0],
        "w_dn": f(w_down)[0], "gpost2": f(ffn_post_g)[0],
        "ident": ident, "maskP": maskP, "maskS": maskS, "selS": selS, "selE": selE,
    }
    x_prompt, x_sample = f(x_prompt), f(x_sample)
    state_hgrn, cache_ffn_conv = f(state_hgrn), f(cache_ffn_conv)
    in_maps = []
    for c in range(NCORES):
        m = dict(shared)
        m["xp"] = x_prompt[c]
        m["xs"] = np.ascontiguousarray(x_sample[c * SB:(c + 1) * SB].transpose(1, 0, 2).reshape(TS, D))
        m["st0"] = state_hgrn[0, c * SB:(c + 1) * SB]
        m["cch"] = np.ascontiguousarray(cache_ffn_conv[0, c * SB:(c + 1) * SB].transpose(1, 0, 2).reshape(32, DUP))
        in_maps.append(m)
    res = run_bass_kernel_spmd(nc, in_maps, core_ids=list(range(NCORES)))
    R = res.results
    yp = np.stack([R[c]["yp"] for c in range(NCORES)], 0)
    ys = np.concatenate([R[c]["ys"].reshape(4, SB, D).transpose(1, 0, 2) for c in range(NCORES)], 0)
    sp = np.stack([R[c]["spo"] for c in range(NCORES)], 0)[None]
    ss = np.concatenate([R[c]["sso"] for c in range(NCORES)], 0)[None]
    cp = np.stack([R[c]["cpo"] for c in range(NCORES)], 0)[None]
    cs = np.concatenate([R[c]["cso"].reshape(2, SB, DUP).transpose(1, 0, 2) for c in range(NCORES)], 0)[None]
    vs = np.concatenate([R[c]["vso"].reshape(4, SB, 512).transpose(1, 0, 2) for c in range(NCORES)], 0)[None]
    return (np.ascontiguousarray(yp, dtype=np.float32), np.ascontiguousarray(ys, dtype=np.float32),
            np.ascontiguousarray(sp, dtype=np.float32), np.ascontiguousarray(ss, dtype=np.float32),
            np.ascontiguousarray(cp, dtype=np.float32), np.ascontiguousarray(cs, dtype=np.float32),
            np.ascontiguousarray(vs, dtype=np.float32))
```

```python
import numpy as np
from contextlib import ExitStack

import concourse.bass as bass
import concourse.mybir as mybir
from concourse.bass_utils import run_bass_kernel_spmd

F32 = mybir.dt.float32
BF16 = mybir.dt.bfloat16
AF = mybir.ActivationFunctionType
ALU = mybir.AluOpType
AX = mybir.AxisListType

NCORES = 8
D = 1024
SEQ = 2048
NPT = SEQ // 128
SB = 16
TS = 64
DIN = 6144
DFF = 2816
DUP = 2 * DFF
NJ = DFF // 128
EPS = 1e-6
WELEMS = 67584


class Op:
    __slots__ = ("eng", "fn", "edges", "deps", "chan", "sig", "need", "busy", "lat", "idx", "succ", "prio",
                 "start", "nun", "tag", "aset", "xedges", "urg")


class Prog:
    ENG = ("pe", "act", "dve", "pool", "sp")
    QUANT = 0.01

    def __init__(self, nc):
        self.nc = nc
        self.regions = [[]]
        self.lastw = {}
        self.readers = {}
        self.chan_last = {}
        self.chan_cnt = {}

    def op(self, eng, fn, r=(), w=(), chan=None, busy=0.3, lat=None, after=(), aset=0):
        o = Op()
        o.eng, o.fn, o.chan, o.need, o.sig = eng, fn, chan, False, None
        o.busy = busy
        o.lat = busy if lat is None else lat
        o.tag = getattr(self, "tag", "")
        o.aset = aset
        o.urg = getattr(self, "urg", 0.0)
        edges, seen = [], set()

        def add(d, is_war):
            if d is None or d is o or id(d) in seen:
                return
            seen.add(id(d))
            edges.append((d, is_war))
        for b in r:
            add(self.lastw.get(b), False)
        for b in w:
            add(self.lastw.get(b), False)
        if chan is not None:
            add(self.chan_last.get(chan), False)
            self.chan_last[chan] = o
        for b in w:
            for d in self.readers.get(b, ()):
                add(d, True)
        for b in after:
            add(self.lastw.get(b), False)
            for d in self.readers.get(b, ()):
                add(d, True)
        o.edges = edges
        for b in r:
            self.readers.setdefault(b, []).append(o)
        for b in w:
            self.lastw[b] = o
            self.readers[b] = []
        self.regions[-1].append(o)
        return o

    def barrier(self, keep=()):
        self.regions.append([])
        kept = {k: v for k, v in self.lastw.items() if v.chan is not None and (k in keep or (isinstance(k, tuple) and k[0] in keep))}
        self.nobar = getattr(self, "nobar", set()) | set(v.chan for v in kept.values())
        self.lastw, self.readers = dict(kept), {}

    def finish(self):
        pass

    @staticmethod
    def _schedule(ops):
        n = len(ops)
        for i, o in enumerate(ops):
            o.idx, o.succ, o.start = i, [], None
        inreg = set(id(o) for o in ops)
        for o in ops:
            o.xedges = [d for (d, wr) in o.edges if id(d) not in inreg]
            o.edges = [(d, wr) for (d, wr) in o.edges if id(d) in inreg]
            o.nun = len(o.edges)
            for d, _ in o.edges:
                d.succ.append(o)
        for o in reversed(ops):
            o.prio = o.lat + max([s_.prio for s_ in o.succ], default=0.0)
        for o in ops:
            if o.chan is not None and o.aset != -1:
                o.prio = 1e9 - o.idx
        free = {e: 0.0 for e in Prog.ENG}
        rel = {e: [] for e in Prog.ENG}
        avail = {}
        for o in ops:
            if o.nun == 0:
                rel[o.eng].append(o)
                avail[id(o)] = 0.0
        order = {e: [] for e in Prog.ENG}
        done = 0
        cur_set = 0
        TL = 1.3
        QUANT = Prog.QUANT
        while done < n:
            best = None
            for e in Prog.ENG:
                lst = rel[e]
                if not lst:
                    continue
                fe = free[e]
                cand, ck = None, None
                for o in lst:
                    t = avail[id(o)]
                    pen = TL if (e == "act" and o.aset > 0 and o.aset != cur_set) else 0.0
                    st_ = max(t, fe) + pen
                    k = (int((st_ - o.urg) / QUANT), -o.prio, st_, o.idx)
                    if ck is None or k < ck:
                        cand, ck = o, k
                if best is None or ck < best[1]:
                    best = (cand, ck)
            o, k = best
            e = o.eng
            rel[e].remove(o)
            if e == "act" and o.aset > 0 and o.aset != cur_set:
                cur_set = o.aset
            o.start = k[2]
            free[e] = o.start + o.busy
            fin = o.start + o.lat
            fin_same = o.start + o.busy
            order[e].append(o)
            done += 1
            for s_ in o.succ:
                a = avail.get(id(s_), 0.0)
                f_ = fin_same if (s_.eng == e and o.chan is None and s_.chan is None) else fin
                if f_ > a:
                    avail[id(s_)] = f_
                else:
                    avail[id(s_)] = a
                s_.nun -= 1
                if s_.nun == 0:
                    rel[s_.eng].append(s_)
        return order, max(free.values())

    def emit(self):
        nc = self.nc
        streams = {e: [] for e in self.ENG}
        self.sim_us = []
        prev_last = None
        for reg in self.regions:
            order, span = self._schedule(reg)
            self.sim_us.append(span)
            if prev_last is not None:
                for e in self.ENG:
                    b = Op()
                    b.eng, b.fn, b.chan, b.need, b.sig = e, None, None, False, None
                    b.xedges = None
                    b.urg = 0.0
                    b.edges = [(d, False) for d in prev_last if not (d.chan is None and d.eng == e)]
                    streams[e].append(b)
            for e in self.ENG:
                streams[e].extend(order[e])
            prev_last = [order[e][-1] for e in ("pe", "act", "dve", "pool") if order[e]]
            chl = {}
            for e in self.ENG:
                for o in order[e]:
                    if o.chan is not None:
                        chl[o.chan] = o
            prev_last += [v for c_, v in chl.items() if c_ not in getattr(self, "nobar", set())]
        fin = Op()
        fin.eng, fin.fn, fin.chan, fin.need, fin.sig = "sp", None, None, False, None
        chl = {}
        for e in self.ENG:
            for o in streams[e]:
                if o.chan is not None:
                    chl[o.chan] = o
        fin.edges = [(d, False) for d in chl.values()]
        fin.xedges = None
        fin.urg = 0.0
        streams["sp"].append(fin)
        for e in self.ENG:
            for i, o in enumerate(streams[e]):
                o.idx = i
        for e in self.ENG:
            for o in streams[e]:
                deps, latest = [], {}
                for d, is_war in o.edges:
                    if d.chan is None and o.chan is None and d.eng == o.eng:
                        if o.eng in ("pe", "sp"):
                            continue
                    if d.chan is None:
                        if d.eng not in latest or latest[d.eng].idx < d.idx:
                            latest[d.eng] = d
                    else:
                        deps.append(d)
                deps.extend(getattr(o, "xedges", None) or [])
                deps.extend(latest.values())
                for d in deps:
                    d.need = True
                o.deps = deps
        with ExitStack() as es:
            sems = {}
            for e in ("pe", "act", "dve", "pool"):
                sems[e] = es.enter_context(nc.semaphore("s_" + e))
            for c in self.chan_cnt_keys():
                sems[("ch", c)] = es.enter_context(nc.semaphore("c_" + str(c)))
            cnt = {e: 0 for e in self.ENG}
            chc = {}
            for e in self.ENG:
                for o in streams[e]:
                    if o.chan is not None:
                        chc[o.chan] = chc.get(o.chan, 0) + 16
                        o.sig = (("ch", o.chan), chc[o.chan], 16)
                    elif o.need:
                        assert o.fn is not None
                        cnt[e] += 1
                        o.sig = (e, cnt[e], 1)
            self.counts = dict(cnt)
            block = es.enter_context(nc.Block())

            def mk(e):
                def body(eng):
                    waited = {}
                    for o in streams[e]:
                        for d in o.deps:
                            k, v, _ = d.sig
                            if waited.get(k, 0) >= v:
                                continue
                            eng.wait_ge(sems[k], v)
                            waited[k] = v
                        if o.fn is None:
                            continue
                        ins = o.fn(eng)
                        if o.sig is not None:
                            ins.then_inc(sems[o.sig[0]], o.sig[2])
                return body

            block.tensor(mk("pe"))
            block.scalar(mk("act"))
            block.vector(mk("dve"))
            block.gpsimd(mk("pool"))
            block.sync(mk("sp"))
        self.ops = streams

    def chan_cnt_keys(self):
        return list(self.chan_last.keys())


def _bc(ap, axis, n):
    a = ap.unsqueeze(axis)
    shp = list(a.shape)
    shp[axis] = n
    return a.to_broadcast(shp)


def build():
    nc = bass.Bass("TRN2", target_bir_lowering=False)
    P = Prog(nc)
    es = ExitStack()

    def din(name, shape):
        return nc.dram_tensor(name, list(shape), F32, kind="ExternalInput").ap()

    def dout(name, shape):
        return nc.dram_tensor(name, list(shape), F32, kind="ExternalOutput").ap()

    xp = din("xp", (SEQ, D))
    xs = din("xs", (TS, D))
    st0 = din("st0", (SB, 8, 128, 64))
    cch = din("cch", (32, DUP))
    lbp = din("lbp", (2, D))
    gpre = din("gpre", (D,))
    w_in = din("w_in", (D, DIN))
    ghn = din("ghn", (64,))
    lng = din("lng", (512,))
    lnb = din("lnb", (512,))
    w_s = din("w_s", (4, 128, 128))
    b_s = din("b_s", (4, 128))
    w_pa = din("w_pa", (512, D))
    w_pb = din("w_pb", (512, D))
    w_o = din("w_o", (D, D))
    gpost = din("gpost", (D,))
    gffn = din("gffn", (D,))
    w_up = din("w_up", (D, DUP))
    conv_w = din("conv_w", (3, DUP))
    conv_b = din("conv_b", (DUP,))
    w_dn = din("w_dn", (DFF, D))
    gpost2 = din("gpost2", (D,))
    ident_d = din("ident", (128, 128))
    maskP_d = din("maskP", (128, 128))
    maskS_d = din("maskS", (64, 64))
    selS_d = din("selS", (64, 16))
    selE_d = din("selE", (4, 64))

    yp = dout("yp", (SEQ, D))
    ys = dout("ys", (TS, D))
    spo = dout("spo", (8, 128, 64))
    sso = dout("sso", (SB, 8, 128, 64))
    cpo = dout("cpo", (2, DUP))
    cso = dout("cso", (32, DUP))
    vso = dout("vso", (TS, 512))
    x1s = nc.dram_tensor("x1s", [SEQ + TS, D], F32, kind="Internal").ap()
    wupb = nc.dram_tensor("wupb", [D, DUP], BF16, kind="Internal").ap()
    wdnb = nc.dram_tensor("wdnb", [DFF, D], BF16, kind="Internal").ap()
    sK_d = nc.dram_tensor("sK_d", [TS, D], BF16, kind="Internal").ap()
    sI_d = nc.dram_tensor("sI_d", [TS, 512], BF16, kind="Internal").ap()
    sG_d = nc.dram_tensor("sG_d", [128, 128], F32, kind="Internal").ap()

    def sb(name, shape, dt=F32):
        return es.enter_context(nc.sbuf_tensor("sb_" + name, list(shape), dt))

    Wh = sb("Wbig", (128, WELEMS), BF16)

    def Wv(off, kstride, kc, c0, n):
        return Wh[:, off + kc * kstride + c0: off + kc * kstride + c0 + n]

    OFF_PA, OFF_PB, OFF_O = 49152, 53248, 57344
    OFF_DN = 8 * DUP

    def Win(kc, c0, n): return Wv(0, DIN, kc, c0, n)
    def Wpa(kc, c0, n): return Wv(OFF_PA, D, kc, c0, n)
    def Wpb(kc, c0, n): return Wv(OFF_PB, D, kc, c0, n)
    def Wo(kc, c0, n): return Wv(OFF_O, D, kc, c0, n)
    def Wup(kc, c0, n): return Wv(0, DUP, kc, c0, n)
    def Wdn(kc, c0, n): return Wv(OFF_DN, D, kc, c0, n)

    identb = sb("identb", (128, 128), BF16)
    identf = sb("identf", (128, 128))
    maskP = sb("maskP", (128, 128))
    maskS = sb("maskS", (64, 64))
    selS = sb("selS", (64, 16))
    wTp = sb("wTp", (128, 4, 128), BF16)
    wTs = sb("wTs", (64, 4, 64), BF16)
    bsP = sb("bsP", (1, 4, 128))
    bsS = sb("bsS", (1, 4, 64))
    onesf = sb("onesf", (1, 128))

    ones2 = sb("ones2", (128, 128))
    epsc = sb("epsc", (128, 1))
    prmT = sb("prmT", (128, 120))
    prmTB = sb("prmTB", (128, 88))
    prm2 = sb("prm2", (128, 3, 8))
    selE = sb("selE", (4, 64))
    bs4 = sb("bs4", (1, 4, 4))
    w4 = sb("w4", (4, 4, 4))
    m1 = sb("m1", (4, 64))
    P0T, P1T, GPRET, GFFNT, LBT, OMLT, NOMLT = range(7)
    ghn_bc = sb("ghn_bc", (128, 64))
    lng_bc = sb("lng_bc", (128, 512))
    lnb_bc = sb("lnb_bc", (128, 512))
    GP_bc = sb("GP_bc", (128, D))
    small = sb("small", (128, 96))

    def prmv(i):
        if i < 4:
            return prmT[:, 8 * i:8 * i + 8]
        return prm2[:, i - 4, :]

    def convb_v(bi):
        return prmT[:, 32 + bi:33 + bi]

    def convw_v(tap, bi):
        if tap == 2:
            return prmT[:, 76 + bi:77 + bi]
        return prmTB[:, tap * 44 + bi:tap * 44 + bi + 1]

    layA = [("xt0", 4096), ("xsb", 2048), ("xnT0", 2048), ("xnT1", 2048), ("SG", 4096), ("F", 4096), ("B", 4096),
            ("GA", 4096), ("GB", 4096), ("QT", 2048), ("KT", 2048), ("Ktok", 2048), ("ibf", 1024),
            ("sog", 2048), ("gv", 2048), ("vnb", 1024), ("guT", 2048), ("scT", 2048), ("R1", 2048),
            ("osb", 2048), ("oab", 1024), ("oaT", 1024), ("obT", 1024), ("hTb", 2048), ("S", 2048), ("Spb", 1024)]
    layB = [("x1t0", 4096), ("x1t1", 4096), ("x1t2", 4096), ("xsb", 2048), ("xnT0", 4096), ("xnT1", 4096),
            ("upx0", 2112), ("upx1", 2112), ("cg0", 1024), ("cg1", 1024), ("cv0", 1024), ("cv1", 1024),
            ("hT", 11264), ("ybuf", 4096), ("tmu0", 2048), ("tmu1", 2048), ("carryP", 384), ("carryS", 5632),
            ("sibf", 1024)]
    szA = sum(s for _, s in layA)
    szB = sum(s for _, s in layB)
    ARENA = max(szA, szB)
    arena = sb("arena", (128, ARENA // 4))

    def mkviews(lay):
        d, off = {}, 0
        for n, s in lay:
            d[n] = (off // 4, s // 4)
            off += s
        return d
    vA, vB = mkviews(layA), mkviews(layB)

    def V(views, name, dt=F32, shape3=None, parts=128):
        o, n = views[name]
        a = arena[0:parts, o:o + n]
        if dt == BF16:
            a = a.bitcast(BF16)
        if shape3 is not None:
            a = a.rearrange("p (a b) -> p a b", b=shape3)
        return a

    def ps(name, n):
        return es.enter_context(nc.psum_tensor("ps_" + name, [128, n], F32))
    FA, FB, T2 = ps("FA", 1024), ps("FB", 1024), ps("T2", 1024)
    XA, XB = ps("XA", 512), ps("XB", 512)
    FA3 = FA[:, :].rearrange("p (a b) -> p a b", b=128)
    FB3 = FB[:, :].rearrange("p (a b) -> p a b", b=128)
    XB3 = XB[:, :].rearrange("p (a b) -> p a b", b=128)
    XAb = XA[:, :].bitcast(BF16)
    XAb3 = XAb.rearrange("p (a b) -> p a b", b=128)
    TA, TB = T2[:, 0:512], T2[:, 512:1024]

    def sm(c, n=1, parts=128):
        return small[0:parts, c:c + n]

    def fsz(ap):
        n = 1
        for d in ap.shape[1:]:
            n *= int(d)
        return n

    def dma(eng, out, in_, r, w, chan, after=(), bpe=4):
        nbytes = fsz(out) * int(out.shape[0]) * bpe
        busy = 0.15 if eng == "sp" else max(1.0, nbytes / 300e3)
        return P.op(eng, lambda e: e.dma_start(out=out, in_=in_), r, w, chan, busy=busy,
                    lat=2.2 + nbytes / 150e3, after=after)

    def actf(out, in_, func, r, w, bias=None, scale=None, accum=None):
        kw = {}
        if bias is not None:
            kw["bias"] = bias
        if scale is not None:
            kw["scale"] = scale
        if accum is not None:
            kw["accum_out"] = accum
        aset = {AF.Exp: 6, AF.Ln: 6, AF.Sigmoid: 2, AF.Silu: 18, AF.Gelu_apprx_tanh: 11}.get(func, 0)
        return P.op("act", lambda e: e.activation(out=out, in_=in_, func=func, **kw), r, w,
                    busy=0.24 + fsz(out) * 0.00078 + (0.1 if accum is not None else 0.0), aset=aset)

    def tt(eng, out, in0, in1, op, r, w):
        b = 0.12 + fsz(out) * 0.0011 if eng == "dve" else 0.2 + fsz(out) * 0.0026
        return P.op(eng, lambda e: e.tensor_tensor(out=out, in0=in0, in1=in1, op=op), r, w, busy=b)

    def ts(eng, out, in0, s1, s2, op0, op1, r, w):
        b = 0.12 + fsz(out) * 0.0008 if eng == "dve" else 0.2 + fsz(out) * 0.0026
        if s2 is None:
            return P.op(eng, lambda e: e.tensor_scalar(out=out, in0=in0, scalar1=s1, scalar2=None, op0=op0), r, w,
                        busy=b)
        return P.op(eng, lambda e: e.tensor_scalar(out=out, in0=in0, scalar1=s1, scalar2=s2, op0=op0, op1=op1), r, w,
                    busy=b)

    def stt(out, in0, scalar, in1, op0, op1, r, w):
        return P.op("dve", lambda e: e.scalar_tensor_tensor(out=out, in0=in0, scalar=scalar, in1=in1,
                                                            op0=op0, op1=op1), r, w,
                    busy=0.16 + fsz(out) * 0.0015)

    def dvop(fn, r, w, n, k=0.0011):
        return P.op("dve", fn, r, w, busy=0.12 + n * k)

    def mms(lst, r, w, after=()):
        def fn(e):
            ins = None
            for (o_, l_, r_, st, sp_) in lst:
                ins = e.matmul(o_, l_, r_, start=st, stop=sp_)
            return ins
        b = sum(0.035 + max(fsz(o_), 96) * 0.00045 for (o_, l_, r_, st, sp_) in lst)
        return P.op("pe", fn, r, w, busy=b, lat=b + 0.25, after=after)

    def transposes(lst, r, w, after=()):
        def fn(e):
            ins = None
            for (o_, i_, id_) in lst:
                ins = e.transpose(o_, i_, id_)
            return ins
        b = 0.1 * len(lst)
        return P.op("pe", fn, r, w, busy=b, lat=b + 0.25, after=after)

    wch = [0]

    wspans = []

    def wdma(out, in_, key, span=None, after=(), own_chan=False, r=(), bpe=4):
        c = ("wo%d" % wch[0]) if own_chan else ("w%d" % (wch[0] % 12))
        wch[0] += 1
        if span is not None:
            wspans.append((key, span[0], span[1]))
        o_ = dma("pool", out, in_, list(r), [key], c, after=after, bpe=bpe)
        if own_chan:
            o_.aset = -1
        return o_

    pch = [0]

    plate = [0.0]

    def pdma(out, in_, key):
        c = "p%d" % (pch[0] % 14)
        pch[0] += 1
        return dma("sp", out, in_, [], [key] if not isinstance(key, list) else key, c)

    wdma(identb[:, :], ident_d[:, :], "identb")
    dma("sp", arena[:, vA["xt0"][0]:vA["xt0"][0] + 1024], xp[0:128, :], [], ["xt0"], "ld0")
    pdma(identf[:, :], ident_d[:, :], "identf")
    PAr = arena[:, vA["B"][0]:vA["B"][0] + 128]
    PBr = arena[:, vA["SG"][0]:vA["SG"][0] + 128]
    KB8 = [("B", h_) for h_ in range(8)]
    KS8 = [("SG", h_) for h_ in range(8)]
    for i_, (r0, vec) in enumerate(((0, lbp[0]), (8, lbp[1]), (16, gpre), (24, gffn))):
        pdma(PAr[r0:r0 + 8, :], vec.rearrange("(h k) -> h k", k=128), ("B", i_))
    pdma(PAr[32:76, :], conv_b.rearrange("(b p) -> b p", p=128), ("B", 4))
    pdma(PAr[76:120, :], conv_w[2].rearrange("(b p) -> b p", p=128), ("B", 5))
    pdma(PBr[0:44, :], conv_w[0].rearrange("(b p) -> b p", p=128), ("SG", 0))
    pdma(PBr[44:88, :], conv_w[1].rearrange("(b p) -> b p", p=128), ("SG", 1))

    pdma(maskP[:, :], maskP_d[:, :], "maskP")
    pdma(maskS[:, :], maskS_d[:, :], "maskS")
    pdma(selS[:, :], selS_d[:, :], "selS")
    pdma(selE[:, :], selE_d[:, :], "selE")
    pdma(ghn_bc[:, :], ghn.partition_broadcast(128), "ghn_bc")
    pdma(lng_bc[:, :], lng.partition_broadcast(128), "lng_bc")
    pdma(lnb_bc[:, :], lnb.partition_broadcast(128), "lnb_bc")
    pdma(GP_bc[:, :], gpost.partition_broadcast(128), "GP_bc")
    pdma(bsP[0:1, :, :], b_s.unsqueeze(0), "bsP")
    pdma(bs4[0:1, :, :], b_s[:, 0:4].unsqueeze(0), "bs4")
    pdma(w4[:, :, :], w_s[:, 0:4, 0:4].rearrange("g b a -> b g a"), "w4")
    transposes([(XB[:, 0:120], PAr[0:120, :], identf[0:120, 0:120])], KB8 + ["identf"], ["XB"])
    actf(prmT[:, :], XB[:, 0:120], AF.Copy, ["XB"], ["prmT"])
    transposes([(XB[:, 0:88], PBr[0:88, :], identf[0:88, 0:88])], KS8 + ["identf"], ["XB"])
    actf(prmTB[:, :], XB[:, 0:88], AF.Copy, ["XB"], ["prmTB"])
    P.op("dve", lambda e: e.memset(onesf[:, :], 1.0), [], ["onesf"])
    P.op("dve", lambda e: e.memset(ones2[:, :], 1.0), [], ["ones2"])
    P.op("dve", lambda e: e.memset(epsc[:, :], EPS), [], ["epsc"])
    tt("dve", prmv(LBT), prmv(P0T), prmv(P1T), ALU.subtract, ["prmT"], ["lbT"])
    dvop(lambda e: e.tensor_copy(out=bsS[0:1, :, :].rearrange("p g (j s) -> p (g j) s", s=16),
                                 in_=_bc(bs4[0:1, :, :].rearrange("p g j -> p (g j)"), 2, 16)), ["bs4"], ["bsS"], 256)
    actf(prmv(LBT), prmv(LBT), AF.Sigmoid, ["lbT"], ["lbT"])
    ts("dve", prmv(OMLT), prmv(LBT), -1.0, 1.0, ALU.mult, ALU.add, ["lbT"], ["omlT"])
    ts("dve", prmv(NOMLT), prmv(OMLT), -1.0, None, ALU.mult, None, ["omlT"], ["nomlT"])

    wTs_keys = [("wTs", g) for g in range(4)]

    def load_w(dst_fn, src, K, ncols, cw, name, order=None, off=0, kstride=0, overlay=False, own_chan=False,
               late=None, rkey=None, bpe=4):
        ncg = ncols // cw
        todo = [(cg, kc) for cg in (order if order is not None else range(ncg)) for kc in range(K)]
        if overlay and late is not None:
            def is_late(cg, kc):
                s0 = off + kc * kstride + cg * cw
                return any(a_ < s0 + cw and s0 < b_ and late(k_) for (k_, a_, b_) in wspans)
            todo = [x for x in todo if not is_late(*x)] + [x for x in todo if is_late(*x)]
        for cg, kc in todo:
            if True:
                s0 = off + kc * kstride + cg * cw
                aft = [k_ for (k_, a_, b_) in wspans if a_ < s0 + cw and s0 < b_] if overlay else ()
                wdma(dst_fn(kc, cg * cw, cw), src[kc * 128:(kc + 1) * 128, cg * cw:(cg + 1) * cw], (name, cg, kc),
                     span=None if overlay else (s0, s0 + cw), after=aft, own_chan=own_chan,
                     r=[rkey(kc)] if rkey is not None else (), bpe=bpe)

    load_w(Win, w_in, 8, DIN, 1024, "w_in", order=[1, 0, 2, 3, 4, 5], off=0, kstride=DIN)
    load_w(Wpa, w_pa, 4, D, 1024, "w_pa", off=OFF_PA, kstride=D)
    load_w(Wpb, w_pb, 4, D, 1024, "w_pb", off=OFF_PB, kstride=D)
    load_w(Wo, w_o, 8, D, 1024, "w_o", off=OFF_O, kstride=D)
    wspans.append(("xt1", 65536, 67584))
    for kc in range(8):
        wdma(wupb[kc * 128:(kc + 1) * 128, :], w_up[kc * 128:(kc + 1) * 128, :], ("wupb", kc))
    for i_ in range(11):
        wdma(wdnb[i_ * 256:(i_ + 1) * 256, :], w_dn[i_ * 256:(i_ + 1) * 256, :], ("wdnb", i_))

    def wk(name, K, c0, n, cw=1024):
        return [(name, cg, kc) for cg in range(c0 // cw, (c0 + n - 1) // cw + 1) for kc in range(K)]

    xt = [V(vA, "xt0"), Wh[:, 65536:67584].bitcast(F32)]
    xsb = V(vA, "xsb", BF16)
    xnT = [V(vA, "xnT0", BF16, 128), V(vA, "xnT1", BF16, 128)]
    SG3, F3, B3, GA3, GB3 = (V(vA, n, F32, 128) for n in ("SG", "F", "B", "GA", "GB"))
    GAt, GBt = V(vA, "GA"), V(vA, "GB")
    QT3, KT3 = V(vA, "QT", BF16, 128), V(vA, "KT", BF16, 128)
    Ktok = V(vA, "Ktok", BF16)
    ibf = V(vA, "ibf", BF16)
    sog = V(vA, "sog")
    gv = V(vA, "gv")
    vnb = V(vA, "vnb", BF16)
    guT3 = V(vA, "guT", F32, 128)
    scT3 = V(vA, "scT", BF16, 128)
    osb = V(vA, "osb")
    oab = V(vA, "oab", BF16)
    oaT3 = V(vA, "oaT", BF16, 128)
    obT3 = V(vA, "obT", BF16, 128)
    hTb3 = V(vA, "hTb", BF16, 128)
    R1 = V(vA, "R1")
    S_ = V(vA, "S")
    S3 = V(vA, "S", F32, 64)
    Spb = V(vA, "Spb", BF16)
    junkA = arena[:, vA["oab"][0]:vA["oab"][0] + 512].bitcast(BF16)
    XAb = XA[:, :].bitcast(BF16)
    XAb3 = XAb.rearrange("p (a b) -> p a b", b=128)
    XBb = XB[:, :].bitcast(BF16)
    TB3 = TB.rearrange("p (a b) -> p a b", b=128)

    def K8(n):
        return [(n, h) for h in range(8)]

    P.op("dve", lambda e: e.memset(S_, 0.0), [], ["S"])
    P.op("dve", lambda e: e.memset(scT3[64:128, :, 0:64], 0.0), [], ["scT"])

    def rms_rstd(ss_ap, ln_ap, out_ap, dim, rkeys, wkey):
        actf(ln_ap, ss_ap, AF.Ln, rkeys + ["epsc"], [wkey + "_ln"], bias=epsc[0:ss_ap.shape[0], :], scale=1.0 / dim)
        actf(out_ap, ln_ap, AF.Exp, [wkey + "_ln"], [wkey], scale=-0.5)

    def xload(ti, kind, which):
        T = 128 if kind == "p" else TS
        src = xp[ti * 128:(ti + 1) * 128, :] if kind == "p" else xs[:, :]
        dma("sp", xt[which][0:T, :], src, [], ["xt%d" % which], "ld%d" % which)

    def make_tile(ti, kind):
        T = 128 if kind == "p" else TS
        par = ti % 2 if kind == "p" else 0
        xtb, xtk = xt[0], "xt0"
        xrb, xrk = xt[1], "xt1"
        xn, xnk = xnT[par], "xnT%d" % par
        row0 = ti * 128 if kind == "p" else SEQ
        C = {}

        def fm(ps3, c0, nb, pkey):
            for b in range(nb):
                mms([(ps3[:, b, 0:T], Win(kc, c0 + b * 128, 128), xn[:, kc, 0:T], kc == 0, kc == 7)
                     for kc in range(8)], [xnk] + wk("w_in", 8, c0 + b * 128, 128), [pkey])

        def tm(psv, c0, pkey):
            mms([(psv[0:T, :], xn[:, kc, 0:T], Win(kc, c0, 512), kc == 0, kc == 7) for kc in range(8)],
                [xnk] + wk("w_in", 8, c0, 512), [pkey])

        def c_F1():
            actf(xsb[0:T, :], xtb[0:T, :], AF.Square, [xtk], ["xsb", "ss"], accum=sm(0, 1, T))
            rms_rstd(sm(0, 1, T), sm(1, 1, T), sm(2, 1, T), D, ["ss"], "rstd")
            actf(xsb[0:T, :], xtb[0:T, :], AF.Identity, [xtk, "rstd"], ["xsb"], scale=sm(2, 1, T))
            transposes([(XAb3[:, kc, 0:T], xsb[0:T, kc * 128:(kc + 1) * 128], identb[0:T, 0:T]) for kc in range(8)],
                       ["xsb", "identb"], ["XA"])
            tt("dve", xn[:, :, 0:T], XAb3[:, :, 0:T], _bc(prmv(GPRET), 2, T), ALU.mult, ["XA", "prmT"], [xnk])

        def c_F2():
            fm(FA3, 1024, 8, "FA")
            fm(FB3, 0, 8, "FB")
            actf(SG3[:, :, 0:T], FA3[:, :, 0:T], AF.Sigmoid, ["FA"], K8("SG"))
            for h in range(8):
                actf(F3[:, h, 0:T], SG3[:, h, 0:T], AF.Ln, [("SG", h), "omlT", "lbT"], [("F", h)],
                     scale=prmv(OMLT)[:, h:h + 1], bias=prmv(LBT)[:, h:h + 1])
            for h in range(8):
                actf(SG3[:, h, 0:T], SG3[:, h, 0:T], AF.Identity, [("SG", h), "omlT", "nomlT"], [("SG", h)],
                     scale=prmv(NOMLT)[:, h:h + 1], bias=prmv(OMLT)[:, h:h + 1])

        def c_F4():
            if kind == "p":
                for h in range(8):
                    dvop(lambda e, h=h: e.tensor_tensor_scan(out=B3[:, h, 0:T], data0=ones2[:, 0:T],
                                                                    data1=F3[:, h, 0:T], initial=0.0,
                                                                    op0=ALU.mult, op1=ALU.add),
                         [("F", h), "ones2"], [("B", h)], T, 0.0022)
                dvop(lambda e: e.tensor_copy(out=small[:, 8:16], in_=B3[:, :, 63]), K8("B"), ["b63"], 8)
                ts("dve", small[:, 16:24], B3[:, :, 63], -1.0, None, ALU.mult, None, K8("B"), ["nb63"])
                actf(small[:, 24:32], small[:, 8:16], AF.Exp, ["b63"], ["eb63"])
                for h in range(8):
                    actf(F3[:, h, 0:T], B3[:, h, 0:T], AF.Exp, [("B", h), "nb63"], [("F", h)],
                         bias=small[:, 16 + h:17 + h], scale=1.0)
                    actf(B3[:, h, 0:T], B3[:, h, 0:T], AF.Exp, [("B", h), "b63"], [("B", h)],
                         bias=small[:, 8 + h:9 + h], scale=-1.0)
            else:
                dvop(lambda e: e.tensor_copy(out=B3[:, :, 0:16], in_=F3[:, :, 0:16]), K8("F"), K8("B"), 128)
                for j in range(1, 4):
                    tt("dve", B3[:, :, 16 * j:16 * j + 16], B3[:, :, 16 * j - 16:16 * j],
                       F3[:, :, 16 * j:16 * j + 16], ALU.add, K8("B") + K8("F"), K8("B"))
                actf(F3[:, :, 0:T], B3[:, :, 0:T], AF.Exp, K8("B"), K8("F"))
                actf(B3[:, :, 0:T], B3[:, :, 0:T], AF.Exp, K8("B"), K8("B"), scale=-1.0)
            tt("dve", KT3[:, :, 0:T], SG3[:, :, 0:T], B3[:, :, 0:T], ALU.mult, K8("SG") + K8("B"), ["KT"])

        def c_F4b():
            actf(B3[:, :, 0:T], FB3[:, :, 0:T], AF.Silu, ["FB"], K8("B"))
            tt("dve", QT3[:, :, 0:T], B3[:, :, 0:T], F3[:, :, 0:T], ALU.mult, K8("B") + K8("F"), ["QT"])
            if kind == "p":
                dvop(lambda e: e.tensor_copy(out=small[:, 32:40], in_=F3[:, :, T - 1]), K8("F"), ["glast"], 8)
            else:
                glS = V(vA, "S", F32, 16)
                dvop(lambda e: e.tensor_copy(out=glS[:, 0:8, :], in_=F3[:, :, 48:64]), K8("F"), ["S"], 128)

        def c_F3():
            tm(XB, 2048, "XB")
            actf(ibf[0:T, :], XB[0:T, :], AF.Copy, ["XB"], ["ibf"])
            tm(TA, 2560, "TA")
            actf(sog[0:T, :], TA[0:T, :], AF.Silu, ["TA"], ["sog"])
            s3 = sog[0:T, :].rearrange("p (h d) -> p h d", d=64)
            tt("dve", s3, s3, _bc(ghn_bc[0:T, :], 1, 8), ALU.mult, ["sog", "ghn_bc"], ["sog"])

        def c_F5():
            tm(XA, 3584, "XA")
            actf(gv[0:T, :], XA[0:T, :], AF.Gelu_apprx_tanh, ["XA"], ["gv"])
            dvop(lambda e: e.bn_stats(out=small[0:T, 40:46], in_=gv[0:T, :]), ["gv"], ["bst"], 512)
            dvop(lambda e: e.bn_aggr(out=small[0:T, 46:48], in_=small[0:T, 40:46]), ["bst"], ["mv"], 8)
            rms_rstd(small[0:T, 47:48], sm(48, 1, T), sm(49, 1, T), 1.0, ["mv"], "rsv")
            stt(sm(63, 1, T), small[0:T, 46:47], -1.0, sm(49, 1, T), ALU.mult, ALU.mult, ["mv", "rsv"], ["nmr"])
            actf(gv[0:T, :], gv[0:T, :], AF.Identity, ["gv", "rsv", "nmr"], ["gv"], scale=sm(49, 1, T),
                 bias=sm(63, 1, T))
            tt("dve", gv[0:T, :], gv[0:T, :], lng_bc[0:T, :], ALU.mult, ["gv", "lng_bc"], ["gv"])
            if kind == "p":
                tt("dve", vnb[0:T, :], gv[0:T, :], lnb_bc[0:T, :], ALU.add, ["gv", "lnb_bc"], ["vnb"])
            else:
                tt("dve", gv[0:T, :], gv[0:T, :], lnb_bc[0:T, :], ALU.add, ["gv", "lnb_bc"], ["gv"])
                dma("sp", vso[:, :], gv[0:T, :], ["gv"], ["vso"], "vs")
                actf(vnb[0:T, :], gv[0:T, :], AF.Copy, ["gv"], ["vnb"])
            fm(XB3, 3072, 4, "XB")
            actf(guT3[:, 0:4, 0:T], XB3[:, 0:4, 0:T], AF.Gelu_apprx_tanh, ["XB"], ["guT"])

        def gates():
            for cb in range(2):
                tm(FA[:, cb * 512:(cb + 1) * 512], 4096 + cb * 512, "FA")
            actf(GAt[0:T, :], FA[0:T, :], AF.Sigmoid, ["FA"], ["GA"])
            for cb in range(2):
                tm(FB[:, cb * 512:(cb + 1) * 512], 5120 + cb * 512, "FB")
            actf(GBt[0:T, :], FB[0:T, :], AF.Sigmoid, ["FB"], ["GB"])

        def c_B1():
            if kind == "s":
                gates()
            transposes([(XBb[0:T, h * 128:(h + 1) * 128], KT3[:, h, 0:T], identb[:, :]) for h in range(8)],
                       ["KT", "identb"], ["XB"])
            actf(Ktok[0:T, :], XBb[0:T, :], AF.Copy, ["XB"], ["Ktok"])

        def c_B2():
            if kind == "p":
                mms([x for h in range(8) for x in
                     ((FA3[0:128, h, 64:128], KT3[:, h, 0:128], QT3[:, h, 64:128], True, True),
                      (FA3[0:64, h, 0:64], KT3[:, h, 0:64], QT3[:, h, 0:64], True, True))],
                    ["KT", "QT"], ["FA"])
                P.urg = URG
                for hb in range(2):
                    hs = slice(4 * hb, 4 * hb + 4)
                    tt("dve", scT3[:, hs, 64:128], FA3[:, hs, 64:128], _bc(maskP[:, 64:128], 1, 4), ALU.mult,
                       ["FA", "maskP"], ["scT"])
                    tt("dve", scT3[0:64, hs, 0:64], FA3[0:64, hs, 0:64], _bc(maskP[0:64, 0:64], 1, 4), ALU.mult,
                       ["FA", "maskP"], ["scT"])
                tt("dve", S3, S3, _bc(small[:, 24:32], 2, 64), ALU.mult, ["S", "eb63"], ["S"])
                actf(Spb, S_, AF.Copy, ["S"], ["Spb"])
                P.urg = 0.0
            else:
                mms([(FA3[0:T, h, 0:T], KT3[:, h, 0:T], QT3[:, h, 0:T], True, True) for h in range(8)],
                    ["KT", "QT"], ["FA"])
                for hb in range(2):
                    hs = slice(4 * hb, 4 * hb + 4)
                    tt("dve", scT3[0:T, hs, 0:T], FA3[0:T, hs, 0:T], _bc(maskS[:, :], 1, 4), ALU.mult,
                       ["FA", "maskS"], ["scT"])

        def c_B3():
            if kind == "p":
                mms([x for h in range(8) for x in
                     ((TA[0:T, h * 64:(h + 1) * 64], scT3[0:T, h, 0:T], ibf[0:T, h * 64:(h + 1) * 64], True, False),
                      (TA[0:T, h * 64:(h + 1) * 64], QT3[:, h, 0:T], Spb[:, h * 64:(h + 1) * 64], False, True))],
                    ["scT", "ibf", "QT", "Spb"], ["TA"])
                actf(osb[0:T, :], TA[0:T, :], AF.Copy, ["TA"], ["osb"])
                mms([(TB[:, h * 64:(h + 1) * 64], Ktok[0:T, h * 128:(h + 1) * 128], ibf[0:T, h * 64:(h + 1) * 64],
                      True, True) for h in range(8)], ["Ktok", "ibf"], ["TB"])
                tt("dve", S_, TB[:, :], S_, ALU.add, ["TB", "S"], ["S"])
                tt("dve", S3, S3, _bc(small[:, 32:40], 2, 64), ALU.mult, ["S", "glast"], ["S"])
                if ti == NPT - 1:
                    dma("sp", spo.rearrange("h k d -> k h d"), S3, ["S"], ["spo"], "spo")
            else:
                mms([(TA[0:T, h * 64:(h + 1) * 64], scT3[0:T, h, 0:T], ibf[0:T, h * 64:(h + 1) * 64], True, True)
                     for h in range(8)], ["scT", "ibf"], ["TA"])
                glS = V(vA, "S", F32, 16)
                def half(name, i_):
                    o_ = vA[name][0] + 512 * i_
                    return arena[:, o_:o_ + 512].bitcast(BF16)
                S0b = [half("SG", 0), half("SG", 1), half("B", 0), half("B", 1), half("xt0", 0), half("xt0", 1),
                       V(vA, "KT", BF16), V(vA, "gv", BF16)]
                S0bk = [[("SG", h_) for h_ in range(4)], [("SG", h_) for h_ in range(4, 8)],
                        [("B", h_) for h_ in range(4)], [("B", h_) for h_ in range(4, 8)], ["xt0"], ["xt0"],
                        ["KT"], ["gv"]]
                NSB = len(S0b)

                def sload(h_):
                    dma("pool", S0b[h_ % NSB].rearrange("p (s d) -> p s d", d=64),
                        st0[:, h_, :, :].rearrange("s k d -> k s d"), [], S0bk[h_ % NSB], "sl%d" % (h_ % NSB))
                seltmp = arena[0:64, vA["scT"][0]:vA["scT"][0] + 1024]
                selk = ["scT", "R1"]
                dma("sp", sK_d[:, :], Ktok[0:T, :], ["Ktok"], ["sK_d"], "sv0")
                dma("sp", sI_d[:, :], ibf[0:T, :], ["ibf"], ["sI_d"], "sv1")
                dma("sp", sG_d[:, :], V(vA, "S")[:, 0:128], ["S"], ["sG_d"], "sv2")
                for h in range(NSB):
                    sload(h)
                for h in range(8):
                    b = h % NSB
                    PB_, pbk_ = (FA, "FA") if h % 2 == 0 else (FB, "FB")
                    mms([(PB_[0:T, 0:512], QT3[:, h, 0:T], S0b[b][:, 0:512], True, True),
                         (PB_[0:T, 512:1024], QT3[:, h, 0:T], S0b[b][:, 512:1024], True, True)],
                        ["QT"] + S0bk[b], [pbk_])
                    if h + NSB < 8:
                        sload(h + NSB)
                    for half in range(2):
                        cs_ = slice(512 * half, 512 * half + 512)
                        tt("dve", seltmp[:, cs_].rearrange("p (s d) -> p s d", d=64),
                           PB_[0:T, cs_].rearrange("p (s d) -> p s d", d=64),
                           _bc(selS[:, 8 * half:8 * half + 8], 2, 64), ALU.mult, [pbk_, "selS"], selk)
                    dvop(lambda e, h=h: e.tensor_reduce(
                        out=osb[0:T, h * 64:(h + 1) * 64], in_=seltmp.rearrange("p (s d) -> p d s", d=64),
                        axis=AX.X, op=ALU.add), selk, [("osbS", h)], 1024)
                tt("dve", osb[0:T, :], TA[0:T, :], osb[0:T, :], ALU.add, ["TA"] + [("osbS", h) for h in range(8)],
                   ["osb"])

        def c_B4():
            osq = R1[0:T, 0:512]
            tt("dve", osq, osb[0:T, :], osb[0:T, :], ALU.mult, ["osb"], ["R1"])
            dvop(lambda e: e.tensor_reduce(out=small[0:T, 50:58], in_=osq.rearrange("p (h d) -> p h d", d=64),
                                           axis=AX.X, op=ALU.add), ["R1"], ["ssq"], 512)
            rms_rstd(small[0:T, 50:58], small[0:T, 64:72], small[0:T, 72:80], 64.0, ["ssq"], "r8")
            o3 = osb[0:T, :].rearrange("p (h d) -> p h d", d=64)
            tt("dve", o3, o3, _bc(small[0:T, 72:80], 2, 64), ALU.mult, ["osb", "r8"], ["osb"])
            tt("dve", oab[0:T, :], osb[0:T, :], sog[0:T, :], ALU.mult, ["osb", "sog"], ["oab"])
            transposes([(XAb3[:, c, 0:T], oab[0:T, c * 128:(c + 1) * 128], identb[0:T, 0:T]) for c in range(4)],
                       ["oab", "identb"], ["XA"])
            actf(oaT3[:, 0:4, 0:T], XAb3[:, 0:4, 0:T], AF.Copy, ["XA"], ["oaT"])

        def c_B5():
            wT = wTp if kind == "p" else wTs
            bsr = bsP if kind == "p" else bsS
            mms([x for g in range(4) for x in
                 ((TB3[:, g, 0:T], vnb[0:T, g * 128:(g + 1) * 128], wT[0:T, g, 0:T], True, False),
                  (TB3[:, g, 0:T], onesf[0:1, :], bsr[0:1, g, 0:T], False, True))],
                ["vnb", "wTp", "onesf", "bsP", "bsS"] + wTs_keys, ["TB"])
            tt("dve", obT3[:, 0:4, 0:T], TB3[:, 0:4, 0:T], guT3[:, 0:4, 0:T], ALU.mult, ["TB", "guT"], ["obT"])
            if kind == "p":
                gates()
            for cb in range(2):
                mms([(FA[0:T, cb * 512:(cb + 1) * 512], oaT3[:, kc, 0:T], Wpa(kc, cb * 512, 512), kc == 0, kc == 3)
                     for kc in range(4)], ["oaT"] + wk("w_pa", 4, 0, 1024), ["FA"])
            for cb in range(2):
                mms([(FB[0:T, cb * 512:(cb + 1) * 512], obT3[:, kc, 0:T], Wpb(kc, cb * 512, 512), kc == 0, kc == 3)
                     for kc in range(4)], ["obT"] + wk("w_pb", 4, 0, 1024), ["FB"])

        def c_B6():
            tt("dve", GAt[0:T, :], FA[0:T, :], GAt[0:T, :], ALU.mult, ["FA", "GA"], ["GA"])
            tt("dve", GBt[0:T, :], FB[0:T, :], GBt[0:T, :], ALU.mult, ["FB", "GB"], ["GB"])
            htok = R1.bitcast(BF16)
            tt("dve", htok[0:T, :], GAt[0:T, :], GBt[0:T, :], ALU.add, ["GA", "GB"], ["R1"])
            transposes([(XAb3[:, c, 0:T], htok[0:T, c * 128:(c + 1) * 128], identb[0:T, 0:T]) for c in range(8)],
                       ["R1", "identb"], ["XA"])
            actf(hTb3[:, :, 0:T], XAb3[:, :, 0:T], AF.Copy, ["XA"], ["hTb"])
            for cb in range(2):
                mms([(FA[0:T, cb * 512:(cb + 1) * 512], hTb3[:, kc, 0:T], Wo(kc, cb * 512, 512), kc == 0, kc == 7)
                     for kc in range(8)], ["hTb"] + wk("w_o", 8, 0, 1024), ["FA"])
            for cb in range(2):
                actf(junkA[0:T, cb * 512:(cb + 1) * 512], FA[0:T, cb * 512:(cb + 1) * 512], AF.Square, ["FA"],
                     ["oab", "oaT", ("ssm", cb)], accum=sm(58 + cb, 1, T))
            tt("dve", sm(60, 1, T), sm(58, 1, T), sm(59, 1, T), ALU.add, [("ssm", 0), ("ssm", 1)], ["ssmt"])
            rms_rstd(sm(60, 1, T), sm(61, 1, T), sm(62, 1, T), D, ["ssmt"], "rm")
            for cb in range(2):
                cs_ = slice(cb * 512, (cb + 1) * 512)
                stt(R1[0:T, :], FA[0:T, cs_], sm(62, 1, T), GP_bc[0:T, cs_], ALU.mult, ALU.mult,
                    ["FA", "rm", "GP_bc"], ["R1"])
                tt("dve", xrb[0:T, cs_], R1[0:T, :], xrb[0:T, cs_], ALU.add, ["R1", xrk], [xrk])
            dma("sp", x1s[row0:row0 + T, :], xrb[0:T, :], [xrk], [("x1s", ti if kind == "p" else NPT)], "x1st")

        C.update(F1=c_F1, F2=c_F2, F4=c_F4, F4b=c_F4b, F3=c_F3, F5=c_F5, B1=c_B1, B2=c_B2, B3=c_B3, B4=c_B4, B5=c_B5, B6=c_B6)
        return C

    def build_gate_weights():
        wraw = V(vA, "GA", F32, 128)
        for g in range(4):
            pdma(wraw[:, g, :], w_s[g], ["GA"])
        transposes([(XB3[:, g, :], wraw[:, g, :], identf[:, :]) for g in range(4)],
                   ["GA"] + ["identf"], ["XB"], after=["QT"])
        tt("dve", wTp[:, :, :], XB3[:, 0:4, :], _bc(maskP[:, :], 1, 4), ALU.mult, ["XB", "maskP"], ["wTp"])
        for g in range(4):
            mms([(XB[0:4, 0:64], w4[:, g, :], selE[:, :], True, True)], ["w4", "selE"], ["XB"], after=["QT"])
            actf(m1[:, :], XB[0:4, 0:64], AF.Copy, ["XB"], ["m1"])
            mms([(XB[0:64, 64:128], selE[:, :], m1[:, :], True, True)], ["m1", "selE"], ["XB"])
            tt("dve", wTs[:, g, :], XB[0:64, 64:128], maskS[:, :], ALU.mult, ["XB", "maskS"], [("wTs", g)])

    URG = 0.0
    ORDER = ["F1", "B1", "F2", "B2", "F4", "B3", "F4b", "B4", "F3", "B5", "F5", "B6"]
    tiles = [make_tile(ti, "p") for ti in range(NPT)]
    tileS = make_tile(0, "s")
    for rnd in range(NPT + 1):
        fr = tiles[rnd] if rnd < NPT else None
        bk = tiles[rnd - 1] if rnd >= 1 else None
        if rnd == 1:
            P.tag = "gatew"
            build_gate_weights()
        if bk is not None:
            P.tag = "xr"
            xload(rnd - 1, "p", 1)
        for c in ORDER:
            P.tag = c
            if c[0] == "F" and fr is not None:
                fr[c]()
                if c == "F1":
                    P.tag = "xl"
                    if rnd + 1 < NPT:
                        xload(rnd + 1, "p", 0)
                    elif rnd + 1 == NPT:
                        xload(0, "s", 0)
            if c[0] == "B" and bk is not None:
                bk[c]()
    P.tag = "xr"
    xload(0, "s", 1)
    for c in ORDER:
        if c[0] == "F":
            P.tag = "s" + c
            tileS[c]()
    for c in ORDER:
        if c[0] == "B":
            P.tag = "s" + c
            tileS[c]()

    P.tag = "preS0"
    xtmp = arena[:, vA["xnT0"][0]:vA["xnT0"][0] + 1024]
    xn2pre = V(vA, "F", BF16, 256)
    for sub in range(2):
        dma("sp", xtmp, x1s[sub * 128:(sub + 1) * 128, :], [("x1s", sub)], ["xnT0", "xnT1"], "pre")
        actf(xsb[:, :], xtmp, AF.Square, ["xnT0", "xnT1"], ["xsb", "ss"], accum=sm(0, 1, 128))
        rms_rstd(sm(0, 1, 128), sm(1, 1, 128), sm(2, 1, 128), D, ["ss"], "rstd")
        ts("dve", xsb[:, :], xtmp, sm(2, 1, 128), None, ALU.mult, None, ["xnT0", "xnT1", "rstd"], ["xsb"])
        transposes([(XAb3[:, kc, :], xsb[:, kc * 128:(kc + 1) * 128], identb[:, :]) for kc in range(8)],
                   ["xsb", "identb"], ["XA"])
        tt("dve", xn2pre[:, :, sub * 128:(sub + 1) * 128], XAb3[:, :, :], _bc(prmv(GFFNT), 2, 128), ALU.mult,
           ["XA", "prmT"], K8("F"))
    P.tag = "wup"
    load_w(Wup, wupb, 8, DUP, 1408, "w_up", order=[0, 2, 1, 3], off=0, kstride=DUP, overlay=True,
           late=lambda k_: k_[0] == "w_in" and k_[1] >= 4, rkey=lambda kc: ("wupb", kc), bpe=2)
    P.barrier(keep=("w_up", "wdnb"))
    load_w(Wdn, wdnb, NJ, D, 1024, "w_dn", off=OFF_DN, kstride=D, overlay=True, own_chan=True, bpe=2,
           rkey=lambda kc: ("wdnb", kc // 2))
    pdma(GP_bc[:, :], gpost2.partition_broadcast(128), "GP_bc")

    TBP = 256
    NSUP = SEQ // TBP
    x1t = [V(vB, "x1t%d" % i) for i in range(3)]
    xrot = [0]

    def getx():
        k = xrot[0] % 3
        xrot[0] += 1
        return x1t[k], "x1t%d" % k, "ldB%d" % k

    xsbB = V(vB, "xsb", BF16)
    xn2 = [V(vB, "xnT0", BF16, TBP), V(vB, "xnT1", BF16, TBP)]
    upx = [V(vB, "upx%d" % i, F32, 264) for i in range(2)]
    cg = [V(vB, "cg0"), V(vB, "cg1")]
    cv = [V(vB, "cv0"), V(vB, "cv1")]
    hT3 = V(vB, "hT", BF16, TBP)
    ybuf = V(vB, "ybuf")
    tmu = [V(vB, "tmu0", parts=32), V(vB, "tmu1", parts=32)]
    carryP = V(vB, "carryP")[:, 0:88].rearrange("p (r g j) -> p g j r", g=2, r=2)
    carryS = V(vB, "carryS").rearrange("p (g j r) -> p g j r", g=2, r=32)
    P.op("dve", lambda e: e.memset(V(vB, "carryP"), 0.0), [], ["carryP"])
    sS0h = V(vB, "carryS")[:, 0:1024].rearrange("p (s d) -> p s d", d=64)
    sgl = V(vB, "carryS")[:, 1024:1152].rearrange("p (h s) -> p h s", s=16)
    simask = V(vB, "tmu0", BF16, parts=64)
    sKtok = V(vB, "tmu1", BF16, parts=64)
    sibf = V(vB, "sibf", BF16, parts=64)
    dma("sp", sKtok[:, :], sK_d[:, :], [], ["tmu1"], "sv0")
    dma("sp", sibf[:, :], sI_d[:, :], [], ["sibf"], "sv1")
    dma("sp", V(vB, "carryS")[:, 1024:1152], sG_d[:, :], [], [("carryS", "g")], "sv2")

    def state_update(h):
        dma("sp", sS0h, st0[:, h, :, :].rearrange("s k d -> k s d"), [], ["carryS"], "st0")
        tt("dve", simask.rearrange("p (s d) -> p s d", d=64), _bc(sibf[:, h * 64:(h + 1) * 64], 1, 16),
           _bc(selS[:, :], 2, 64), ALU.mult, ["sibf", "selS"], ["tmu0"])
        mms([(TA[:, :], sKtok[:, h * 128:(h + 1) * 128], simask[:, 0:512], True, True)], ["tmu1", "tmu0"], [("T2", 0)])
        mms([(TB[:, :], sKtok[:, h * 128:(h + 1) * 128], simask[:, 512:1024], True, True)], ["tmu1", "tmu0"],
            [("T2", 1)])
        S0f = sS0h.rearrange("p s d -> p (s d)")
        tt("dve", S0f[:, 0:512], TA[:, :], S0f[:, 0:512], ALU.add, [("T2", 0), "carryS"], ["carryS"])
        tt("dve", S0f[:, 512:1024], TB[:, :], S0f[:, 512:1024], ALU.add, [("T2", 1), "carryS"], ["carryS"])
        tt("dve", sS0h, sS0h, _bc(sgl[:, h, :], 2, 64), ALU.mult, ["carryS", ("carryS", "g")], ["carryS"])
        dma("sp", sso[:, h, :, :].rearrange("s k d -> k s d"), sS0h, ["carryS"], [("sso", h)], "so0")

    def cache_prologue():
        cbuf = V(vB, "tmu1", parts=32)
        for c in range(11):
            dma("sp", cbuf[:, 0:512], cch[:, c * 512:(c + 1) * 512], [], ["tmu1"], "cch")
            transposes([(XB[:, b * 32:(b + 1) * 32], cbuf[:, b * 128:(b + 1) * 128], identf[0:32, 0:32])
                        for b in range(4)], ["tmu1", "identf"], ["XB"])
            dst = V(vB, "carryS")[:, c * 128:(c + 1) * 128]
            actf(dst, XB[:, 0:128], AF.Copy, ["XB"], ["carryS"])

    def geo(kind):
        if kind == "p":
            return TBP, 2, 1, 2, 128
        return TS, 32, 16, 1, TS

    def S0(n, kind):
        T, PV, sh, nsub, Tt = geo(kind)
        xn = xn2[n % 2]
        xnk = "xnT%d" % (n % 2)
        for sub in range(nsub):
            row0 = n * TBP + sub * 128 if kind == "p" else SEQ
            xb_, xk, xc = getx()
            dma("sp", xb_[0:Tt, :], x1s[row0:row0 + Tt, :], [], [xk], xc)
            actf(xsbB[0:Tt, :], xb_[0:Tt, :], AF.Square, [xk], ["xsb", "ss"], accum=sm(0, 1, Tt))
            rms_rstd(sm(0, 1, Tt), sm(1, 1, Tt), sm(2, 1, Tt), D, ["ss"], "rstd")
            ts("dve", xsbB[0:Tt, :], xb_[0:Tt, :], sm(2, 1, Tt), None, ALU.mult, None, [xk, "rstd"], ["xsb"])
            transposes([(XAb3[:, kc, 0:Tt], xsbB[0:Tt, kc * 128:(kc + 1) * 128], identb[0:Tt, 0:Tt])
                        for kc in range(8)], ["xsb", "identb"], ["XA"])
            tt("dve", xn[:, :, sub * 128:sub * 128 + Tt], XAb3[:, :, 0:Tt], _bc(prmv(GFFNT), 2, Tt), ALU.mult,
               ["XA", "prmT"], [(xnk, sub)])

    pbanks = [(FA[:, 0:512], ("FA", 0)), (FA[:, 512:1024], ("FA", 1)),
              (FB[:, 0:512], ("FB", 0)), (FB[:, 512:1024], ("FB", 1))]

    def pbank(j):
        bk, bkey = pbanks[j % 4]
        return bk.rearrange("p (g t) -> p g t", g=2), bkey

    def stA(n, kind, j):
        T, PV, sh, nsub, Tt = geo(kind)
        bk3, bkey = pbank(j)
        xn = xn2[n % 2]
        xnk = "xnT%d" % (n % 2)
        for g2 in range(2):
            c0 = g2 * DFF + j * 128
            mms([(bk3[:, g2, 0:T], Wup(kc, c0, 128), xn[:, kc, 0:T], kc == 0, kc == 7) for kc in range(8)],
                [(xnk, 0), (xnk, 1)] + wk("w_up", 8, c0, 128, 1408), [bkey])

    def stB(n, kind, j):
        T, PV, sh, nsub, Tt = geo(kind)
        bk3, bkey = pbank(j)
        u = j % 2
        ux, uk = upx[u], "upx%d" % u
        carry = carryP if kind == "p" else carryS
        ckey = "carryP" if kind == "p" else "carryS"
        actf(ux[:, :, PV:PV + T], bk3[:, :, 0:T], AF.Copy, [bkey], [uk])
        P.op("pool", lambda e: e.tensor_copy(out=ux[:, :, 0:PV], in_=carry[:, :, j, :]), [ckey], [(uk, "c")])
        if kind == "p":
            P.op("pool", lambda e: e.tensor_copy(out=carry[:, :, j, :], in_=ux[:, :, T:T + PV]), [uk], [ckey])
        for g2, acc, ak in ((0, cg[u], "cg%d" % u), (1, cv[u], "cv%d" % u)):
            bi = g2 * NJ + j
            actf(acc[:, 0:T], bk3[:, g2, 0:T], AF.Identity, [bkey, "prmT"], [ak], bias=convb_v(bi), scale=convw_v(2, bi))

    def stC(n, kind, j):
        T, PV, sh, nsub, Tt = geo(kind)
        u = j % 2
        ux, uk = upx[u], "upx%d" % u
        for g2, acc, ak in ((0, cg[u], "cg%d" % u), (1, cv[u], "cv%d" % u)):
            bi = g2 * NJ + j
            a = acc[:, 0:T]
            stt(a, ux[:, g2, 0:T], convw_v(0, bi), a, ALU.mult, ALU.add, [uk, (uk, "c"), "prmTB", ak], [ak])
            stt(a, ux[:, g2, sh:sh + T], convw_v(1, bi), a, ALU.mult, ALU.add, [uk, (uk, "c"), "prmTB", ak], [ak])

    def stD(n, kind, j):
        T, PV, sh, nsub, Tt = geo(kind)
        u = j % 2
        actf(cg[u][:, 0:T], cg[u][:, 0:T], AF.Gelu_apprx_tanh, ["cg%d" % u], ["cg%d" % u])
        tt("pool", hT3[:, j, 0:T], cg[u][:, 0:T], cv[u][:, 0:T], ALU.mult, ["cg%d" % u, "cv%d" % u], [("hT", j)])

    XAf = XA[:, :]

    def ytail(n, kind):
        T, PV, sh, nsub, Tt = geo(kind)
        phs = [[(TA, ("T2", 0)), (TB, ("T2", 1))], [(XAf, "XA"), (XB[:, :], "XB")]]
        for kc in range(NJ):
            for sub in range(nsub):
                for cb in range(2):
                    mms([(phs[sub][cb][0][0:Tt, :], hT3[:, kc, sub * 128:sub * 128 + Tt], Wdn(kc, cb * 512, 512),
                          kc == 0, kc == NJ - 1)], [("hT", kc), ("w_dn", 0, kc)], [phs[sub][cb][1]])
        for sub in range(nsub):
            row0 = n * TBP + sub * 128 if kind == "p" else SEQ
            dst = yp[row0:row0 + Tt, :] if kind == "p" else ys[:, :]
            ph = phs[sub]
            xb_, xk, xc = getx()
            dma("sp", xb_[0:Tt, :], x1s[row0:row0 + Tt, :], [], [xk], xc)
            for cb in range(2):
                actf(xsbB[0:Tt, cb * 512:(cb + 1) * 512], ph[cb][0][0:Tt, :], AF.Square, [ph[cb][1]],
                     ["xsb", ("ssm", cb)], accum=sm(58 + cb, 1, Tt))
            tt("dve", sm(60, 1, Tt), sm(58, 1, Tt), sm(59, 1, Tt), ALU.add, [("ssm", 0), ("ssm", 1)], ["ssmt"])
            rms_rstd(sm(60, 1, Tt), sm(61, 1, Tt), sm(62, 1, Tt), D, ["ssmt"], "rm")
            for cb in range(2):
                cs_ = slice(cb * 512, (cb + 1) * 512)
                stt(ybuf[0:Tt, cs_], ph[cb][0][0:Tt, :], sm(62, 1, Tt), GP_bc[0:Tt, cs_], ALU.mult, ALU.mult,
                    [ph[cb][1], "rm", "GP_bc"], ["ybuf"])
            tt("pool", ybuf[0:Tt, :], ybuf[0:Tt, :], xb_[0:Tt, :], ALU.add, ["ybuf", xk], ["ybuf"])
            dma("sp", dst, ybuf[0:Tt, :], ["ybuf"], [("y", n, sub)], "yst")

    def pair_loop(n, kind, nxt):
        for s_ in range(NJ + 3):
            if 0 <= s_ - 3 < NJ:
                P.tag = "stD"
                stD(n, kind, s_ - 3)
            if 0 <= s_ - 2 < NJ:
                P.tag = "stC"
                stC(n, kind, s_ - 2)
            if 0 <= s_ - 1 < NJ:
                P.tag = "stB"
                stB(n, kind, s_ - 1)
            if s_ < NJ:
                P.tag = "stA"
                stA(n, kind, s_)
            if s_ == 6 and nxt is not None:
                P.tag = "S0"
                S0(*nxt)

    for n in range(NSUP):
        nxt = (n + 1, "p") if n + 1 < NSUP else (NSUP, "s")
        pair_loop(n, "p", nxt)
        P.tag = "ytail"
        ytail(n, "p")
        if n < 4:
            P.tag = "state"
            state_update(2 * n)
            state_update(2 * n + 1)
        if n == 3:
            P.tag = "cache"
            cache_prologue()
    transposes([(XB[0:88, 0:128], V(vB, "carryP")[:, 0:88], identf[:, :])], ["carryP", "identf"], ["XB"])
    cpT = V(vB, "tmu0")
    actf(cpT[0:88, 0:128], XB[0:88, 0:128], AF.Copy, ["XB"], ["tmu0"])
    for r_ in range(2):
        dma("sp", cpo[r_].rearrange("(b p) -> b p", p=128), cpT[r_ * 44:(r_ + 1) * 44, 0:128], ["tmu0"],
            [("cpo", r_)], "cpo")
    pair_loop(NSUP, "s", None)
    xnS = xn2[NSUP % 2]
    xnSk = "xnT%d" % (NSUP % 2)
    for cb in range(11):
        tb = cb % 2
        pbk = TA if tb == 0 else TB
        mms([(pbk[0:32, :], xnS[:, kc, 32:64], Wup(kc, cb * 512, 512), kc == 0, kc == 7) for kc in range(8)],
            [(xnSk, 0)] + wk("w_up", 8, cb * 512, 512, 1408), [("T2", tb)])
        actf(tmu[tb][:, :], pbk[0:32, :], AF.Copy, [("T2", tb)], ["tmu%d" % tb])
        dma("sp", cso[:, cb * 512:(cb + 1) * 512], tmu[tb][:, :], ["tmu%d" % tb], [("cso", cb)], "cso%d" % tb)
    ytail(NSUP, "s")
    P.finish()

    with nc.allow_non_contiguous_dma(reason="small parameter / state layouts"):
        P.emit()
    es.close()
    return nc, P


def _host_consts():
    ident = np.eye(128, dtype=np.float32)
    s = np.arange(128)
    maskP = (s[:, None] <= s[None, :]).astype(np.float32)
    a = np.arange(64)
    maskS = ((a[:, None] % 16 == a[None, :] % 16) & (a[:, None] // 16 <= a[None, :] // 16)).astype(np.float32)
    selS = (a[:, None] % 16 == np.arange(16)[None, :]).astype(np.float32)
    selE = (np.arange(4)[:, None] == a[None, :] // 16).astype(np.float32)
    return ident, maskP, maskS, selS, selE


_CACHE = {}


def kernel(x_prompt, x_sample, state_hgrn, cache_ffn_conv, lb_param, mix_pre_g, w_in, hgrn_norm_g,
           gmlp_ln_g, gmlp_ln_b, w_s, b_s, w_pa, w_pb, w_o, mix_post_g, ffn_pre_g, w_up, conv_w,
           conv_b, w_down, ffn_post_g):
    f = lambda a: np.ascontiguousarray(np.asarray(a, dtype=np.float32))
    if "nc" not in _CACHE:
        _CACHE["nc"] = build()[0]
    nc = _CACHE["nc"]
    ident, maskP, maskS, selS, selE = _host_consts()
    shared = {
        "lbp": f(lb_param), "gpre": f(mix_pre_g)[0], "w_in": f(w_in)[0], "ghn": f(hgrn_norm_g)[0],
        "lng": f(gmlp_ln_g)[0], "lnb": f(gmlp_ln_b)[0], "w_s": f(w_s)[0], "b_s": f(b_s)[0],
        "w_pa": f(w_pa)[0], "w_pb": f(w_pb)[0], "w_o": f(w_o)[0], "gpost": f(mix_post_g)[0],
        "gffn": f(ffn_pre_g)[0], "w_up": f(w_up)[0], "conv_w": f(conv_w)[0], "conv_b": f(conv_b)[0],
        "w_dn": f(w_down)[0], "gpost2": f(ffn_post_g)[0],
        "ident": ident, "maskP": maskP, "maskS": maskS, "selS": selS, "selE": selE,
    }
    x_prompt, x_sample = f(x_prompt), f(x_sample)
    state_hgrn, cache_ffn_conv = f(state_hgrn), f(cache_ffn_conv)
    in_maps = []
    for c in range(NCORES):
        m = dict(shared)
        m["xp"] = x_prompt[c]
        m["xs"] = np.ascontiguousarray(x_sample[c * SB:(c + 1) * SB].transpose(1, 0, 2).reshape(TS, D))
        m["st0"] = state_hgrn[0, c * SB:(c + 1) * SB]
        m["cch"] = np.ascontiguousarray(cache_ffn_conv[0, c * SB:(c + 1) * SB].transpose(1, 0, 2).reshape(32, DUP))
        in_maps.append(m)
    res = run_bass_kernel_spmd(nc, in_maps, core_ids=list(range(NCORES)))
    R = res.results
    yp = np.stack([R[c]["yp"] for c in range(NCORES)], 0)
    ys = np.concatenate([R[c]["ys"].reshape(4, SB, D).transpose(1, 0, 2) for c in range(NCORES)], 0)
    sp = np.stack([R[c]["spo"] for c in range(NCORES)], 0)[None]
    ss = np.concatenate([R[c]["sso"] for c in range(NCORES)], 0)[None]
    cp = np.stack([R[c]["cpo"] for c in range(NCORES)], 0)[None]
    cs = np.concatenate([R[c]["cso"].reshape(2, SB, DUP).transpose(1, 0, 2) for c in range(NCORES)], 0)[None]
    vs = np.concatenate([R[c]["vso"].reshape(4, SB, 512).transpose(1, 0, 2) for c in range(NCORES)], 0)[None]
    return (np.ascontiguousarray(yp, dtype=np.float32), np.ascontiguousarray(ys, dtype=np.float32),
            np.ascontiguousarray(sp, dtype=np.float32), np.ascontiguousarray(ss, dtype=np.float32),
            np.ascontiguousarray(cp, dtype=np.float32), np.ascontiguousarray(cs, dtype=np.float32),
            np.ascontiguousarray(vs, dtype=np.float32))
```

```python
import numpy as np
from contextlib import ExitStack

import concourse.bass as bass
import concourse.mybir as mybir
from concourse.bass_utils import run_bass_kernel_spmd

F32 = mybir.dt.float32
BF16 = mybir.dt.bfloat16
AF = mybir.ActivationFunctionType
ALU = mybir.AluOpType
AX = mybir.AxisListType

NCORES = 8
D = 1024
SEQ = 2048
NPT = SEQ // 128
SB = 16
TS = 64
DIN = 6144
DFF = 2816
DUP = 2 * DFF
NJ = DFF // 128
EPS = 1e-6
WELEMS = 67584


class Op:
    __slots__ = ("eng", "fn", "edges", "deps", "chan", "sig", "need", "busy", "lat", "idx", "succ", "prio",
                 "start", "nun", "tag", "aset", "xedges", "urg")


class Prog:
    ENG = ("pe", "act", "dve", "pool", "sp")
    QUANT = 0.01

    def __init__(self, nc):
        self.nc = nc
        self.regions = [[]]
        self.lastw = {}
        self.readers = {}
        self.chan_last = {}
        self.chan_cnt = {}

    def op(self, eng, fn, r=(), w=(), chan=None, busy=0.3, lat=None, after=(), aset=0):
        o = Op()
        o.eng, o.fn, o.chan, o.need, o.sig = eng, fn, chan, False, None
        o.busy = busy
        o.lat = busy if lat is None else lat
        o.tag = getattr(self, "tag", "")
        o.aset = aset
        o.urg = getattr(self, "urg", 0.0)
        edges, seen = [], set()

        def add(d, is_war):
            if d is None or d is o or id(d) in seen:
                return
            seen.add(id(d))
            edges.append((d, is_war))
        for b in r:
            add(self.lastw.get(b), False)
        for b in w:
            add(self.lastw.get(b), False)
        if chan is not None:
            add(self.chan_last.get(chan), False)
            self.chan_last[chan] = o
        for b in w:
            for d in self.readers.get(b, ()):
                add(d, True)
        for b in after:
            add(self.lastw.get(b), False)
            for d in self.readers.get(b, ()):
                add(d, True)
        o.edges = edges
        for b in r:
            self.readers.setdefault(b, []).append(o)
        for b in w:
            self.lastw[b] = o
            self.readers[b] = []
        self.regions[-1].append(o)
        return o

    def barrier(self, keep=()):
        self.regions.append([])
        kept = {k: v for k, v in self.lastw.items() if v.chan is not None and (k in keep or (isinstance(k, tuple) and k[0] in keep))}
        self.nobar = getattr(self, "nobar", set()) | set(v.chan for v in kept.values())
        self.lastw, self.readers = dict(kept), {}

    def finish(self):
        pass

    @staticmethod
    def _schedule(ops):
        n = len(ops)
        for i, o in enumerate(ops):
            o.idx, o.succ, o.start = i, [], None
        inreg = set(id(o) for o in ops)
        for o in ops:
            o.xedges = [d for (d, wr) in o.edges if id(d) not in inreg]
            o.edges = [(d, wr) for (d, wr) in o.edges if id(d) in inreg]
            o.nun = len(o.edges)
            for d, _ in o.edges:
                d.succ.append(o)
        for o in reversed(ops):
            o.prio = o.lat + max([s_.prio for s_ in o.succ], default=0.0)
        for o in ops:
            if o.chan is not None and o.aset != -1:
                o.prio = 1e9 - o.idx
        free = {e: 0.0 for e in Prog.ENG}
        rel = {e: [] for e in Prog.ENG}
        avail = {}
        for o in ops:
            if o.nun == 0:
                rel[o.eng].append(o)
                avail[id(o)] = 0.0
        order = {e: [] for e in Prog.ENG}
        done = 0
        cur_set = 0
        TL = 1.3
        QUANT = Prog.QUANT
        while done < n:
            best = None
            for e in Prog.ENG:
                lst = rel[e]
                if not lst:
                    continue
                fe = free[e]
                cand, ck = None, None
                for o in lst:
                    t = avail[id(o)]
                    pen = TL if (e == "act" and o.aset > 0 and o.aset != cur_set) else 0.0
                    st_ = max(t, fe) + pen
                    k = (int((st_ - o.urg) / QUANT), -o.prio, st_, o.idx)
                    if ck is None or k < ck:
                        cand, ck = o, k
                if best is None or ck < best[1]:
                    best = (cand, ck)
            o, k = best
            e = o.eng
            rel[e].remove(o)
            if e == "act" and o.aset > 0 and o.aset != cur_set:
                cur_set = o.aset
            o.start = k[2]
            free[e] = o.start + o.busy
            fin = o.start + o.lat
            fin_same = o.start + o.busy
            order[e].append(o)
            done += 1
            for s_ in o.succ:
                a = avail.get(id(s_), 0.0)
                f_ = fin_same if (s_.eng == e and o.chan is None and s_.chan is None) else fin
                if f_ > a:
                    avail[id(s_)] = f_
                else:
                    avail[id(s_)] = a
                s_.nun -= 1
                if s_.nun == 0:
                    rel[s_.eng].append(s_)
        return order, max(free.values())

    def emit(self):
        nc = self.nc
        streams = {e: [] for e in self.ENG}
        self.sim_us = []
        prev_last = None
        for reg in self.regions:
            order, span = self._schedule(reg)
            self.sim_us.append(span)
            if prev_last is not None:
                for e in self.ENG:
                    b = Op()
                    b.eng, b.fn, b.chan, b.need, b.sig = e, None, None, False, None
                    b.xedges = None
                    b.urg = 0.0
                    b.edges = [(d, False) for d in prev_last if not (d.chan is None and d.eng == e)]
                    streams[e].append(b)
            for e in self.ENG:
                streams[e].extend(order[e])
            prev_last = [order[e][-1] for e in ("pe", "act", "dve", "pool") if order[e]]
            chl = {}
            for e in self.ENG:
                for o in order[e]:
                    if o.chan is not None:
                        chl[o.chan] = o
            prev_last += [v for c_, v in chl.items() if c_ not in getattr(self, "nobar", set())]
        fin = Op()
        fin.eng, fin.fn, fin.chan, fin.need, fin.sig = "sp", None, None, False, None
        chl = {}
        for e in self.ENG:
            for o in streams[e]:
                if o.chan is not None:
                    chl[o.chan] = o
        fin.edges = [(d, False) for d in chl.values()]
        fin.xedges = None
        fin.urg = 0.0
        streams["sp"].append(fin)
        for e in self.ENG:
            for i, o in enumerate(streams[e]):
                o.idx = i
        for e in self.ENG:
            for o in streams[e]:
                deps, latest = [], {}
                for d, is_war in o.edges:
                    if d.chan is None and o.chan is None and d.eng == o.eng:
                        if o.eng in ("pe", "sp"):
                            continue
                    if d.chan is None:
                        if d.eng not in latest or latest[d.eng].idx < d.idx:
                            latest[d.eng] = d
                    else:
                        deps.append(d)
                deps.extend(getattr(o, "xedges", None) or [])
                deps.extend(latest.values())
                for d in deps:
                    d.need = True
                o.deps = deps
        with ExitStack() as es:
            sems = {}
            for e in ("pe", "act", "dve", "pool"):
                sems[e] = es.enter_context(nc.semaphore("s_" + e))
            for c in self.chan_cnt_keys():
                sems[("ch", c)] = es.enter_context(nc.semaphore("c_" + str(c)))
            cnt = {e: 0 for e in self.ENG}
            chc = {}
            for e in self.ENG:
                for o in streams[e]:
                    if o.chan is not None:
                        chc[o.chan] = chc.get(o.chan, 0) + 16
                        o.sig = (("ch", o.chan), chc[o.chan], 16)
                    elif o.need:
                        assert o.fn is not None
                        cnt[e] += 1
                        o.sig = (e, cnt[e], 1)
            self.counts = dict(cnt)
            block = es.enter_context(nc.Block())

            def mk(e):
                def body(eng):
                    waited = {}
                    for o in streams[e]:
                        for d in o.deps:
                            k, v, _ = d.sig
                            if waited.get(k, 0) >= v:
                                continue
                            eng.wait_ge(sems[k], v)
                            waited[k] = v
                        if o.fn is None:
                            continue
                        ins = o.fn(eng)
                        if o.sig is not None:
                            ins.then_inc(sems[o.sig[0]], o.sig[2])
                return body

            block.tensor(mk("pe"))
            block.scalar(mk("act"))
            block.vector(mk("dve"))
            block.gpsimd(mk("pool"))
            block.sync(mk("sp"))
        self.ops = streams

    def chan_cnt_keys(self):
        return list(self.chan_last.keys())


def _bc(ap, axis, n):
    a = ap.unsqueeze(axis)
    shp = list(a.shape)
    shp[axis] = n
    return a.to_broadcast(shp)


def build():
    nc = bass.Bass("TRN2", target_bir_lowering=False)
    P = Prog(nc)
    es = ExitStack()

    def din(name, shape):
        return nc.dram_tensor(name, list(shape), F32, kind="ExternalInput").ap()

    def dout(name, shape):
        return nc.dram_tensor(name, list(shape), F32, kind="ExternalOutput").ap()

    xp = din("xp", (SEQ, D))
    xs = din("xs", (TS, D))
    st0 = din("st0", (SB, 8, 128, 64))
    cch = din("cch", (32, DUP))
    lbp = din("lbp", (2, D))
    gpre = din("gpre", (D,))
    w_in = din("w_in", (D, DIN))
    ghn = din("ghn", (64,))
    lng = din("lng", (512,))
    lnb = din("lnb", (512,))
    w_s = din("w_s", (4, 128, 128))
    b_s = din("b_s", (4, 128))
    w_pa = din("w_pa", (512, D))
    w_pb = din("w_pb", (512, D))
    w_o = din("w_o", (D, D))
    gpost = din("gpost", (D,))
    gffn = din("gffn", (D,))
    w_up = din("w_up", (D, DUP))
    conv_w = din("conv_w", (3, DUP))
    conv_b = din("conv_b", (DUP,))
    w_dn = din("w_dn", (DFF, D))
    gpost2 = din("gpost2", (D,))
    ident_d = din("ident", (128, 128))
    maskP_d = din("maskP", (128, 128))
    maskS_d = din("maskS", (64, 64))
    selS_d = din("selS", (64, 16))
    selE_d = din("selE", (4, 64))

    yp = dout("yp", (SEQ, D))
    ys = dout("ys", (TS, D))
    spo = dout("spo", (8, 128, 64))
    sso = dout("sso", (SB, 8, 128, 64))
    cpo = dout("cpo", (2, DUP))
    cso = dout("cso", (32, DUP))
    vso = dout("vso", (TS, 512))
    x1s = nc.dram_tensor("x1s", [SEQ + TS, D], F32, kind="Internal").ap()
    wupb = nc.dram_tensor("wupb", [D, DUP], BF16, kind="Internal").ap()
    wdnb = nc.dram_tensor("wdnb", [DFF, D], BF16, kind="Internal").ap()
    sK_d = nc.dram_tensor("sK_d", [TS, D], BF16, kind="Internal").ap()
    sI_d = nc.dram_tensor("sI_d", [TS, 512], BF16, kind="Internal").ap()
    sG_d = nc.dram_tensor("sG_d", [128, 128], F32, kind="Internal").ap()

    def sb(name, shape, dt=F32):
        return es.enter_context(nc.sbuf_tensor("sb_" + name, list(shape), dt))

    Wh = sb("Wbig", (128, WELEMS), BF16)

    def Wv(off, kstride, kc, c0, n):
        return Wh[:, off + kc * kstride + c0: off + kc * kstride + c0 + n]

    OFF_PA, OFF_PB, OFF_O = 49152, 53248, 57344
    OFF_DN = 8 * DUP

    def Win(kc, c0, n): return Wv(0, DIN, kc, c0, n)
    def Wpa(kc, c0, n): return Wv(OFF_PA, D, kc, c0, n)
    def Wpb(kc, c0, n): return Wv(OFF_PB, D, kc, c0, n)
    def Wo(kc, c0, n): return Wv(OFF_O, D, kc, c0, n)
    def Wup(kc, c0, n): return Wv(0, DUP, kc, c0, n)
    def Wdn(kc, c0, n): return Wv(OFF_DN, D, kc, c0, n)

    identb = sb("identb", (128, 128), BF16)
    identf = sb("identf", (128, 128))
    maskP = sb("maskP", (128, 128))
    maskS = sb("maskS", (64, 64))
    selS = sb("selS", (64, 16))
    wTp = sb("wTp", (128, 4, 128), BF16)
    wTs = sb("wTs", (64, 4, 64), BF16)
    bsP = sb("bsP", (1, 4, 128))
    bsS = sb("bsS", (1, 4, 64))
    onesf = sb("onesf", (1, 128))

    ones2 = sb("ones2", (128, 128))
    epsc = sb("epsc", (128, 1))
    prmT = sb("prmT", (128, 120))
    prmTB = sb("prmTB", (128, 88))
    prm2 = sb("prm2", (128, 3, 8))
    selE = sb("selE", (4, 64))
    bs4 = sb("bs4", (1, 4, 4))
    w4 = sb("w4", (4, 4, 4))
    m1 = sb("m1", (4, 64))
    P0T, P1T, GPRET, GFFNT, LBT, OMLT, NOMLT = range(7)
    ghn_bc = sb("ghn_bc", (128, 64))
    lng_bc = sb("lng_bc", (128, 512))
    lnb_bc = sb("lnb_bc", (128, 512))
    GP_bc = sb("GP_bc", (128, D))
    small = sb("small", (128, 96))

    def prmv(i):
        if i < 4:
            return prmT[:, 8 * i:8 * i + 8]
        return prm2[:, i - 4, :]

    def convb_v(bi):
        return prmT[:, 32 + bi:33 + bi]

    def convw_v(tap, bi):
        if tap == 2:
            return prmT[:, 76 + bi:77 + bi]
        return prmTB[:, tap * 44 + bi:tap * 44 + bi + 1]

    layA = [("xt0", 4096), ("xsb", 2048), ("xnT0", 2048), ("xnT1", 2048), ("SG", 4096), ("F", 4096), ("B", 4096),
            ("GA", 4096), ("GB", 4096), ("QT", 2048), ("KT", 2048), ("Ktok", 2048), ("ibf", 1024),
            ("sog", 2048), ("gv", 2048), ("vnb", 1024), ("guT", 2048), ("scT", 2048), ("R1", 2048),
            ("osb", 2048), ("oab", 1024), ("oaT", 1024), ("obT", 1024), ("hTb", 2048), ("S", 2048), ("Spb", 1024)]
    layB = [("x1t0", 4096), ("x1t1", 4096), ("x1t2", 4096), ("xsb", 2048), ("xnT0", 4096), ("xnT1", 4096),
            ("upx0", 2112), ("upx1", 2112), ("cg0", 1024), ("cg1", 1024), ("cv0", 1024), ("cv1", 1024),
            ("hT", 11264), ("ybuf", 4096), ("tmu0", 2048), ("tmu1", 2048), ("carryP", 384), ("carryS", 5632),
            ("sibf", 1024)]
    szA = sum(s for _, s in layA)
    szB = sum(s for _, s in layB)
    ARENA = max(szA, szB)
    arena = sb("arena", (128, ARENA // 4))

    def mkviews(lay):
        d, off = {}, 0
        for n, s in lay:
            d[n] = (off // 4, s // 4)
            off += s
        return d
    vA, vB = mkviews(layA), mkviews(layB)

    def V(views, name, dt=F32, shape3=None, parts=128):
        o, n = views[name]
        a = arena[0:parts, o:o + n]
        if dt == BF16:
            a = a.bitcast(BF16)
        if shape3 is not None:
            a = a.rearrange("p (a b) -> p a b", b=shape3)
        return a

    def ps(name, n):
        return es.enter_context(nc.psum_tensor("ps_" + name, [128, n], F32))
    FA, FB, T2 = ps("FA", 1024), ps("FB", 1024), ps("T2", 1024)
    XA, XB = ps("XA", 512), ps("XB", 512)
    FA3 = FA[:, :].rearrange("p (a b) -> p a b", b=128)
    FB3 = FB[:, :].rearrange("p (a b) -> p a b", b=128)
    XB3 = XB[:, :].rearrange("p (a b) -> p a b", b=128)
    XAb = XA[:, :].bitcast(BF16)
    XAb3 = XAb.rearrange("p (a b) -> p a b", b=128)
    TA, TB = T2[:, 0:512], T2[:, 512:1024]

    def sm(c, n=1, parts=128):
        return small[0:parts, c:c + n]

    def fsz(ap):
        n = 1
        for d in ap.shape[1:]:
            n *= int(d)
        return n

    def dma(eng, out, in_, r, w, chan, after=(), bpe=4):
        nbytes = fsz(out) * int(out.shape[0]) * bpe
        busy = 0.15 if eng == "sp" else max(1.0, nbytes / 300e3)
        return P.op(eng, lambda e: e.dma_start(out=out, in_=in_), r, w, chan, busy=busy,
                    lat=2.2 + nbytes / 150e3, after=after)

    def actf(out, in_, func, r, w, bias=None, scale=None, accum=None):
        kw = {}
        if bias is not None:
            kw["bias"] = bias
        if scale is not None:
            kw["scale"] = scale
        if accum is not None:
            kw["accum_out"] = accum
        aset = {AF.Exp: 6, AF.Ln: 6, AF.Sigmoid: 2, AF.Silu: 18, AF.Gelu_apprx_tanh: 11}.get(func, 0)
        return P.op("act", lambda e: e.activation(out=out, in_=in_, func=func, **kw), r, w,
                    busy=0.24 + fsz(out) * 0.00078 + (0.1 if accum is not None else 0.0), aset=aset)

    def tt(eng, out, in0, in1, op, r, w):
        b = 0.12 + fsz(out) * 0.0011 if eng == "dve" else 0.2 + fsz(out) * 0.0026
        return P.op(eng, lambda e: e.tensor_tensor(out=out, in0=in0, in1=in1, op=op), r, w, busy=b)

    def ts(eng, out, in0, s1, s2, op0, op1, r, w):
        b = 0.12 + fsz(out) * 0.0008 if eng == "dve" else 0.2 + fsz(out) * 0.0026
        if s2 is None:
            return P.op(eng, lambda e: e.tensor_scalar(out=out, in0=in0, scalar1=s1, scalar2=None, op0=op0), r, w,
                        busy=b)
        return P.op(eng, lambda e: e.tensor_scalar(out=out, in0=in0, scalar1=s1, scalar2=s2, op0=op0, op1=op1), r, w,
                    busy=b)

    def stt(out, in0, scalar, in1, op0, op1, r, w):
        return P.op("dve", lambda e: e.scalar_tensor_tensor(out=out, in0=in0, scalar=scalar, in1=in1,
                                                            op0=op0, op1=op1), r, w,
                    busy=0.16 + fsz(out) * 0.0015)

    def dvop(fn, r, w, n, k=0.0011):
        return P.op("dve", fn, r, w, busy=0.12 + n * k)

    def mms(lst, r, w, after=()):
        def fn(e):
            ins = None
            for (o_, l_, r_, st, sp_) in lst:
                ins = e.matmul(o_, l_, r_, start=st, stop=sp_)
            return ins
        b = sum(0.035 + max(fsz(o_), 96) * 0.00045 for (o_, l_, r_, st, sp_) in lst)
        return P.op("pe", fn, r, w, busy=b, lat=b + 0.25, after=after)

    def transposes(lst, r, w, after=()):
        def fn(e):
            ins = None
            for (o_, i_, id_) in lst:
                ins = e.transpose(o_, i_, id_)
            return ins
        b = 0.1 * len(lst)
        return P.op("pe", fn, r, w, busy=b, lat=b + 0.25, after=after)

    wch = [0]

    wspans = []

    def wdma(out, in_, key, span=None, after=(), own_chan=False, r=(), bpe=4):
        c = ("wo%d" % wch[0]) if own_chan else ("w%d" % (wch[0] % 6))
        wch[0] += 1
        if span is not None:
            wspans.append((key, span[0], span[1]))
        o_ = dma("pool", out, in_, list(r), [key], c, after=after, bpe=bpe)
        if own_chan:
            o_.aset = -1
        return o_

    pch = [0]

    plate = [0.0]

    def pdma(out, in_, key):
        c = "p%d" % (pch[0] % 14)
        pch[0] += 1
        return dma("sp", out, in_, [], [key] if not isinstance(key, list) else key, c)

    wdma(identb[:, :], ident_d[:, :], "identb")
    dma("sp", arena[:, vA["xt0"][0]:vA["xt0"][0] + 1024], xp[0:128, :], [], ["xt0"], "ld0")
    pdma(identf[:, :], ident_d[:, :], "identf")
    PAr = arena[:, vA["B"][0]:vA["B"][0] + 128]
    PBr = arena[:, vA["SG"][0]:vA["SG"][0] + 128]
    KB8 = [("B", h_) for h_ in range(8)]
    KS8 = [("SG", h_) for h_ in range(8)]
    for i_, (r0, vec) in enumerate(((0, lbp[0]), (8, lbp[1]), (16, gpre), (24, gffn))):
        pdma(PAr[r0:r0 + 8, :], vec.rearrange("(h k) -> h k", k=128), ("B", i_))
    pdma(PAr[32:76, :], conv_b.rearrange("(b p) -> b p", p=128), ("B", 4))
    pdma(PAr[76:120, :], conv_w[2].rearrange("(b p) -> b p", p=128), ("B", 5))
    pdma(PBr[0:44, :], conv_w[0].rearrange("(b p) -> b p", p=128), ("SG", 0))
    pdma(PBr[44:88, :], conv_w[1].rearrange("(b p) -> b p", p=128), ("SG", 1))

    pdma(maskP[:, :], maskP_d[:, :], "maskP")
    pdma(maskS[:, :], maskS_d[:, :], "maskS")
    pdma(selS[:, :], selS_d[:, :], "selS")
    pdma(selE[:, :], selE_d[:, :], "selE")
    pdma(ghn_bc[:, :], ghn.partition_broadcast(128), "ghn_bc")
    pdma(lng_bc[:, :], lng.partition_broadcast(128), "lng_bc")
    pdma(lnb_bc[:, :], lnb.partition_broadcast(128), "lnb_bc")
    pdma(GP_bc[:, :], gpost.partition_broadcast(128), "GP_bc")
    pdma(bsP[0:1, :, :], b_s.unsqueeze(0), "bsP")
    pdma(bs4[0:1, :, :], b_s[:, 0:4].unsqueeze(0), "bs4")
    pdma(w4[:, :, :], w_s[:, 0:4, 0:4].rearrange("g b a -> b g a"), "w4")
    transposes([(XB[:, 0:120], PAr[0:120, :], identf[0:120, 0:120])], KB8 + ["identf"], ["XB"])
    actf(prmT[:, :], XB[:, 0:120], AF.Copy, ["XB"], ["prmT"])
    transposes([(XB[:, 0:88], PBr[0:88, :], identf[0:88, 0:88])], KS8 + ["identf"], ["XB"])
    actf(prmTB[:, :], XB[:, 0:88], AF.Copy, ["XB"], ["prmTB"])
    P.op("dve", lambda e: e.memset(onesf[:, :], 1.0), [], ["onesf"])
    P.op("dve", lambda e: e.memset(ones2[:, :], 1.0), [], ["ones2"])
    P.op("dve", lambda e: e.memset(epsc[:, :], EPS), [], ["epsc"])
    tt("dve", prmv(LBT), prmv(P0T), prmv(P1T), ALU.subtract, ["prmT"], ["lbT"])
    dvop(lambda e: e.tensor_copy(out=bsS[0:1, :, :].rearrange("p g (j s) -> p (g j) s", s=16),
                                 in_=_bc(bs4[0:1, :, :].rearrange("p g j -> p (g j)"), 2, 16)), ["bs4"], ["bsS"], 256)
    actf(prmv(LBT), prmv(LBT), AF.Sigmoid, ["lbT"], ["lbT"])
    ts("dve", prmv(OMLT), prmv(LBT), -1.0, 1.0, ALU.mult, ALU.add, ["lbT"], ["omlT"])
    ts("dve", prmv(NOMLT), prmv(OMLT), -1.0, None, ALU.mult, None, ["omlT"], ["nomlT"])

    wTs_keys = [("wTs", g) for g in range(4)]

    def load_w(dst_fn, src, K, ncols, cw, name, order=None, off=0, kstride=0, overlay=False, own_chan=False,
               late=None, rkey=None, bpe=4):
        ncg = ncols // cw
        todo = [(cg, kc) for cg in (order if order is not None else range(ncg)) for kc in range(K)]
        if overlay and late is not None:
            def is_late(cg, kc):
                s0 = off + kc * kstride + cg * cw
                return any(a_ < s0 + cw and s0 < b_ and late(k_) for (k_, a_, b_) in wspans)
            todo = [x for x in todo if not is_late(*x)] + [x for x in todo if is_late(*x)]
        for cg, kc in todo:
            if True:
                s0 = off + kc * kstride + cg * cw
                aft = [k_ for (k_, a_, b_) in wspans if a_ < s0 + cw and s0 < b_] if overlay else ()
                wdma(dst_fn(kc, cg * cw, cw), src[kc * 128:(kc + 1) * 128, cg * cw:(cg + 1) * cw], (name, cg, kc),
                     span=None if overlay else (s0, s0 + cw), after=aft, own_chan=own_chan,
                     r=[rkey(kc)] if rkey is not None else (), bpe=bpe)

    load_w(Win, w_in, 8, DIN, 1024, "w_in", order=[1, 0, 2, 3, 4, 5], off=0, kstride=DIN)
    load_w(Wpa, w_pa, 4, D, 1024, "w_pa", off=OFF_PA, kstride=D)
    load_w(Wpb, w_pb, 4, D, 1024, "w_pb", off=OFF_PB, kstride=D)
    load_w(Wo, w_o, 8, D, 1024, "w_o", off=OFF_O, kstride=D)
    wspans.append(("xt1", 65536, 67584))
    for kc in range(8):
        wdma(wupb[kc * 128:(kc + 1) * 128, :], w_up[kc * 128:(kc + 1) * 128, :], ("wupb", kc))
    for i_ in range(11):
        wdma(wdnb[i_ * 256:(i_ + 1) * 256, :], w_dn[i_ * 256:(i_ + 1) * 256, :], ("wdnb", i_))

    def wk(name, K, c0, n, cw=1024):
        return [(name, cg, kc) for cg in range(c0 // cw, (c0 + n - 1) // cw + 1) for kc in range(K)]

    xt = [V(vA, "xt0"), Wh[:, 65536:67584].bitcast(F32)]
    xsb = V(vA, "xsb", BF16)
    xnT = [V(vA, "xnT0", BF16, 128), V(vA, "xnT1", BF16, 128)]
    SG3, F3, B3, GA3, GB3 = (V(vA, n, F32, 128) for n in ("SG", "F", "B", "GA", "GB"))
    GAt, GBt = V(vA, "GA"), V(vA, "GB")
    QT3, KT3 = V(vA, "QT", BF16, 128), V(vA, "KT", BF16, 128)
    Ktok = V(vA, "Ktok", BF16)
    ibf = V(vA, "ibf", BF16)
    sog = V(vA, "sog")
    gv = V(vA, "gv")
    vnb = V(vA, "vnb", BF16)
    guT3 = V(vA, "guT", F32, 128)
    scT3 = V(vA, "scT", BF16, 128)
    osb = V(vA, "osb")
    oab = V(vA, "oab", BF16)
    oaT3 = V(vA, "oaT", BF16, 128)
    obT3 = V(vA, "obT", BF16, 128)
    hTb3 = V(vA, "hTb", BF16, 128)
    R1 = V(vA, "R1")
    S_ = V(vA, "S")
    S3 = V(vA, "S", F32, 64)
    Spb = V(vA, "Spb", BF16)
    junkA = arena[:, vA["oab"][0]:vA["oab"][0] + 512].bitcast(BF16)
    XAb = XA[:, :].bitcast(BF16)
    XAb3 = XAb.rearrange("p (a b) -> p a b", b=128)
    XBb = XB[:, :].bitcast(BF16)
    TB3 = TB.rearrange("p (a b) -> p a b", b=128)

    def K8(n):
        return [(n, h) for h in range(8)]

    P.op("dve", lambda e: e.memset(S_, 0.0), [], ["S"])
    P.op("dve", lambda e: e.memset(scT3[64:128, :, 0:64], 0.0), [], ["scT"])

    def rms_rstd(ss_ap, ln_ap, out_ap, dim, rkeys, wkey):
        actf(ln_ap, ss_ap, AF.Ln, rkeys + ["epsc"], [wkey + "_ln"], bias=epsc[0:ss_ap.shape[0], :], scale=1.0 / dim)
        actf(out_ap, ln_ap, AF.Exp, [wkey + "_ln"], [wkey], scale=-0.5)

    def xload(ti, kind, which):
        T = 128 if kind == "p" else TS
        src = xp[ti * 128:(ti + 1) * 128, :] if kind == "p" else xs[:, :]
        dma("sp", xt[which][0:T, :], src, [], ["xt%d" % which], "ld%d" % which)

    def make_tile(ti, kind):
        T = 128 if kind == "p" else TS
        par = ti % 2 if kind == "p" else 0
        xtb, xtk = xt[0], "xt0"
        xrb, xrk = xt[1], "xt1"
        xn, xnk = xnT[par], "xnT%d" % par
        row0 = ti * 128 if kind == "p" else SEQ
        C = {}

        def fm(ps3, c0, nb, pkey):
            for b in range(nb):
                mms([(ps3[:, b, 0:T], Win(kc, c0 + b * 128, 128), xn[:, kc, 0:T], kc == 0, kc == 7)
                     for kc in range(8)], [xnk] + wk("w_in", 8, c0 + b * 128, 128), [pkey])

        def tm(psv, c0, pkey):
            mms([(psv[0:T, :], xn[:, kc, 0:T], Win(kc, c0, 512), kc == 0, kc == 7) for kc in range(8)],
                [xnk] + wk("w_in", 8, c0, 512), [pkey])

        def c_F1():
            actf(xsb[0:T, :], xtb[0:T, :], AF.Square, [xtk], ["xsb", "ss"], accum=sm(0, 1, T))
            rms_rstd(sm(0, 1, T), sm(1, 1, T), sm(2, 1, T), D, ["ss"], "rstd")
            actf(xsb[0:T, :], xtb[0:T, :], AF.Identity, [xtk, "rstd"], ["xsb"], scale=sm(2, 1, T))
            transposes([(XAb3[:, kc, 0:T], xsb[0:T, kc * 128:(kc + 1) * 128], identb[0:T, 0:T]) for kc in range(8)],
                       ["xsb", "identb"], ["XA"])
            tt("dve", xn[:, :, 0:T], XAb3[:, :, 0:T], _bc(prmv(GPRET), 2, T), ALU.mult, ["XA", "prmT"], [xnk])

        def c_F2():
            fm(FA3, 1024, 8, "FA")
            fm(FB3, 0, 8, "FB")
            actf(SG3[:, :, 0:T], FA3[:, :, 0:T], AF.Sigmoid, ["FA"], K8("SG"))
            for h in range(8):
                actf(F3[:, h, 0:T], SG3[:, h, 0:T], AF.Ln, [("SG", h), "omlT", "lbT"], [("F", h)],
                     scale=prmv(OMLT)[:, h:h + 1], bias=prmv(LBT)[:, h:h + 1])
            for h in range(8):
                actf(SG3[:, h, 0:T], SG3[:, h, 0:T], AF.Identity, [("SG", h), "omlT", "nomlT"], [("SG", h)],
                     scale=prmv(NOMLT)[:, h:h + 1], bias=prmv(OMLT)[:, h:h + 1])

        def c_F4():
            if kind == "p":
                for h in range(8):
                    dvop(lambda e, h=h: e.tensor_tensor_scan(out=B3[:, h, 0:T], data0=ones2[:, 0:T],
                                                                    data1=F3[:, h, 0:T], initial=0.0,
                                                                    op0=ALU.mult, op1=ALU.add),
                         [("F", h), "ones2"], [("B", h)], T, 0.0022)
                dvop(lambda e: e.tensor_copy(out=small[:, 8:16], in_=B3[:, :, 63]), K8("B"), ["b63"], 8)
                ts("dve", small[:, 16:24], B3[:, :, 63], -1.0, None, ALU.mult, None, K8("B"), ["nb63"])
                actf(small[:, 24:32], small[:, 8:16], AF.Exp, ["b63"], ["eb63"])
                for h in range(8):
                    actf(F3[:, h, 0:T], B3[:, h, 0:T], AF.Exp, [("B", h), "nb63"], [("F", h)],
                         bias=small[:, 16 + h:17 + h], scale=1.0)
                    actf(B3[:, h, 0:T], B3[:, h, 0:T], AF.Exp, [("B", h), "b63"], [("B", h)],
                         bias=small[:, 8 + h:9 + h], scale=-1.0)
            else:
                dvop(lambda e: e.tensor_copy(out=B3[:, :, 0:16], in_=F3[:, :, 0:16]), K8("F"), K8("B"), 128)
                for j in range(1, 4):
                    tt("dve", B3[:, :, 16 * j:16 * j + 16], B3[:, :, 16 * j - 16:16 * j],
                       F3[:, :, 16 * j:16 * j + 16], ALU.add, K8("B") + K8("F"), K8("B"))
                actf(F3[:, :, 0:T], B3[:, :, 0:T], AF.Exp, K8("B"), K8("F"))
                actf(B3[:, :, 0:T], B3[:, :, 0:T], AF.Exp, K8("B"), K8("B"), scale=-1.0)
            tt("dve", KT3[:, :, 0:T], SG3[:, :, 0:T], B3[:, :, 0:T], ALU.mult, K8("SG") + K8("B"), ["KT"])

        def c_F4b():
            actf(B3[:, :, 0:T], FB3[:, :, 0:T], AF.Silu, ["FB"], K8("B"))
            tt("dve", QT3[:, :, 0:T], B3[:, :, 0:T], F3[:, :, 0:T], ALU.mult, K8("B") + K8("F"), ["QT"])
            if kind == "p":
                dvop(lambda e: e.tensor_copy(out=small[:, 32:40], in_=F3[:, :, T - 1]), K8("F"), ["glast"], 8)
            else:
                glS = V(vA, "S", F32, 16)
                dvop(lambda e: e.tensor_copy(out=glS[:, 0:8, :], in_=F3[:, :, 48:64]), K8("F"), ["S"], 128)

        def c_F3():
            tm(XB, 2048, "XB")
            actf(ibf[0:T, :], XB[0:T, :], AF.Copy, ["XB"], ["ibf"])
            tm(TA, 2560, "TA")
            actf(sog[0:T, :], TA[0:T, :], AF.Silu, ["TA"], ["sog"])
            s3 = sog[0:T, :].rearrange("p (h d) -> p h d", d=64)
            tt("dve", s3, s3, _bc(ghn_bc[0:T, :], 1, 8), ALU.mult, ["sog", "ghn_bc"], ["sog"])

        def c_F5():
            tm(XA, 3584, "XA")
            actf(gv[0:T, :], XA[0:T, :], AF.Gelu_apprx_tanh, ["XA"], ["gv"])
            dvop(lambda e: e.bn_stats(out=small[0:T, 40:46], in_=gv[0:T, :]), ["gv"], ["bst"], 512)
            dvop(lambda e: e.bn_aggr(out=small[0:T, 46:48], in_=small[0:T, 40:46]), ["bst"], ["mv"], 8)
            rms_rstd(small[0:T, 47:48], sm(48, 1, T), sm(49, 1, T), 1.0, ["mv"], "rsv")
            stt(sm(63, 1, T), small[0:T, 46:47], -1.0, sm(49, 1, T), ALU.mult, ALU.mult, ["mv", "rsv"], ["nmr"])
            actf(gv[0:T, :], gv[0:T, :], AF.Identity, ["gv", "rsv", "nmr"], ["gv"], scale=sm(49, 1, T),
                 bias=sm(63, 1, T))
            tt("dve", gv[0:T, :], gv[0:T, :], lng_bc[0:T, :], ALU.mult, ["gv", "lng_bc"], ["gv"])
            if kind == "p":
                tt("dve", vnb[0:T, :], gv[0:T, :], lnb_bc[0:T, :], ALU.add, ["gv", "lnb_bc"], ["vnb"])
            else:
                tt("dve", gv[0:T, :], gv[0:T, :], lnb_bc[0:T, :], ALU.add, ["gv", "lnb_bc"], ["gv"])
                dma("sp", vso[:, :], gv[0:T, :], ["gv"], ["vso"], "vs")
                actf(vnb[0:T, :], gv[0:T, :], AF.Copy, ["gv"], ["vnb"])
            fm(XB3, 3072, 4, "XB")
            actf(guT3[:, 0:4, 0:T], XB3[:, 0:4, 0:T], AF.Gelu_apprx_tanh, ["XB"], ["guT"])

        def gates():
            for cb in range(2):
                tm(FA[:, cb * 512:(cb + 1) * 512], 4096 + cb * 512, "FA")
            actf(GAt[0:T, :], FA[0:T, :], AF.Sigmoid, ["FA"], ["GA"])
            for cb in range(2):
                tm(FB[:, cb * 512:(cb + 1) * 512], 5120 + cb * 512, "FB")
            actf(GBt[0:T, :], FB[0:T, :], AF.Sigmoid, ["FB"], ["GB"])

        def c_B1():
            if kind == "s":
                gates()
            transposes([(XBb[0:T, h * 128:(h + 1) * 128], KT3[:, h, 0:T], identb[:, :]) for h in range(8)],
                       ["KT", "identb"], ["XB"])
            actf(Ktok[0:T, :], XBb[0:T, :], AF.Copy, ["XB"], ["Ktok"])

        def c_B2():
            if kind == "p":
                mms([x for h in range(8) for x in
                     ((FA3[0:128, h, 64:128], KT3[:, h, 0:128], QT3[:, h, 64:128], True, True),
                      (FA3[0:64, h, 0:64], KT3[:, h, 0:64], QT3[:, h, 0:64], True, True))],
                    ["KT", "QT"], ["FA"])
                P.urg = URG
                for hb in range(2):
                    hs = slice(4 * hb, 4 * hb + 4)
                    tt("dve", scT3[:, hs, 64:128], FA3[:, hs, 64:128], _bc(maskP[:, 64:128], 1, 4), ALU.mult,
                       ["FA", "maskP"], ["scT"])
                    tt("dve", scT3[0:64, hs, 0:64], FA3[0:64, hs, 0:64], _bc(maskP[0:64, 0:64], 1, 4), ALU.mult,
                       ["FA", "maskP"], ["scT"])
                tt("dve", S3, S3, _bc(small[:, 24:32], 2, 64), ALU.mult, ["S", "eb63"], ["S"])
                actf(Spb, S_, AF.Copy, ["S"], ["Spb"])
                P.urg = 0.0
            else:
                mms([(FA3[0:T, h, 0:T], KT3[:, h, 0:T], QT3[:, h, 0:T], True, True) for h in range(8)],
                    ["KT", "QT"], ["FA"])
                for hb in range(2):
                    hs = slice(4 * hb, 4 * hb + 4)
                    tt("dve", scT3[0:T, hs, 0:T], FA3[0:T, hs, 0:T], _bc(maskS[:, :], 1, 4), ALU.mult,
                       ["FA", "maskS"], ["scT"])

        def c_B3():
            if kind == "p":
                mms([x for h in range(8) for x in
                     ((TA[0:T, h * 64:(h + 1) * 64], scT3[0:T, h, 0:T], ibf[0:T, h * 64:(h + 1) * 64], True, False),
                      (TA[0:T, h * 64:(h + 1) * 64], QT3[:, h, 0:T], Spb[:, h * 64:(h + 1) * 64], False, True))],
                    ["scT", "ibf", "QT", "Spb"], ["TA"])
                actf(osb[0:T, :], TA[0:T, :], AF.Copy, ["TA"], ["osb"])
                mms([(TB[:, h * 64:(h + 1) * 64], Ktok[0:T, h * 128:(h + 1) * 128], ibf[0:T, h * 64:(h + 1) * 64],
                      True, True) for h in range(8)], ["Ktok", "ibf"], ["TB"])
                tt("dve", S_, TB[:, :], S_, ALU.add, ["TB", "S"], ["S"])
                tt("dve", S3, S3, _bc(small[:, 32:40], 2, 64), ALU.mult, ["S", "glast"], ["S"])
                if ti == NPT - 1:
                    dma("sp", spo.rearrange("h k d -> k h d"), S3, ["S"], ["spo"], "spo")
            else:
                mms([(TA[0:T, h * 64:(h + 1) * 64], scT3[0:T, h, 0:T], ibf[0:T, h * 64:(h + 1) * 64], True, True)
                     for h in range(8)], ["scT", "ibf"], ["TA"])
                glS = V(vA, "S", F32, 16)
                def half(name, i_):
                    o_ = vA[name][0] + 512 * i_
                    return arena[:, o_:o_ + 512].bitcast(BF16)
                S0b = [half("SG", 0), half("SG", 1), half("B", 0), half("B", 1), half("xt0", 0), half("xt0", 1),
                       V(vA, "KT", BF16), V(vA, "gv", BF16)]
                S0bk = [[("SG", h_) for h_ in range(4)], [("SG", h_) for h_ in range(4, 8)],
                        [("B", h_) for h_ in range(4)], [("B", h_) for h_ in range(4, 8)], ["xt0"], ["xt0"],
                        ["KT"], ["gv"]]
                NSB = len(S0b)

                def sload(h_):
                    dma("pool", S0b[h_ % NSB].rearrange("p (s d) -> p s d", d=64),
                        st0[:, h_, :, :].rearrange("s k d -> k s d"), [], S0bk[h_ % NSB], "sl%d" % (h_ % NSB))
                seltmp = arena[0:64, vA["scT"][0]:vA["scT"][0] + 1024]
                selk = ["scT", "R1"]
                dma("sp", sK_d[:, :], Ktok[0:T, :], ["Ktok"], ["sK_d"], "sv0")
                dma("sp", sI_d[:, :], ibf[0:T, :], ["ibf"], ["sI_d"], "sv1")
                dma("sp", sG_d[:, :], V(vA, "S")[:, 0:128], ["S"], ["sG_d"], "sv2")
                for h in range(NSB):
                    sload(h)
                for h in range(8):
                    b = h % NSB
                    PB_, pbk_ = (FA, "FA") if h % 2 == 0 else (FB, "FB")
                    mms([(PB_[0:T, 0:512], QT3[:, h, 0:T], S0b[b][:, 0:512], True, True),
                         (PB_[0:T, 512:1024], QT3[:, h, 0:T], S0b[b][:, 512:1024], True, True)],
                        ["QT"] + S0bk[b], [pbk_])
                    if h + NSB < 8:
                        sload(h + NSB)
                    for half in range(2):
                        cs_ = slice(512 * half, 512 * half + 512)
                        tt("dve", seltmp[:, cs_].rearrange("p (s d) -> p s d", d=64),
                           PB_[0:T, cs_].rearrange("p (s d) -> p s d", d=64),
                           _bc(selS[:, 8 * half:8 * half + 8], 2, 64), ALU.mult, [pbk_, "selS"], selk)
                    dvop(lambda e, h=h: e.tensor_reduce(
                        out=osb[0:T, h * 64:(h + 1) * 64], in_=seltmp.rearrange("p (s d) -> p d s", d=64),
                        axis=AX.X, op=ALU.add), selk, [("osbS", h)], 1024)
                tt("dve", osb[0:T, :], TA[0:T, :], osb[0:T, :], ALU.add, ["TA"] + [("osbS", h) for h in range(8)],
                   ["osb"])

        def c_B4():
            osq = R1[0:T, 0:512]
            tt("dve", osq, osb[0:T, :], osb[0:T, :], ALU.mult, ["osb"], ["R1"])
            dvop(lambda e: e.tensor_reduce(out=small[0:T, 50:58], in_=osq.rearrange("p (h d) -> p h d", d=64),
                                           axis=AX.X, op=ALU.add), ["R1"], ["ssq"], 512)
            rms_rstd(small[0:T, 50:58], small[0:T, 64:72], small[0:T, 72:80], 64.0, ["ssq"], "r8")
            o3 = osb[0:T, :].rearrange("p (h d) -> p h d", d=64)
            tt("dve", o3, o3, _bc(small[0:T, 72:80], 2, 64), ALU.mult, ["osb", "r8"], ["osb"])
            tt("dve", oab[0:T, :], osb[0:T, :], sog[0:T, :], ALU.mult, ["osb", "sog"], ["oab"])
            transposes([(XAb3[:, c, 0:T], oab[0:T, c * 128:(c + 1) * 128], identb[0:T, 0:T]) for c in range(4)],
                       ["oab", "identb"], ["XA"])
            actf(oaT3[:, 0:4, 0:T], XAb3[:, 0:4, 0:T], AF.Copy, ["XA"], ["oaT"])

        def c_B5():
            wT = wTp if kind == "p" else wTs
            bsr = bsP if kind == "p" else bsS
            mms([x for g in range(4) for x in
                 ((TB3[:, g, 0:T], vnb[0:T, g * 128:(g + 1) * 128], wT[0:T, g, 0:T], True, False),
                  (TB3[:, g, 0:T], onesf[0:1, :], bsr[0:1, g, 0:T], False, True))],
                ["vnb", "wTp", "onesf", "bsP", "bsS"] + wTs_keys, ["TB"])
            tt("dve", obT3[:, 0:4, 0:T], TB3[:, 0:4, 0:T], guT3[:, 0:4, 0:T], ALU.mult, ["TB", "guT"], ["obT"])
            if kind == "p":
                gates()
            for cb in range(2):
                mms([(FA[0:T, cb * 512:(cb + 1) * 512], oaT3[:, kc, 0:T], Wpa(kc, cb * 512, 512), kc == 0, kc == 3)
                     for kc in range(4)], ["oaT"] + wk("w_pa", 4, 0, 1024), ["FA"])
            for cb in range(2):
                mms([(FB[0:T, cb * 512:(cb + 1) * 512], obT3[:, kc, 0:T], Wpb(kc, cb * 512, 512), kc == 0, kc == 3)
                     for kc in range(4)], ["obT"] + wk("w_pb", 4, 0, 1024), ["FB"])

        def c_B6():
            tt("dve", GAt[0:T, :], FA[0:T, :], GAt[0:T, :], ALU.mult, ["FA", "GA"], ["GA"])
            tt("dve", GBt[0:T, :], FB[0:T, :], GBt[0:T, :], ALU.mult, ["FB", "GB"], ["GB"])
            htok = R1.bitcast(BF16)
            tt("dve", htok[0:T, :], GAt[0:T, :], GBt[0:T, :], ALU.add, ["GA", "GB"], ["R1"])
            transposes([(XAb3[:, c, 0:T], htok[0:T, c * 128:(c + 1) * 128], identb[0:T, 0:T]) for c in range(8)],
                       ["R1", "identb"], ["XA"])
            actf(hTb3[:, :, 0:T], XAb3[:, :, 0:T], AF.Copy, ["XA"], ["hTb"])
            for cb in range(2):
                mms([(FA[0:T, cb * 512:(cb + 1) * 512], hTb3[:, kc, 0:T], Wo(kc, cb * 512, 512), kc == 0, kc == 7)
                     for kc in range(8)], ["hTb"] + wk("w_o", 8, 0, 1024), ["FA"])
            for cb in range(2):
                actf(junkA[0:T, cb * 512:(cb + 1) * 512], FA[0:T, cb * 512:(cb + 1) * 512], AF.Square, ["FA"],
                     ["oab", "oaT", ("ssm", cb)], accum=sm(58 + cb, 1, T))
            tt("dve", sm(60, 1, T), sm(58, 1, T), sm(59, 1, T), ALU.add, [("ssm", 0), ("ssm", 1)], ["ssmt"])
            rms_rstd(sm(60, 1, T), sm(61, 1, T), sm(62, 1, T), D, ["ssmt"], "rm")
            for cb in range(2):
                cs_ = slice(cb * 512, (cb + 1) * 512)
                stt(R1[0:T, :], FA[0:T, cs_], sm(62, 1, T), GP_bc[0:T, cs_], ALU.mult, ALU.mult,
                    ["FA", "rm", "GP_bc"], ["R1"])
                tt("dve", xrb[0:T, cs_], R1[0:T, :], xrb[0:T, cs_], ALU.add, ["R1", xrk], [xrk])
            dma("sp", x1s[row0:row0 + T, :], xrb[0:T, :], [xrk], [("x1s", ti if kind == "p" else NPT)], "x1st")

        C.update(F1=c_F1, F2=c_F2, F4=c_F4, F4b=c_F4b, F3=c_F3, F5=c_F5, B1=c_B1, B2=c_B2, B3=c_B3, B4=c_B4, B5=c_B5, B6=c_B6)
        return C

    def build_gate_weights():
        wraw = V(vA, "GA", F32, 128)
        for g in range(4):
            pdma(wraw[:, g, :], w_s[g], ["GA"])
        transposes([(XB3[:, g, :], wraw[:, g, :], identf[:, :]) for g in range(4)],
                   ["GA"] + ["identf"], ["XB"], after=["QT"])
        tt("dve", wTp[:, :, :], XB3[:, 0:4, :], _bc(maskP[:, :], 1, 4), ALU.mult, ["XB", "maskP"], ["wTp"])
        for g in range(4):
            mms([(XB[0:4, 0:64], w4[:, g, :], selE[:, :], True, True)], ["w4", "selE"], ["XB"], after=["QT"])
            actf(m1[:, :], XB[0:4, 0:64], AF.Copy, ["XB"], ["m1"])
            mms([(XB[0:64, 64:128], selE[:, :], m1[:, :], True, True)], ["m1", "selE"], ["XB"])
            tt("dve", wTs[:, g, :], XB[0:64, 64:128], maskS[:, :], ALU.mult, ["XB", "maskS"], [("wTs", g)])

    URG = 0.0
    ORDER = ["F1", "B1", "F2", "B2", "F4", "B3", "F4b", "B4", "F3", "B5", "F5", "B6"]
    tiles = [make_tile(ti, "p") for ti in range(NPT)]
    tileS = make_tile(0, "s")
    for rnd in range(NPT + 1):
        fr = tiles[rnd] if rnd < NPT else None
        bk = tiles[rnd - 1] if rnd >= 1 else None
        if rnd == 1:
            P.tag = "gatew"
            build_gate_weights()
        if bk is not None:
            P.tag = "xr"
            xload(rnd - 1, "p", 1)
        for c in ORDER:
            P.tag = c
            if c[0] == "F" and fr is not None:
                fr[c]()
                if c == "F1":
                    P.tag = "xl"
                    if rnd + 1 < NPT:
                        xload(rnd + 1, "p", 0)
                    elif rnd + 1 == NPT:
                        xload(0, "s", 0)
            if c[0] == "B" and bk is not None:
                bk[c]()
    P.tag = "xr"
    xload(0, "s", 1)
    for c in ORDER:
        if c[0] == "F":
            P.tag = "s" + c
            tileS[c]()
    for c in ORDER:
        if c[0] == "B":
            P.tag = "s" + c
            tileS[c]()

    P.tag = "preS0"
    xtmp = arena[:, vA["xnT0"][0]:vA["xnT0"][0] + 1024]
    xn2pre = V(vA, "F", BF16, 256)
    for sub in range(2):
        dma("sp", xtmp, x1s[sub * 128:(sub + 1) * 128, :], [("x1s", sub)], ["xnT0", "xnT1"], "pre")
        actf(xsb[:, :], xtmp, AF.Square, ["xnT0", "xnT1"], ["xsb", "ss"], accum=sm(0, 1, 128))
        rms_rstd(sm(0, 1, 128), sm(1, 1, 128), sm(2, 1, 128), D, ["ss"], "rstd")
        ts("dve", xsb[:, :], xtmp, sm(2, 1, 128), None, ALU.mult, None, ["xnT0", "xnT1", "rstd"], ["xsb"])
        transposes([(XAb3[:, kc, :], xsb[:, kc * 128:(kc + 1) * 128], identb[:, :]) for kc in range(8)],
                   ["xsb", "identb"], ["XA"])
        tt("dve", xn2pre[:, :, sub * 128:(sub + 1) * 128], XAb3[:, :, :], _bc(prmv(GFFNT), 2, 128), ALU.mult,
           ["XA", "prmT"], K8("F"))
    P.tag = "wup"
    load_w(Wup, wupb, 8, DUP, 1408, "w_up", order=[0, 2, 1, 3], off=0, kstride=DUP, overlay=True,
           late=lambda k_: k_[0] == "w_in" and k_[1] >= 4, rkey=lambda kc: ("wupb", kc), bpe=2)
    P.barrier(keep=("w_up", "wdnb"))
    load_w(Wdn, wdnb, NJ, D, 1024, "w_dn", off=OFF_DN, kstride=D, overlay=True, own_chan=True, bpe=2,
           rkey=lambda kc: ("wdnb", kc // 2))
    pdma(GP_bc[:, :], gpost2.partition_broadcast(128), "GP_bc")

    TBP = 256
    NSUP = SEQ // TBP
    x1t = [V(vB, "x1t%d" % i) for i in range(3)]
    xrot = [0]

    def getx():
        k = xrot[0] % 3
        xrot[0] += 1
        return x1t[k], "x1t%d" % k, "ldB%d" % k

    xsbB = V(vB, "xsb", BF16)
    xn2 = [V(vB, "xnT0", BF16, TBP), V(vB, "xnT1", BF16, TBP)]
    upx = [V(vB, "upx%d" % i, F32, 264) for i in range(2)]
    cg = [V(vB, "cg0"), V(vB, "cg1")]
    cv = [V(vB, "cv0"), V(vB, "cv1")]
    hT3 = V(vB, "hT", BF16, TBP)
    ybuf = V(vB, "ybuf")
    tmu = [V(vB, "tmu0", parts=32), V(vB, "tmu1", parts=32)]
    carryP = V(vB, "carryP")[:, 0:88].rearrange("p (r g j) -> p g j r", g=2, r=2)
    carryS = V(vB, "carryS").rearrange("p (g j r) -> p g j r", g=2, r=32)
    P.op("dve", lambda e: e.memset(V(vB, "carryP"), 0.0), [], ["carryP"])
    sS0h = V(vB, "carryS")[:, 0:1024].rearrange("p (s d) -> p s d", d=64)
    sgl = V(vB, "carryS")[:, 1024:1152].rearrange("p (h s) -> p h s", s=16)
    simask = V(vB, "tmu0", BF16, parts=64)
    sKtok = V(vB, "tmu1", BF16, parts=64)
    sibf = V(vB, "sibf", BF16, parts=64)
    dma("sp", sKtok[:, :], sK_d[:, :], [], ["tmu1"], "sv0")
    dma("sp", sibf[:, :], sI_d[:, :], [], ["sibf"], "sv1")
    dma("sp", V(vB, "carryS")[:, 1024:1152], sG_d[:, :], [], [("carryS", "g")], "sv2")

    def state_update(h):
        dma("sp", sS0h, st0[:, h, :, :].rearrange("s k d -> k s d"), [], ["carryS"], "st0")
        tt("dve", simask.rearrange("p (s d) -> p s d", d=64), _bc(sibf[:, h * 64:(h + 1) * 64], 1, 16),
           _bc(selS[:, :], 2, 64), ALU.mult, ["sibf", "selS"], ["tmu0"])
        mms([(TA[:, :], sKtok[:, h * 128:(h + 1) * 128], simask[:, 0:512], True, True)], ["tmu1", "tmu0"], [("T2", 0)])
        mms([(TB[:, :], sKtok[:, h * 128:(h + 1) * 128], simask[:, 512:1024], True, True)], ["tmu1", "tmu0"],
            [("T2", 1)])
        S0f = sS0h.rearrange("p s d -> p (s d)")
        tt("dve", S0f[:, 0:512], TA[:, :], S0f[:, 0:512], ALU.add, [("T2", 0), "carryS"], ["carryS"])
        tt("dve", S0f[:, 512:1024], TB[:, :], S0f[:, 512:1024], ALU.add, [("T2", 1), "carryS"], ["carryS"])
        tt("dve", sS0h, sS0h, _bc(sgl[:, h, :], 2, 64), ALU.mult, ["carryS", ("carryS", "g")], ["carryS"])
        dma("sp", sso[:, h, :, :].rearrange("s k d -> k s d"), sS0h, ["carryS"], [("sso", h)], "so0")

    def cache_prologue():
        cbuf = V(vB, "tmu1", parts=32)
        for c in range(11):
            dma("sp", cbuf[:, 0:512], cch[:, c * 512:(c + 1) * 512], [], ["tmu1"], "cch")
            transposes([(XB[:, b * 32:(b + 1) * 32], cbuf[:, b * 128:(b + 1) * 128], identf[0:32, 0:32])
                        for b in range(4)], ["tmu1", "identf"], ["XB"])
            dst = V(vB, "carryS")[:, c * 128:(c + 1) * 128]
            actf(dst, XB[:, 0:128], AF.Copy, ["XB"], ["carryS"])

    def geo(kind):
        if kind == "p":
            return TBP, 2, 1, 2, 128
        return TS, 32, 16, 1, TS

    def S0(n, kind):
        T, PV, sh, nsub, Tt = geo(kind)
        xn = xn2[n % 2]
        xnk = "xnT%d" % (n % 2)
        for sub in range(nsub):
            row0 = n * TBP + sub * 128 if kind == "p" else SEQ
            xb_, xk, xc = getx()
            dma("sp", xb_[0:Tt, :], x1s[row0:row0 + Tt, :], [], [xk], xc)
            actf(xsbB[0:Tt, :], xb_[0:Tt, :], AF.Square, [xk], ["xsb", "ss"], accum=sm(0, 1, Tt))
            rms_rstd(sm(0, 1, Tt), sm(1, 1, Tt), sm(2, 1, Tt), D, ["ss"], "rstd")
            ts("dve", xsbB[0:Tt, :], xb_[0:Tt, :], sm(2, 1, Tt), None, ALU.mult, None, [xk, "rstd"], ["xsb"])
            transposes([(XAb3[:, kc, 0:Tt], xsbB[0:Tt, kc * 128:(kc + 1) * 128], identb[0:Tt, 0:Tt])
                        for kc in range(8)], ["xsb", "identb"], ["XA"])
            tt("dve", xn[:, :, sub * 128:sub * 128 + Tt], XAb3[:, :, 0:Tt], _bc(prmv(GFFNT), 2, Tt), ALU.mult,
               ["XA", "prmT"], [(xnk, sub)])

    pbanks = [(FA[:, 0:512], ("FA", 0)), (FA[:, 512:1024], ("FA", 1)),
              (FB[:, 0:512], ("FB", 0)), (FB[:, 512:1024], ("FB", 1))]

    def pbank(j):
        bk, bkey = pbanks[j % 4]
        return bk.rearrange("p (g t) -> p g t", g=2), bkey

    def stA(n, kind, j):
        T, PV, sh, nsub, Tt = geo(kind)
        bk3, bkey = pbank(j)
        xn = xn2[n % 2]
        xnk = "xnT%d" % (n % 2)
        for g2 in range(2):
            c0 = g2 * DFF + j * 128
            mms([(bk3[:, g2, 0:T], Wup(kc, c0, 128), xn[:, kc, 0:T], kc == 0, kc == 7) for kc in range(8)],
                [(xnk, 0), (xnk, 1)] + wk("w_up", 8, c0, 128, 1408), [bkey])

    def stB(n, kind, j):
        T, PV, sh, nsub, Tt = geo(kind)
        bk3, bkey = pbank(j)
        u = j % 2
        ux, uk = upx[u], "upx%d" % u
        carry = carryP if kind == "p" else carryS
        ckey = "carryP" if kind == "p" else "carryS"
        actf(ux[:, :, PV:PV + T], bk3[:, :, 0:T], AF.Copy, [bkey], [uk])
        P.op("pool", lambda e: e.tensor_copy(out=ux[:, :, 0:PV], in_=carry[:, :, j, :]), [ckey], [(uk, "c")])
        if kind == "p":
            P.op("pool", lambda e: e.tensor_copy(out=carry[:, :, j, :], in_=ux[:, :, T:T + PV]), [uk], [ckey])
        for g2, acc, ak in ((0, cg[u], "cg%d" % u), (1, cv[u], "cv%d" % u)):
            bi = g2 * NJ + j
            actf(acc[:, 0:T], bk3[:, g2, 0:T], AF.Identity, [bkey, "prmT"], [ak], bias=convb_v(bi), scale=convw_v(2, bi))

    def stC(n, kind, j):
        T, PV, sh, nsub, Tt = geo(kind)
        u = j % 2
        ux, uk = upx[u], "upx%d" % u
        for g2, acc, ak in ((0, cg[u], "cg%d" % u), (1, cv[u], "cv%d" % u)):
            bi = g2 * NJ + j
            a = acc[:, 0:T]
            stt(a, ux[:, g2, 0:T], convw_v(0, bi), a, ALU.mult, ALU.add, [uk, (uk, "c"), "prmTB", ak], [ak])
            stt(a, ux[:, g2, sh:sh + T], convw_v(1, bi), a, ALU.mult, ALU.add, [uk, (uk, "c"), "prmTB", ak], [ak])

    def stD(n, kind, j):
        T, PV, sh, nsub, Tt = geo(kind)
        u = j % 2
        actf(cg[u][:, 0:T], cg[u][:, 0:T], AF.Gelu_apprx_tanh, ["cg%d" % u], ["cg%d" % u])
        tt("pool", hT3[:, j, 0:T], cg[u][:, 0:T], cv[u][:, 0:T], ALU.mult, ["cg%d" % u, "cv%d" % u], [("hT", j)])

    XAf = XA[:, :]

    def ytail(n, kind):
        T, PV, sh, nsub, Tt = geo(kind)
        phs = [[(TA, ("T2", 0)), (TB, ("T2", 1))], [(XAf, "XA"), (XB[:, :], "XB")]]
        for kc in range(NJ):
            for sub in range(nsub):
                for cb in range(2):
                    mms([(phs[sub][cb][0][0:Tt, :], hT3[:, kc, sub * 128:sub * 128 + Tt], Wdn(kc, cb * 512, 512),
                          kc == 0, kc == NJ - 1)], [("hT", kc), ("w_dn", 0, kc)], [phs[sub][cb][1]])
        for sub in range(nsub):
            row0 = n * TBP + sub * 128 if kind == "p" else SEQ
            dst = yp[row0:row0 + Tt, :] if kind == "p" else ys[:, :]
            ph = phs[sub]
            xb_, xk, xc = getx()
            dma("sp", xb_[0:Tt, :], x1s[row0:row0 + Tt, :], [], [xk], xc)
            for cb in range(2):
                actf(xsbB[0:Tt, cb * 512:(cb + 1) * 512], ph[cb][0][0:Tt, :], AF.Square, [ph[cb][1]],
                     ["xsb", ("ssm", cb)], accum=sm(58 + cb, 1, Tt))
            tt("dve", sm(60, 1, Tt), sm(58, 1, Tt), sm(59, 1, Tt), ALU.add, [("ssm", 0), ("ssm", 1)], ["ssmt"])
            rms_rstd(sm(60, 1, Tt), sm(61, 1, Tt), sm(62, 1, Tt), D, ["ssmt"], "rm")
            for cb in range(2):
                cs_ = slice(cb * 512, (cb + 1) * 512)
                stt(ybuf[0:Tt, cs_], ph[cb][0][0:Tt, :], sm(62, 1, Tt), GP_bc[0:Tt, cs_], ALU.mult, ALU.mult,
                    [ph[cb][1], "rm", "GP_bc"], ["ybuf"])
            tt("pool", ybuf[0:Tt, :], ybuf[0:Tt, :], xb_[0:Tt, :], ALU.add, ["ybuf", xk], ["ybuf"])
            dma("sp", dst, ybuf[0:Tt, :], ["ybuf"], [("y", n, sub)], "yst")

    def pair_loop(n, kind, nxt):
        for s_ in range(NJ + 3):
            if 0 <= s_ - 3 < NJ:
                P.tag = "stD"
                stD(n, kind, s_ - 3)
            if 0 <= s_ - 2 < NJ:
                P.tag = "stC"
                stC(n, kind, s_ - 2)
            if 0 <= s_ - 1 < NJ:
                P.tag = "stB"
                stB(n, kind, s_ - 1)
            if s_ < NJ:
                P.tag = "stA"
                stA(n, kind, s_)
            if s_ == 6 and nxt is not None:
                P.tag = "S0"
                S0(*nxt)

    for n in range(NSUP):
        nxt = (n + 1, "p") if n + 1 < NSUP else (NSUP, "s")
        pair_loop(n, "p", nxt)
        P.tag = "ytail"
        ytail(n, "p")
        if n < 4:
            P.tag = "state"
            state_update(2 * n)
            state_update(2 * n + 1)
        if n == 3:
            P.tag = "cache"
            cache_prologue()
    transposes([(XB[0:88, 0:128], V(vB, "carryP")[:, 0:88], identf[:, :])], ["carryP", "identf"], ["XB"])
    cpT = V(vB, "tmu0")
    actf(cpT[0:88, 0:128], XB[0:88, 0:128], AF.Copy, ["XB"], ["tmu0"])
    for r_ in range(2):
        dma("sp", cpo[r_].rearrange("(b p) -> b p", p=128), cpT[r_ * 44:(r_ + 1) * 44, 0:128], ["tmu0"],
            [("cpo", r_)], "cpo")
    pair_loop(NSUP, "s", None)
    xnS = xn2[NSUP % 2]
    xnSk = "xnT%d" % (NSUP % 2)
    for cb in range(11):
        tb = cb % 2
        pbk = TA if tb == 0 else TB
        mms([(pbk[0:32, :], xnS[:, kc, 32:64], Wup(kc, cb * 512, 512), kc == 0, kc == 7) for kc in range(8)],
            [(xnSk, 0)] + wk("w_up", 8, cb * 512, 512, 1408), [("T2", tb)])
        actf(tmu[tb][:, :], pbk[0:32, :], AF.Copy, [("T2", tb)], ["tmu%d" % tb])
        dma("sp", cso[:, cb * 512:(cb + 1) * 512], tmu[tb][:, :], ["tmu%d" % tb], [("cso", cb)], "cso%d" % tb)
    ytail(NSUP, "s")
    P.finish()

    with nc.allow_non_contiguous_dma(reason="small parameter / state layouts"):
        P.emit()
    es.close()
    return nc, P


def _host_consts():
    ident = np.eye(128, dtype=np.float32)
    s = np.arange(128)
    maskP = (s[:, None] <= s[None, :]).astype(np.float32)
    a = np.arange(64)
    maskS = ((a[:, None] % 16 == a[None, :] % 16) & (a[:, None] // 16 <= a[None, :] // 16)).astype(np.float32)
    selS = (a[:, None] % 16 == np.arange(16)[None, :]).astype(np.float32)
    selE = (np.arange(4)[:, None] == a[None, :] // 16).astype(np.float32)
    return ident, maskP, maskS, selS, selE


_CACHE = {}


def kernel(x_prompt, x_sample, state_hgrn, cache_ffn_conv, lb_param, mix_pre_g, w_in, hgrn_norm_g,
           gmlp_ln_g, gmlp_ln_b, w_s, b_s, w_pa, w_pb, w_o, mix_post_g, ffn_pre_g, w_up, conv_w,
           conv_b, w_down, ffn_post_g):
    f = lambda a: np.ascontiguousarray(np.asarray(a, dtype=np.float32))
    if "nc" not in _CACHE:
        _CACHE["nc"] = build()[0]
    nc = _CACHE["nc"]
    ident, maskP, maskS, selS, selE = _host_consts()
    shared = {
        "lbp": f(lb_param), "gpre": f(mix_pre_g)[0], "w_in": f(w_in)[0], "ghn": f(hgrn_norm_g)[0],
        "lng": f(gmlp_ln_g)[0], "lnb": f(gmlp_ln_b)[0], "w_s": f(w_s)[0], "b_s": f(b_s)[0],
        "w_pa": f(w_pa)[0], "w_pb": f(w_pb)[0], "w_o": f(w_o)[0], "gpost": f(mix_post_g)[0],
        "gffn": f(ffn_pre_g)[0], "w_up": f(w_up)[0], "conv_w": f(conv_w)[0], "conv_b": f(conv_b)[0],
        "w_dn": f(w_down)[0], "gpost2": f(ffn_post_g)[0],
        "ident": ident, "maskP": maskP, "maskS": maskS, "selS": selS, "selE": selE,
    }
    x_prompt, x_sample = f(x_prompt), f(x_sample)
    state_hgrn, cache_ffn_conv = f(state_hgrn), f(cache_ffn_conv)
    in_maps = []
    for c in range(NCORES):
        m = dict(shared)
        m["xp"] = x_prompt[c]
        m["xs"] = np.ascontiguousarray(x_sample[c * SB:(c + 1) * SB].transpose(1, 0, 2).reshape(TS, D))
        m["st0"] = state_hgrn[0, c * SB:(c + 1) * SB]
        m["cch"] = np.ascontiguousarray(cache_ffn_conv[0, c * SB:(c + 1) * SB].transpose(1, 0, 2).reshape(32, DUP))
        in_maps.append(m)
    res = run_bass_kernel_spmd(nc, in_maps, core_ids=list(range(NCORES)))
    R = res.results
    yp = np.stack([R[c]["yp"] for c in range(NCORES)], 0)
    ys = np.concatenate([R[c]["ys"].reshape(4, SB, D).transpose(1, 0, 2) for c in range(NCORES)], 0)
    sp = np.stack([R[c]["spo"] for c in range(NCORES)], 0)[None]
    ss = np.concatenate([R[c]["sso"] for c in range(NCORES)], 0)[None]
    cp = np.stack([R[c]["cpo"] for c in range(NCORES)], 0)[None]
    cs = np.concatenate([R[c]["cso"].reshape(2, SB, DUP).transpose(1, 0, 2) for c in range(NCORES)], 0)[None]
    vs = np.concatenate([R[c]["vso"].reshape(4, SB, 512).transpose(1, 0, 2) for c in range(NCORES)], 0)[None]
    return (np.ascontiguousarray(yp, dtype=np.float32), np.ascontiguousarray(ys, dtype=np.float32),
            np.ascontiguousarray(sp, dtype=np.float32), np.ascontiguousarray(ss, dtype=np.float32),
            np.ascontiguousarray(cp, dtype=np.float32), np.ascontiguousarray(cs, dtype=np.float32),
            np.ascontiguousarray(vs, dtype=np.float32))
```
